# Optimizing a Trainium2 kernel written in Bass

```python
import jax, jax.numpy as jnp
from jax import lax
import numpy as np

D_MODEL = 1024
BATCH = 1
SEQ = 16384
DEPTH = 1
DEC_BATCH = 32
DEC_SEQ = 16
PAST_LEN = 4096

CHUNK = 64
POOL_WIDTH = D_MODEL // 2
POOL_WINDOWS = (2, 4, 8, 16)
POOL_GROUPS = len(POOL_WINDOWS)
POOL_GD = POOL_WIDTH // POOL_GROUPS
POOL_HIST = max(POOL_WINDOWS) - 1
GLA_HEADS = 4
GLA_DK = D_MODEL // 2
GLA_DV = D_MODEL
GLA_HK = GLA_DK // GLA_HEADS
GLA_HV = GLA_DV // GLA_HEADS
GLA_LOWRANK = 16
GLA_GATE_NORM = 16.0
D_FF = 4 * D_MODEL
EPS = 1e-6
IN_SIZES = (POOL_WIDTH, GLA_DK, GLA_DK, GLA_DV, GLA_DV, GLA_LOWRANK, D_MODEL, D_MODEL)
IN_WIDTH = sum(IN_SIZES)
IN_OFFSETS = tuple(int(o) for o in np.cumsum(IN_SIZES)[:-1])

kernel_name = "hybrid_pool_gla_adaln_stream_step"


def rmsnorm(x, g):
    xf = x.astype(jnp.float32)
    r = lax.rsqrt(jnp.mean(xf * xf, axis=-1, keepdims=True) + EPS)
    return (xf * r * g.astype(jnp.float32)).astype(x.dtype)


def modulate(h, shift, scale):
    return h * (1 + scale[:, None, :]) + shift[:, None, :]


def pool_mixer(u, hist, pos0, w_pool, pool_scale):
    B, L, _ = u.shape
    ext = jnp.concatenate([hist.astype(u.dtype), u], axis=1)
    cs = jnp.cumsum(ext.astype(jnp.float32), axis=1)
    cs = jnp.concatenate([jnp.zeros((B, 1, POOL_WIDTH), jnp.float32), cs], axis=1)
    end = cs[:, POOL_HIST + 1:]
    pos = pos0 + jnp.arange(L)
    uf = u.astype(jnp.float32)
    outs = []
    for gi, w in enumerate(POOL_WINDOWS):
        sl = slice(gi * POOL_GD, (gi + 1) * POOL_GD)
        start = cs[:, POOL_HIST + 1 - w: POOL_HIST + 1 - w + L, sl]
        cnt = jnp.minimum(pos + 1, w).astype(jnp.float32)[None, :, None]
        outs.append((end[..., sl] - start) / cnt - uf[..., sl])
    d = jnp.stack(outs, axis=2)
    mixed = jnp.einsum('blgc,gcd->blgd', d, w_pool.astype(jnp.float32)).reshape(B, L, POOL_WIDTH)
    mixed = mixed * pool_scale.astype(jnp.float32)
    return mixed.astype(u.dtype), ext[:, -POOL_HIST:]


def gla_block(S, q, k, v, a):
    C = q.shape[2]
    b = jnp.cumsum(a, axis=2)
    mask = jnp.tril(jnp.ones((C, C), dtype=bool))
    diff = b[:, :, :, None, :] - b[:, :, None, :, :]
    decay = jnp.exp(jnp.where(mask[None, None, :, :, None], diff, -jnp.inf))
    scores = jnp.einsum('bhic,bhijc,bhjc->bhij', q, decay, k)
    o = jnp.einsum('bhij,bhjv->bhiv', scores, v) + jnp.einsum('bhic,bhcv->bhiv', q * jnp.exp(b), S)
    b_last = b[:, :, -1:, :]
    S_new = jnp.exp(b_last[:, :, 0, :])[..., None] * S + jnp.einsum('bhjc,bhjv->bhcv', k * jnp.exp(b_last - b), v)
    return S_new, o


def gla_mixer(q, k, v, g, alr, S0, w_alpha, b_alpha, gla_norm_g):
    B, L, _ = q.shape
    f32 = jnp.float32
    a = jax.nn.log_sigmoid((alr @ w_alpha + b_alpha).astype(f32)) / GLA_GATE_NORM

    def heads(t, d):
        return t.astype(f32).reshape(B, L, GLA_HEADS, d).transpose(0, 2, 1, 3)

    qh = heads(q, GLA_HK) * (GLA_HK ** -0.5)
    kh = heads(k, GLA_HK)
    vh = heads(v, GLA_HV)
    ah = heads(a, GLA_HK)
    S0f = S0.astype(f32)
    C = min(L, CHUNK)
    N = L // C
    if N == 1:
        S_new, o = gla_block(S0f, qh, kh, vh, ah)
    else:
        def to_blocks(t):
            return t.reshape(B, GLA_HEADS, N, C, t.shape[-1]).transpose(2, 0, 1, 3, 4)
        S_new, o = lax.scan(lambda S, xs: gla_block(S, *xs), S0f,
                            (to_blocks(qh), to_blocks(kh), to_blocks(vh), to_blocks(ah)))
        o = o.transpose(1, 2, 0, 3, 4).reshape(B, GLA_HEADS, L, GLA_HV)
    o = rmsnorm(o.transpose(0, 2, 1, 3), gla_norm_g)
    o = o.reshape(B, L, GLA_DV) * jax.nn.silu(g.astype(f32))
    return o.astype(q.dtype), S_new.astype(S0.dtype)


def layer(x, c, pool_hist, S0, pos0, w_ada, b_ada, norm1_g, w_in, w_alpha, b_alpha, w_pool,
          pool_scale, gla_norm_g, w_pa, w_pb, w_out, norm2_g, w_ff1, w_ff2):
    mod = jax.nn.silu(c) @ w_ada + b_ada
    sh1, sc1, gt1, sh2, sc2, gt2 = jnp.split(mod, 6, axis=-1)
    h = modulate(rmsnorm(x, norm1_g), sh1, sc1)
    proj = h @ w_in
    u_pool, q, k, v, g, alr, ga, gb = jnp.split(proj, IN_OFFSETS, axis=-1)
    a_out, new_hist = pool_mixer(u_pool, pool_hist, pos0, w_pool, pool_scale)
    b_out, S_new = gla_mixer(q, k, v, g, alr, S0, w_alpha, b_alpha, gla_norm_g)
    merged = jax.nn.sigmoid(ga) * (a_out @ w_pa) + jax.nn.sigmoid(gb) * (b_out @ w_pb)
    x = x + gt1[:, None, :] * (merged @ w_out)
    h2 = modulate(rmsnorm(x, norm2_g), sh2, sc2)
    ff = jnp.square(jax.nn.relu(h2 @ w_ff1)) @ w_ff2
    x = x + gt2[:, None, :] * ff
    return x, new_hist, S_new


def setup_inputs(seed: int = 0) -> dict:
    key = jax.random.key(seed)
    ks = jax.random.split(key, 24)
    f32 = jnp.float32
    nrm = lambda k, shape, s: (jax.random.normal(k, shape, f32) * s)
    L_ = DEPTH
    return {
        "x_prompt": nrm(ks[0], (BATCH, SEQ, D_MODEL), 1.0),
        "x_sample": nrm(ks[1], (DEC_BATCH, DEC_SEQ, D_MODEL), 1.0),
        "c_prompt": nrm(ks[2], (BATCH, D_MODEL), 1.0),
        "c_sample": nrm(ks[3], (DEC_BATCH, D_MODEL), 1.0),
        "state_gla": nrm(ks[4], (L_, DEC_BATCH, GLA_HEADS, GLA_HK, GLA_HV), 1.0),
        "cache_pool": nrm(ks[5], (L_, DEC_BATCH, POOL_HIST, POOL_WIDTH), 1.0),
        "w_ada": nrm(ks[6], (L_, D_MODEL, 6 * D_MODEL), 0.5 * D_MODEL ** -0.5),
        "b_ada": nrm(ks[7], (L_, 6 * D_MODEL), 0.02),
        "norm1_g": 1.0 + nrm(ks[8], (L_, D_MODEL), 0.05),
        "w_in": nrm(ks[9], (L_, D_MODEL, IN_WIDTH), D_MODEL ** -0.5),
        "w_alpha": nrm(ks[10], (L_, GLA_LOWRANK, GLA_DK), GLA_LOWRANK ** -0.5),
        "b_alpha": nrm(ks[11], (L_, GLA_DK), 0.1),
        "w_pool": nrm(ks[12], (L_, POOL_GROUPS, POOL_GD, POOL_GD), POOL_GD ** -0.5),
        "pool_scale": 1.0 + nrm(ks[13], (L_, POOL_WIDTH), 0.1),
        "gla_norm_g": 1.0 + nrm(ks[14], (L_, GLA_HV), 0.05),
        "w_pa": nrm(ks[15], (L_, POOL_WIDTH, D_MODEL), POOL_WIDTH ** -0.5),
        "w_pb": nrm(ks[16], (L_, GLA_DV, D_MODEL), GLA_DV ** -0.5),
        "w_out": nrm(ks[17], (L_, D_MODEL, D_MODEL), D_MODEL ** -0.5),
        "norm2_g": 1.0 + nrm(ks[18], (L_, D_MODEL), 0.05),
        "w_ff1": nrm(ks[19], (L_, D_MODEL, D_FF), D_MODEL ** -0.5),
        "w_ff2": nrm(ks[20], (L_, D_FF, D_MODEL), D_FF ** -0.5),
        "final_g": 1.0 + nrm(ks[21], (D_MODEL,), 0.05),
    }


def reference(x_prompt, x_sample, c_prompt, c_sample, state_gla, cache_pool, w_ada, b_ada,
              norm1_g, w_in, w_alpha, b_alpha, w_pool, pool_scale, gla_norm_g, w_pa, w_pb,
              w_out, norm2_g, w_ff1, w_ff2, final_g):
    yp, ys = x_prompt, x_sample
    sp_list, hp_list, ss_list, hs_list = [], [], [], []
    for l in range(DEPTH):
        params = (w_ada[l], b_ada[l], norm1_g[l], w_in[l], w_alpha[l], b_alpha[l], w_pool[l],
                  pool_scale[l], gla_norm_g[l], w_pa[l], w_pb[l], w_out[l], norm2_g[l],
                  w_ff1[l], w_ff2[l])
        hist0 = jnp.zeros((BATCH, POOL_HIST, POOL_WIDTH), x_prompt.dtype)
        S0 = jnp.zeros((BATCH, GLA_HEADS, GLA_HK, GLA_HV), state_gla.dtype)
        yp, hp, sp = layer(yp, c_prompt, hist0, S0, 0, *params)
        ys, hs, ss = layer(ys, c_sample, cache_pool[l], state_gla[l], PAST_LEN, *params)
        sp_list.append(sp)
        hp_list.append(hp)
        ss_list.append(ss)
        hs_list.append(hs)
    y_prompt = rmsnorm(yp, final_g)
    y_sample = rmsnorm(ys, final_g)
    state_gla_prompt = jnp.stack(sp_list, axis=0)
    cache_pool_prompt = jnp.stack(hp_list, axis=0)
    state_gla_sample = jnp.stack(ss_list, axis=0)
    cache_pool_sample = jnp.stack(hs_list, axis=0)
    return (y_prompt, y_sample, state_gla_prompt, cache_pool_prompt, state_gla_sample, cache_pool_sample)
```

```python
import contextlib
import numpy as np
import concourse.bass as bass
import concourse.mybir as mybir
from concourse.bass_utils import run_bass_kernel_spmd

F32 = mybir.dt.float32
BF16 = mybir.dt.bfloat16
ALU = mybir.AluOpType
AF = mybir.ActivationFunctionType

NCORES = 8
D = 1024
SEQ = 16384
TPC = SEQ // NCORES
NT = 4
N = 512
SPC = 4
LS = 16
INW = 5648
EPS = 1e-6
O_U, O_Q, O_K, O_V, O_G, O_ALR, O_GA, O_GB = 0, 512, 1024, 1536, 2560, 3584, 3600, 4624

ENGS = ("pe", "act", "dve", "pool", "sp")
WITH_EXCHANGE = False
STRICT_SAME_ENGINE = True
CARRY = "prefix"
NPRE = 7 * NT
DEBUG = False
MODE = "full"


class TT:
    def __init__(self, name, h):
        self.name = name
        self.h = h
        self.w = {}
        self.r = {}
        self.kw = {}
        self.kr = {}
        self.dsem = None
        self.dcnt = 0
        self.ssem = None
        self.scnt = 0

    def __getitem__(self, idx):
        return self.h[idx]


class Sched:
    def __init__(self, nc, es):
        self.nc = nc
        self.es = es
        self.sems = {}
        self.cnt = {e: 0 for e in ENGS}
        self.seen = {e: {} for e in ENGS}
        self.prog = {e: [] for e in ENGS}
        self.nwaits = 0
        self.nops = 0
        for e in ENGS:
            self._sem("E_" + e)

    def _sem(self, name):
        if name not in self.sems:
            self.sems[name] = self.es.enter_context(self.nc.semaphore(name))
        return name

    def sb(self, name, shape, dt=F32):
        h = self.es.enter_context(self.nc.sbuf_tensor(name, list(shape), dt))
        return TT(name, h)

    def ps(self, name, shape, dt=F32):
        h = self.es.enter_context(self.nc.psum_tensor(name, list(shape), dt))
        return TT(name, h)

    @staticmethod
    def _split(lst):
        tts, keys = [], []
        for x in lst:
            if isinstance(x, tuple):
                tts.append(x[0])
                keys.append(x[1])
            else:
                tts.append(x)
                keys.append(None)
        return tts, keys

    def _waits(self, e, reads, writes, rkeys=None, wkeys=None):
        waits = {}
        own = "E_" + e
        rkeys = rkeys or [None] * len(reads)
        wkeys = wkeys or [None] * len(writes)

        def need(nm, v):
            if self.seen[e].get(nm, 0) >= v:
                return
            if waits.get(nm, 0) < v:
                waits[nm] = v

        same = e not in ("pe", "sp")
        for t in reads:
            for nm, v in t.w.items():
                if nm == own:
                    if same:
                        need(nm, v)
                else:
                    need(nm, v)
        for t, key in zip(writes, wkeys):
            for dct, kd in ((t.w, t.kw), (t.r, t.kr)):
                for nm, v in dct.items():
                    if nm == own:
                        if same and STRICT_SAME_ENGINE:
                            if key is None:
                                need(nm, v)
                            else:
                                vv = max(kd.get(key, {}).get(own, 0), kd.get(None, {}).get(own, 0))
                                if vv:
                                    need(nm, vv)
                    else:
                        need(nm, v)
        for nm, v in waits.items():
            self.seen[e][nm] = v
        return list(waits.items())

    def op(self, e, fn, reads=(), writes=(), signal=True):
        reads, rkeys = self._split(reads)
        writes, wkeys = self._split(writes)
        waits = self._waits(e, reads, writes, rkeys, wkeys)
        own = "E_" + e
        if signal:
            self.cnt[e] += 1
            val = self.cnt[e]
        else:
            val = self.cnt[e] + 1
        for t, key in zip(reads, rkeys):
            if t.r.get(own, 0) < val:
                t.r[own] = val
            t.kr.setdefault(key, {})[own] = val
        for t, key in zip(writes, wkeys):
            if t.w.get(own, 0) < val:
                t.w[own] = val
            t.kw.setdefault(key, {})[own] = val
        sems = self.sems
        self.nwaits += len(waits)
        self.nops += 1

        def emit(eng):
            for nm, v in waits:
                eng.wait_ge(sems[nm], v)
            ins = fn(eng)
            if signal:
                ins.then_inc(sems[own], 1)

        self.prog[e].append(emit)

    def dma_load(self, q, dst, dst_ap, src_ap, reads=(), **kw):
        if dst.dsem is None:
            dst.dsem = self._sem("L_" + dst.name)
        waits = self._waits(q, reads, [dst])
        dst.dcnt += 1
        val = 16 * dst.dcnt
        dst.w[dst.dsem] = val
        for t in reads:
            t.r[dst.dsem] = val
        sems = self.sems
        sem = sems[dst.dsem]
        self.nwaits += len(waits)

        def emit(eng):
            for nm, v in waits:
                eng.wait_ge(sems[nm], v)
            eng.dma_start(out=dst_ap, in_=src_ap, **kw).then_inc(sem, 16)

        self.prog[q].append(emit)

    def dma_store(self, q, src, dst_ap, src_ap, dram_tt=None, **kw):
        if src.ssem is None:
            src.ssem = self._sem("S_" + src.name)
        waits = self._waits(q, [src], [])
        src.scnt += 1
        val = 16 * src.scnt
        src.r[src.ssem] = val
        if dram_tt is not None:
            dram_tt.w[src.ssem] = val
        sems = self.sems
        sem = sems[src.ssem]
        self.nwaits += len(waits)

        def emit(eng):
            for nm, v in waits:
                eng.wait_ge(sems[nm], v)
            eng.dma_start(out=dst_ap, in_=src_ap, **kw).then_inc(sem, 16)

        self.prog[q].append(emit)

    def fence(self, tiles):
        allev = {}
        for t in tiles:
            for dct in (t.w, t.r):
                for nm, v in dct.items():
                    if allev.get(nm, 0) < v:
                        allev[nm] = v
        for t in tiles:
            kwn = t.kw.setdefault(None, {})
            krn = t.kr.setdefault(None, {})
            for nm, v in allev.items():
                if t.r.get(nm, 0) < v:
                    t.r[nm] = v
                if t.w.get(nm, 0) < v:
                    t.w[nm] = v
                if kwn.get(nm, 0) < v:
                    kwn[nm] = v
                if krn.get(nm, 0) < v:
                    krn[nm] = v

    def raw(self, e, fn):
        self.prog[e].append(fn)

    def finish(self, out_tiles):
        sems = self.sems
        waits = []
        for t in out_tiles:
            if t.ssem is not None:
                waits.append((t.ssem, 16 * t.scnt))
        for e in ENGS:
            if e != "sp" and self.cnt[e] > 0:
                waits.append(("E_" + e, self.cnt[e]))

        def emit(eng):
            for nm, v in waits:
                eng.wait_ge(sems[nm], v)

        self.prog["sp"].append(emit)

    def emit_all(self):
        nc = self.nc
        prog = self.prog
        with nc.Block() as block:

            @block.tensor
            def _(eng):
                for f in prog["pe"]:
                    f(eng)

            @block.scalar
            def _(eng):
                for f in prog["act"]:
                    f(eng)

            @block.vector
            def _(eng):
                for f in prog["dve"]:
                    f(eng)

            @block.gpsimd
            def _(eng):
                for f in prog["pool"]:
                    f(eng)

            @block.sync
            def _(eng):
                for f in prog["sp"]:
                    f(eng)


def build():
    nc = bass.Bass("TRN2", target_bir_lowering=False)

    def din(name, shape):
        return nc.dram_tensor(name, list(shape), F32, kind="ExternalInput").ap()

    def dout(name, shape):
        return nc.dram_tensor(name, list(shape), F32, kind="ExternalOutput").ap()

    xp = din("xp", [TPC, D])
    xpre = din("xpre", [7 * TPC, D]) if CARRY == "prefix" else None
    xh = din("xh", [16, D])
    xs = din("xs", [SPC * LS, D])
    cvec = din("cvec", [5, D])
    s0 = din("s0", [SPC, 4, 128, 256])
    cache = din("cache", [SPC * 15, 512])
    w_ada = din("w_ada", [D, 6 * D])
    b_ada = din("b_ada", [1, 6 * D])
    norm1_g = din("norm1_g", [1, D])
    w_in = din("w_in", [D, INW])
    w_alpha = din("w_alpha", [16, 512])
    b_alpha = din("b_alpha", [1, 512])
    w_pool = din("w_pool", [4, 128, 128])
    pool_scale = din("pool_scale", [1, 512])
    gla_norm_g = din("gla_norm_g", [1, 256])
    w_pa = din("w_pa", [512, D])
    w_pb = din("w_pb", [D, D])
    w_out = din("w_out", [D, D])
    norm2_g = din("norm2_g", [1, D])
    w_ff1 = din("w_ff1", [D, 4 * D])
    w_ff2 = din("w_ff2", [4 * D, D])
    final_g = din("final_g", [1, D])
    cst128 = din("cst128", [128, 384])
    cst64 = din("cst64", [64, 132])
    sel = din("sel", [5, 192])
    invcnt = din("invcnt", [128, 64])
    flags = din("flags", [128, 16])

    yp = dout("yp", [TPC, D])
    ys = dout("ys", [SPC * LS, D])
    sp_o = dout("sp_o", [4, 128, 256])
    cp_o = dout("cp_o", [15, 512])
    ss_o = dout("ss_o", [SPC, 4, 128, 256])
    cs_o = dout("cs_o", [SPC * 15, 512])
    gin = nc.dram_tensor("gin", [128, 1028], F32, kind="Internal").ap()
    gout = nc.dram_tensor("gout", [NCORES * 128, 1028], F32, kind="Internal").ap()
    wscr = nc.dram_tensor("wscr", [40, 128, 4096], BF16, kind="Internal").ap()

    dbg_outs = {}

    def dump(S_, name, tt, ap, rows, cols, inner=None):
        o = nc.dram_tensor("dbg_" + name, [rows, cols], F32, kind="ExternalOutput").ap()
        dbg_outs[name] = o
        stg = dbg_stage[0]
        sv_ = stg[0:rows, 0:cols]
        if inner is not None:
            sv_ = sv_.rearrange("p (a b) -> p a b", b=inner)
        S_.op("dve", lambda e: e.tensor_copy(sv_, ap), [tt], [stg])
        S_.dma_store("sp", stg, o, stg[0:rows, 0:cols])

    dbg_stage = [None]
    with contextlib.ExitStack() as es:
        S = Sched(nc, es)
        es.enter_context(nc.allow_non_contiguous_dma("tiny vector re-layouts"))
        out_tiles = []

        xt = S.sb("xt", [128, 4, D])
        xn = S.sb("xn", [128, D], BF16)
        hT = S.sb("hT", [128, 8, N], BF16)
        NSLOT = 4
        ring = [S.sb("ring%d" % i, [128, 4096], BF16) for i in range(NSLOT)]
        G1 = S.sb("G1", [128, D])
        G2 = S.sb("G2", [128, D])
        FG = S.sb("FG", [128, D])
        St = S.sb("St", [128, 4, 256])
        Sb = [S.sb("Sb%d" % i, [128, 4, 256], BF16) for i in range(4)]
        yo = S.sb("yo", [128, D])
        if DEBUG:
            dbg_stage[0] = S.sb("dbgstg", [128, D])
        c128 = S.sb("c128", [128, 384])
        identb = S.sb("identb", [128, 128], BF16)
        CUM4 = S.sb("CUM4", [128, 4, 128])
        c64 = S.sb("c64", [64, 132])
        CUM4s = S.sb("CUM4s", [64, 4, 64])
        ones_f = S.sb("ones_f", [128, 128])
        ones_b = S.sb("ones_b", [128, 128], BF16)
        walpha = S.sb("walpha", [17, 512])
        balpha = S.sb("balpha", [1, 512])
        wpool = S.sb("wpool", [128, 4, 128], BF16)
        vecs = S.sb("vecs", [128, 24])
        modT = S.sb("modT", [128, 6, 8, 5])
        AB = S.sb("AB", [128, 4, 8, 5])
        gtm1 = S.sb("gtm1", [5, D])
        gtm2 = S.sb("gtm2", [5, D])
        selt = S.sb("selt", [5, 192])
        invc = S.sb("invc", [128, 64])
        flg = S.sb("flg", [128, 16])
        stat = S.sb("stat", [128, 16])
        stat2 = S.sb("stat2", [128, 16])
        Lsum = S.sb("Lsum", [128, 4])
        ebl = S.sb("ebl", [128, 8])
        xht = S.sb("xht", [16, D])
        hTh = S.sb("hTh", [128, 8, 16], BF16)

        AR_WORDS = 24200
        arena = S.sb("arena", [128, AR_WORDS])
        AA = arena.h[:]
        arena_tts = []
        off = [0]

        def carve(name, words, dt, shape=None, at=None):
            o = off[0] if at is None else at
            v = AA[:, o:o + words]
            if dt == BF16:
                v = v.bitcast(BF16)
            if shape is not None and len(shape) == 2:
                v = v.rearrange("p (a b) -> p a b", b=shape[1])
            elif shape is not None and len(shape) == 3:
                v = v.rearrange("p (a b c) -> p a b c", b=shape[1], c=shape[2])
            if at is None:
                off[0] += words
            t = TT(name, v)
            arena_tts.append(t)
            return t, o

        uT, _ = carve("uT", 2112, F32)
        qT, o_q = carve("qT", 2048, F32, (4, N))
        kT, _ = carve("kT", 2048, F32, (4, N))
        ktm, o_ktm = carve("ktm", 2048, F32, (4, 512))
        vtm, _ = carve("vtm", 2048, BF16, (4, 1024))
        sgT, o_sg = carve("sgT", 2048, BF16, (8, N))
        alrT, o_alr = carve("alrT", 512, F32)
        apt, _ = carve("apt", 512, F32)
        E3, _ = carve("E3", 512, F32)
        khat, _ = carve("khat", 256, BF16)
        khs, _ = carve("khs", 256, BF16)
        E1, _ = carve("E1", 512, F32, (4, 128))
        E2, _ = carve("E2", 512, F32, (4, 128))
        qtil, _ = carve("qtil", 256, BF16, (4, 128))
        ktil, _ = carve("ktil", 256, BF16, (4, 128))
        scm, _ = carve("scm", 256, BF16, (4, 128))
        sq, _ = carve("sq", 512, BF16, (8, 128))
        rstT, _ = carve("rstT", 512, F32, (4, 128))
        otmp, _ = carve("otmp", 256, F32, (2, 128))
        boT, o_bo = carve("boT", 2048, BF16, (8, N))
        dT, _ = carve("dT", 1024, BF16, (4, N))
        aoT, _ = carve("aoT", 1024, BF16, (4, N))
        tA, _ = carve("tA", 528, F32)
        tB, _ = carve("tB", 528, F32)
        utm, _ = carve("utm", 512, F32)
        assert off[0] <= AR_WORDS, off[0]
        mgA, _ = carve("mgA", 4096, F32, (8, N), at=o_q)
        mrg, _ = carve("mrg", 2048, BF16, (8, N), at=o_ktm)
        ffT, _ = carve("ffT", 8192, BF16, (32, N), at=o_q)
        rl = [carve("rl%d" % i, 512, F32, at=o_sg + 512 * i)[0] for i in range(2)]
        rt = [carve("rt%d" % i, 512, F32, at=o_sg + 1024 + 512 * i)[0] for i in range(2)]
        xnx, _ = carve("xnx", 4096, F32, (4, D), at=o_alr)
        xn2, _ = carve("xn2", 512, BF16, at=o_alr + 4096)
        jkm = [carve("jkm%d" % i, 512, BF16, at=o_alr + 4608 + 512 * i)[0] for i in range(2)]
        yo2, _ = carve("yo2", 1024, F32, at=o_alr + 5632)
        assert o_alr + 6656 <= AR_WORDS
        cl, _ = carve("cl", 1024, F32, at=0)
        scl, _ = carve("scl", 1024, F32, at=1024)
        mpart, _ = carve("mpart", 1024, F32, at=2048)
        badab, _ = carve("badab", 1024, F32, at=3072)
        scT, _ = carve("scT", 20, BF16, (8, 5), at=4096)
        gbuf, _ = carve("gbuf", 1028, F32, at=4200)
        gb2 = [carve("gb2%d" % i, 1028, F32, at=5300 + 1100 * i)[0] for i in range(2)]
        Tacc, _ = carve("Tacc", 1024, F32, (4, 256), at=7600)
        Dj, _ = carve("Dj", 4, F32, at=8700)
        cach, _ = carve("cach", 512, F32, at=o_bo)

        def fence():
            S.fence(arena_tts)

        PSB = [S.ps("psb%d" % i, [128, 2 * N], BF16) for i in range(2)]
        PS = [S.ps("ps%d" % i, [128, N]) for i in range(6)]
        psc = [0, 0]

        def nps():
            t = PS[psc[0] % 6]
            psc[0] += 1
            return t

        def npsb():
            t = PSB[psc[1] % 2]
            psc[1] += 1
            return t

        slotc = [0]

        scr = {}

        def slab(wname, w, c0, C, K=8, r0=0, cache=True):
            t = ring[slotc[0] % NSLOT]
            slotc[0] += 1
            flat = t.h[:, 0:K * C]
            v = flat.rearrange("p (k c) -> p k c", c=C)
            key = (wname, c0, C, K, r0)
            if not cache:
                S.dma_load("pool", t, v, wsl(w, c0, C, K, r0))
            elif key not in scr:
                idx = len(scr)
                dtt = TT("scr%d" % idx, None)
                scr[key] = (idx, dtt)
                S.dma_load("pool", t, v, wsl(w, c0, C, K, r0))
                S.dma_store("sp", t, wscr[idx, :, 0:K * C], flat, dram_tt=dtt)
            else:
                idx, dtt = scr[key]
                S.dma_load("pool", t, flat, wscr[idx, :, 0:K * C], reads=[dtt])
            return t, v

        def wsl(w, c0, C, K=8, r0=0):
            return w[r0:r0 + K * 128, c0:c0 + C].rearrange("(k p) n -> p k n", p=128)

        S.dma_load("sp", c128, c128[:], cst128)
        S.dma_load("sp", c64, c64[:], cst64)
        S.dma_load("sp", selt, selt[:], sel)
        S.dma_load("sp", invc, invc[:], invcnt)
        S.dma_load("sp", flg, flg[:], flags)
        S.dma_load("sp", walpha, walpha[0:16, :], w_alpha)
        S.dma_load("sp", walpha, walpha[16:17, :], b_alpha)
        S.dma_load("sp", balpha, balpha[:], b_alpha)
        S.dma_load("sp", FG, FG[:], final_g.partition_broadcast(128))
        S.dma_load("sp", vecs, vecs[:, 0:8], norm1_g.rearrange("o (k p) -> p (o k)", p=128))
        S.dma_load("sp", vecs, vecs[:, 8:16], norm2_g.rearrange("o (k p) -> p (o k)", p=128))
        S.dma_load("sp", vecs, vecs[:, 16:20], pool_scale.rearrange("o (k p) -> p (o k)", p=128))
        S.dma_load("sp", vecs, vecs[:, 20:22], gla_norm_g.rearrange("o (k p) -> p (o k)", p=128))
        S.dma_load("pool", wpool, wpool[:], w_pool.rearrange("g c d -> c g d"))
        S.dma_load("sp", cl, cl[0:5, :], cvec)
        S.dma_load("sp", xht, xht[:], xh)
        identf = c128.h[:, 0:128]
        CUM = c128.h[:, 128:256]
        REM = c128.h[:, 256:384]
        CUMs = c64.h[:, 0:64]
        REMs = c64.h[:, 64:128]
        SEGM = c64.h[:, 128:132]
        S.op("dve", lambda e: e.tensor_copy(identb[:], identf), [c128], [identb])
        for h in range(4):
            S.op("dve", lambda e, h=h: e.tensor_copy(CUM4[:, h, :], CUM), [c128], [CUM4])
            S.op("dve", lambda e, h=h: e.tensor_copy(CUM4s[:, h, :], CUMs), [c64], [CUM4s])
        S.op("dve", lambda e: e.memset(ones_f[:], 1.0), [], [ones_f])
        S.op("dve", lambda e: e.memset(ones_b[:], 1.0), [], [ones_b])
        S.op("dve", lambda e: e.memset(St[:], 0.0), [], [St])
        S.op("dve", lambda e: e.memset(Sb[0][:], 0.0), [], [Sb[0]])
        S.op("dve", lambda e: e.memset(Lsum[:], 0.0), [], [Lsum])

        S.op("act", lambda e: e.activation(scl[0:5, :], cl[0:5, :], AF.Silu), [cl], [scl])
        p = nps()
        for k in range(8):
            S.op("pe", lambda e, k=k, p=p: e.transpose(p[:, k * 5:(k + 1) * 5], scl[0:5, k * 128:(k + 1) * 128],
                                                        identf[0:5, 0:5]), [scl, c128], [p], signal=(k == 7))
        S.op("dve", lambda e, p=p: e.tensor_copy(scT[:].rearrange("p a b -> p (a b)"), p[:, 0:40]), [p], [scT])
        for j in range(6):
            S.dma_load("sp", badab, badab[0:5, :], b_ada[:, j * D:(j + 1) * D].partition_broadcast(5))
            dst = gtm1 if j == 2 else (gtm2 if j == 5 else mpart)
            for hf in range(2):
                st, sv = slab("w_ada", w_ada, j * D + hf * 512, 512, cache=False)
                p = nps()
                for k in range(8):
                    S.op("pe", lambda e, k=k, p=p, sv=sv: e.matmul(p[0:5, :], scT[:, k, :], sv[:, k, :],
                                                                    start=(k == 0), stop=(k == 7)),
                         [scT, st], [p], signal=(k == 7))
                S.op("dve", lambda e, p=p, hf=hf, dst=dst: e.tensor_tensor(
                    dst[0:5, hf * 512:(hf + 1) * 512], p[0:5, :], badab[0:5, hf * 512:(hf + 1) * 512], ALU.add),
                    [p, badab], [dst])
            p = nps()
            for k in range(8):
                S.op("pe", lambda e, k=k, p=p, dst=dst: e.transpose(p[:, k * 5:(k + 1) * 5], dst[0:5, k * 128:(k + 1) * 128],
                                                                      identf[0:5, 0:5]), [dst, c128], [p], signal=(k == 7))
            S.op("dve", lambda e, p=p, j=j: e.tensor_copy(modT[:, j, :, :].rearrange("p a b -> p (a b)"), p[:, 0:40]),
                 [p], [modT])
        for i, (jsh, jsc, vo) in enumerate(((0, 1, 0), (3, 4, 8))):
            S.op("dve", lambda e, i=i, jsc=jsc: e.tensor_scalar(AB[:, 2 * i, :, :], modT[:, jsc, :, :], 1.0, None, ALU.add),
                 [modT], [AB])
            for k in range(8):
                S.op("dve", lambda e, i=i, k=k, vo=vo: e.tensor_scalar(AB[:, 2 * i, k, :], AB[:, 2 * i, k, :],
                                                                        vecs[:, vo + k:vo + k + 1], None, ALU.mult),
                     [AB, vecs], [AB])
            S.op("dve", lambda e, i=i, jsh=jsh: e.tensor_copy(AB[:, 2 * i + 1, :, :], modT[:, jsh, :, :]), [modT], [AB])

        def build_gates(selv, G):
            for gt, Gd in ((gtm1, G1), (gtm2, G2)):
                for hf in range(2):
                    p = nps()
                    S.op("pe", lambda e, p=p, gt=gt, hf=hf: e.matmul(p[0:G, :], selv, gt[0:5, hf * 512:(hf + 1) * 512],
                                                                      start=True, stop=True), [selt, gt], [p])
                    S.op("dve", lambda e, p=p, Gd=Gd, hf=hf: e.tensor_copy(Gd[0:G, hf * 512:(hf + 1) * 512], p[0:G, :]),
                         [p], [Gd])

        build_gates(selt.h[0:5, 0:128], 128)
        fence()

        def norm_to_hT(xsrc, rows, gcols, which, segs, hdst=None, xn=xn, stc=0):
            xsrc_t, xa = xsrc
            hd = hT if hdst is None else hdst
            ncol = rows
            S.op("act", lambda e: e.activation(xn[0:rows, :], xa, AF.Square, accum_out=stat[0:rows, 0:1]),
                 [xsrc_t], [xn, stat])
            S.op("act", lambda e: e.activation(stat[0:rows, 1:2], stat[0:rows, 0:1], AF.Ln, bias=EPS, scale=1.0 / D),
                 [stat], [stat])
            S.op("act", lambda e: e.activation(stat[0:rows, 2:3], stat[0:rows, 1:2], AF.Exp, scale=-0.5), [stat], [stat])
            S.op("dve", lambda e: e.tensor_scalar(xn[0:rows, :], xa, stat[0:rows, 2:3], None, ALU.mult),
                 [xsrc_t, stat], [xn])
            p = npsb()
            for k in range(8):
                S.op("pe", lambda e, k=k, p=p: e.transpose(p[:, k * 128:k * 128 + rows], xn[0:rows, k * 128:(k + 1) * 128],
                                                            identb[0:rows, 0:rows]), [xn, identb], [p], signal=(k == 7))
            ia, ib = (0, 1) if which == 1 else (2, 3)
            for k in range(8):
                for (c0, c1, r) in segs:
                    S.op("dve", lambda e, k=k, p=p, c0=c0, c1=c1, r=r: e.tensor_scalar(
                        hd[:, k, gcols + c0:gcols + c1], p[:, k * 128 + c0:k * 128 + c1],
                        AB[:, ia, k, r:r + 1], AB[:, ib, k, r:r + 1], ALU.mult, ALU.add), [p, AB], [(hd, (k, gcols + c0))])

        def mm_fm(p, M, st, sv, c0, ncols, rhs_t=None, K=8):
            rt_ = hT if rhs_t is None else rhs_t
            for k in range(K):
                S.op("pe", lambda e, k=k: e.matmul(p[0:M, 0:ncols], sv[:, k, c0:c0 + M], rt_[:, k, 0:ncols],
                                                   start=(k == 0), stop=(k == K - 1)),
                     [st, rt_], [p], signal=(k == K - 1))

        def mm_tm(p, G, gcols, st, sv, c0, C, lhs_t=None, K=8, k0=0, first=True, last=True):
            lt = hT if lhs_t is None else lhs_t
            for k in range(K):
                S.op("pe", lambda e, k=k: e.matmul(p[0:G, 0:C], lt[:, k0 + k, gcols:gcols + G], sv[:, k, c0:c0 + C],
                                                   start=(first and k == 0), stop=(last and k == K - 1)),
                     [st, lt], [p], signal=(k == K - 1))

        def gla_decay(G, gcols, cumv, remv, need_full):
            p = nps()
            S.op("pe", lambda e: e.matmul(p[0:G, :], alrT[0:17, gcols:gcols + G], walpha[0:17, :], start=True, stop=True),
                 [alrT, walpha], [p])
            S.op("act", lambda e: e.activation(apt[0:G, :], p[0:G, :], AF.Exp, scale=-1.0), [p], [apt])
            S.op("act", lambda e: e.activation(apt[0:G, :], apt[0:G, :], AF.Ln, bias=1.0), [apt], [apt])
            p2 = nps()
            S.op("pe", lambda e: e.matmul(p2[0:G, :], remv, apt[0:G, :], start=True, stop=True), [c128, c64, apt], [p2])
            S.op("act", lambda e: e.activation(E3[0:G, :], p2[0:G, :], AF.Exp, scale=-1.0 / 16), [p2], [E3])
            S.op("dve", lambda e: e.tensor_tensor(khat[0:G, :], ktm[0:G, gcols // 128 if G == 128 else 0, :], E3[0:G, :],
                                                  ALU.mult), [ktm, E3], [khat])
            if not need_full:
                return
            p3 = nps()
            for h in range(4):
                S.op("pe", lambda e, h=h: e.matmul(p3[:, h * G:(h + 1) * G], apt[0:G, h * 128:(h + 1) * 128], cumv,
                                                   start=True, stop=True), [apt, c128, c64], [p3], signal=(h == 3))
            S.op("act", lambda e: e.activation(E1[:, :, 0:G], p3[:, 0:4 * G].rearrange("p (a b) -> p a b", b=G), AF.Exp,
                                               scale=-1.0 / 16), [p3], [E1])
            S.op("act", lambda e: e.activation(E2[:, :, 0:G], p3[:, 0:4 * G].rearrange("p (a b) -> p a b", b=G), AF.Exp,
                                               scale=1.0 / 16), [p3], [E2])
            S.op("dve", lambda e: e.scalar_tensor_tensor(qtil[:, :, 0:G], qT[:, :, gcols:gcols + G], 128.0 ** -0.5,
                                                         E1[:, :, 0:G], ALU.mult, ALU.mult), [qT, E1], [qtil])
            S.op("dve", lambda e: e.tensor_tensor(ktil[:, :, 0:G], kT[:, :, gcols:gcols + G], E2[:, :, 0:G], ALU.mult),
                 [kT, E2], [ktil])

        def pool_branch(nseq, L, NN, first_tile):
            W = 16 + L
            U = uT.h[:, 0:4 * nseq * W].rearrange("p (g s w) -> p g s w", s=nseq, w=W)
            A = tA.h[:, 0:nseq * W].rearrange("p (s w) -> p s w", w=W)
            B = tB.h[:, 0:nseq * W].rearrange("p (s w) -> p s w", w=W)
            dv = dT.h[:, :, 0:NN].rearrange("p g (s l) -> p g s l", l=L)
            for gi in range(4):
                src = U[:, gi]
                cur_t, cur = uT, src
                lo = 0
                for lev in range(gi + 1):
                    sh = 1 << lev
                    lo2 = lo + sh
                    dstt, dst = (tA, A) if lev % 2 == 0 else (tB, B)
                    S.op("dve", lambda e, dst=dst, cur=cur, lo2=lo2, sh=sh: e.tensor_tensor(
                        dst[:, :, lo2:W], cur[:, :, lo2:W], cur[:, :, lo2 - sh:W - sh], ALU.add), [cur_t], [dstt])
                    cur_t, cur, lo = dstt, dst, lo2
                wdt = float(1 << (gi + 1))
                S.op("dve", lambda e, cur=cur, gi=gi, wdt=wdt: e.scalar_tensor_tensor(
                    dv[:, gi], cur[:, :, 16:W], 1.0 / wdt, U[:, gi, :, 16:W], ALU.mult, ALU.subtract), [cur_t, uT], [dT])
                if first_tile:
                    S.op("dve", lambda e, cur=cur, gi=gi: e.tensor_tensor(
                        cur[:, 0, 16:32], cur[:, 0, 16:32], invc[:, gi * 16:(gi + 1) * 16], ALU.mult), [cur_t, invc], [cur_t])
                    S.op("dve", lambda e, cur=cur, gi=gi: e.tensor_tensor(
                        dT[:, gi, 0:16], cur[:, 0, 16:32], U[:, gi, 0, 16:32], ALU.subtract), [cur_t, uT], [dT])
            for gi in range(4):
                p = nps()
                S.op("pe", lambda e, gi=gi, p=p: e.matmul(p[:, 0:NN], wpool[:, gi, :], dT[:, gi, 0:NN], start=True, stop=True),
                     [wpool, dT], [p])
                S.op("dve", lambda e, gi=gi, p=p: e.tensor_scalar(aoT[:, gi, 0:NN], p[:, 0:NN], vecs[:, 16 + gi:17 + gi], None,
                                                                  ALU.mult), [p, vecs], [(aoT, gi)])

        def gla_out(G, gcols, vrow, mask4, segstates, sbsel, mid_hook=None):
            p = nps()
            for h in range(4):
                S.op("pe", lambda e, h=h: e.matmul(p[0:G, h * G:(h + 1) * G], ktil[:, h, 0:G], qtil[:, h, 0:G],
                                                   start=True, stop=True), [ktil, qtil], [p], signal=(h == 3))
            S.op("dve", lambda e: e.tensor_tensor(scm[0:G, :, 0:G], p[0:G, 0:4 * G].rearrange("p (a b) -> p a b", b=G),
                                                  mask4, ALU.mult), [p, CUM4, CUM4s], [scm])
            po = [nps(), nps()]
            for h in range(4):
                for vc in range(2):
                    pp = po[h // 2]
                    o0 = ((h % 2) * 2 + vc) * G
                    S.op("pe", lambda e, h=h, vc=vc, pp=pp, o0=o0: e.matmul(
                        pp[:, o0:o0 + G], vtm[0:G, vrow, h * 256 + vc * 128:h * 256 + (vc + 1) * 128], scm[0:G, h, 0:G],
                        start=True, stop=False), [vtm, scm], [pp], signal=False)
                    ns = len(segstates)
                    for si, (c0, c1, sbt) in enumerate(segstates):
                        S.op("pe", lambda e, h=h, vc=vc, pp=pp, o0=o0, c0=c0, c1=c1, sbt=sbt, si=si: e.matmul(
                            pp[:, o0 + c0:o0 + c1], sbt[:, h, vc * 128:(vc + 1) * 128], qtil[:, h, c0:c1],
                            start=False, stop=(si == ns - 1)), [sbt, qtil], [pp],
                            signal=(si == ns - 1 and vc == 1 and h % 2 == 1))
            if mid_hook is not None:
                mid_hook()
            for half in range(2):
                S.op("act", lambda e, half=half: e.activation(
                    sq[:, half * 4:(half + 1) * 4, 0:G], po[half][:, 0:4 * G].rearrange("p (a b) -> p a b", b=G), AF.Square),
                    [po[half]], [(sq, half)])
            pss = nps()
            for h in range(4):
                for vc in range(2):
                    S.op("pe", lambda e, h=h, vc=vc: e.matmul(pss[:, h * G:(h + 1) * G], ones_b[:], sq[:, h * 2 + vc, 0:G],
                                                              start=(vc == 0), stop=(vc == 1)), [ones_b, sq], [pss],
                         signal=(h == 3 and vc == 1))
            S.op("act", lambda e: e.activation(rstT[:, :, 0:G], pss[:, 0:4 * G].rearrange("p (a b) -> p a b", b=G), AF.Ln,
                                               bias=EPS, scale=1.0 / 256), [pss], [rstT])
            S.op("act", lambda e: e.activation(rstT[:, :, 0:G], rstT[:, :, 0:G], AF.Exp, scale=-0.5), [rstT], [rstT])
            for h in range(4):
                for vc in range(2):
                    pp = po[h // 2]
                    o0 = ((h % 2) * 2 + vc) * G
                    S.op("dve", lambda e, h=h, vc=vc, pp=pp, o0=o0: e.scalar_tensor_tensor(
                        otmp[:, vc, 0:G], pp[:, o0:o0 + G], vecs[:, 20 + vc:21 + vc], rstT[:, h, 0:G],
                        ALU.mult, ALU.mult), [pp, vecs, rstT], [(otmp, vc)])
                S.op("dve", lambda e, h=h: e.tensor_tensor(boT[:, 2 * h:2 * h + 2, gcols:gcols + G], otmp[:, :, 0:G],
                                                           sgT[:, 2 * h:2 * h + 2, gcols:gcols + G], ALU.mult), [otmp, sgT], [boT])

        def state_update(G, vrow, khv_t, khv, Sfp, ecol, Sbf_dst, vtm=vtm):
            pp = [nps(), nps()]
            for h in range(4):
                S.op("pe", lambda e, h=h: e.matmul(pp[h // 2][:, (h % 2) * 256:(h % 2 + 1) * 256], khv[0:G, h * 128:(h + 1) * 128],
                                                   vtm[0:G, vrow, h * 256:(h + 1) * 256], start=True, stop=True),
                     [khv_t, vtm], [pp[h // 2]], signal=(h % 2 == 1))
            for h in range(4):
                et, ea = ecol(h)
                S.op("dve", lambda e, h=h, ea=ea: e.scalar_tensor_tensor(
                    Sfp[:, h, :], Sfp[:, h, :], ea, pp[h // 2][:, (h % 2) * 256:(h % 2 + 1) * 256], ALU.mult, ALU.add),
                    [Sfp, et, pp[h // 2]], [Sfp])
            if Sbf_dst is not None:
                S.op("act", lambda e: e.activation(Sbf_dst[:], Sfp[:], AF.Copy), [Sfp], [Sbf_dst])

        if CARRY == "allgather":
            for t in range(NT):
                S.dma_load("sp", xt, xt[:], xp[t * N:(t + 1) * N, :].rearrange("(g p) d -> p g d", p=128))
                for g in range(4):
                    norm_to_hT((xt, xt[:, g, :]), 128, g * 128, 1, [(0, 128, 0)])
                st, sv = slab("w_in", w_in, O_K, 512)
                for g in range(4):
                    p = nps()
                    mm_tm(p, 128, g * 128, st, sv, 0, 512)
                    S.op("act", lambda e, p=p, g=g: e.activation(ktm[:, g, :], p[:], AF.Copy), [p], [ktm])
                for hf in range(2):
                    st, sv = slab("w_in", w_in, O_V + hf * 512, 512)
                    for g in range(4):
                        p = nps()
                        mm_tm(p, 128, g * 128, st, sv, 0, 512)
                        S.op("act", lambda e, p=p, g=g, hf=hf: e.activation(vtm[:, g, hf * 512:(hf + 1) * 512], p[:], AF.Copy),
                             [p], [vtm])
                st, sv = slab("w_in", w_in, O_ALR, 16)
                p = nps()
                mm_fm(p, 16, st, sv, 0, N)
                S.op("act", lambda e, p=p: e.activation(alrT[0:16, :], p[0:16, :], AF.Copy), [p], [alrT])
                for g in range(4):
                    gla_decay(128, g * 128, CUM, REM, False)
                    pl = nps()
                    for h in range(4):
                        S.op("pe", lambda e, h=h, pl=pl: e.matmul(pl[:, h:h + 1], apt[:, h * 128:(h + 1) * 128], ones_f[:, 0:1],
                                                                  start=True, stop=True), [apt, ones_f], [pl], signal=(h == 3))
                    S.op("act", lambda e, pl=pl: e.activation(ebl[:, 0:4], pl[:, 0:4], AF.Exp, scale=-1.0 / 16), [pl], [ebl])
                    S.op("dve", lambda e, pl=pl: e.tensor_tensor(Lsum[:], Lsum[:], pl[:, 0:4], ALU.add), [Lsum, pl], [Lsum])
                    state_update(128, g, khat, khat, St, lambda h: (ebl, ebl[:, h:h + 1]), None)
                fence()
            S.op("act", lambda e: e.activation(gbuf[:, 0:1024], St[:].rearrange("p a b -> p (a b)"), AF.Copy), [St], [gbuf])
            S.op("act", lambda e: e.activation(gbuf[:, 1024:1028], Lsum[:], AF.Copy), [Lsum], [gbuf])
            S.dma_store("sp", gbuf, gin, gbuf[:])
            ssem = S.sems[gbuf.ssem]
            sval = 16 * gbuf.scnt
            ccs = S.sems[S._sem("CC")]

            def cc(eng):
                eng.wait_ge(ssem, sval)
                eng.collective_compute("AllGather", ALU.bypass, replica_groups=[list(range(NCORES))],
                                       ins=[gin], outs=[gout]).then_inc(ccs, 1)
                eng.wait_ge(ccs, 1)
            S.raw("pool", cc)
            S.op("dve", lambda e: e.memset(St[:], 0.0), [], [St])
            S.op("dve", lambda e: e.memset(Tacc[:], 0.0), [], [Tacc])
            for j in range(NCORES):
                gj = gb2[j % 2]
                gj.w["CC"] = 1
                S.dma_load("sp", gj, gj[:], gout[j * 128:(j + 1) * 128, :])
                S.op("dve", lambda e, j=j: e.scalar_tensor_tensor(St[:], Tacc[:], flg[:, 1 + j:2 + j], St[:], ALU.mult, ALU.add),
                     [Tacc, flg, St], [St])
                S.op("act", lambda e, gj=gj: e.activation(Dj[:, 0:4], gj[:, 1024:1028], AF.Exp, scale=-1.0 / 16), [gj], [Dj])
                for h in range(4):
                    S.op("dve", lambda e, h=h, gj=gj: e.scalar_tensor_tensor(
                        Tacc[:, h, :], Tacc[:, h, :], Dj[:, h:h + 1], gj[:, h * 256:(h + 1) * 256], ALU.mult, ALU.add),
                        [Tacc, Dj, gj], [Tacc])
            S.op("act", lambda e: e.activation(Sb[0][:], St[:], AF.Copy), [St], [Sb[0]])
            fence()

        if CARRY == "prefix":
            PW = [0]

            def pcarve(name, words, dt, shape=None):
                t, o = carve(name, words, dt, shape, at=PW[0])
                PW[0] += words
                return t
            xg = [pcarve("xg%d" % i, 1024, F32) for i in range(4)]
            xnp = [xn, pcarve("xnp1", 512, BF16)]
            jks = [pcarve("jk%d" % i, 512, BF16) for i in range(2)]
            hTp = [hT, pcarve("hTp1", 2048, BF16, (8, N))]
            ktmp = [pcarve("ktmp%d" % i, 2048, F32, (4, 512)) for i in range(2)]
            vtmp = [pcarve("vtmp%d" % i, 2048, BF16, (4, 1024)) for i in range(2)]
            alrp = [pcarve("alrp%d" % i, 512, F32) for i in range(2)]
            aptp = [pcarve("aptp%d" % i, 512, F32) for i in range(4)]
            E3p = [pcarve("E3p%d" % i, 512, F32) for i in range(4)]
            khp = [pcarve("khp%d" % i, 256, BF16) for i in range(4)]
            eblp = [pcarve("eblp%d" % i, 4, F32) for i in range(4)]
            statp = [pcarve("statp%d" % i, 12, F32) for i in range(2)]
            assert PW[0] <= AR_WORDS, PW[0]
            fence()
            for a_ in alrp:
                S.op("dve", lambda e, a_=a_: e.memset(a_[0:32, :], 1.0), [], [a_])

            def A1(i):
                sp_ = statp[i % 2]
                for g in range(4):
                    S.dma_load("sp", xg[g], xg[g][:], xpre[i * N + g * 128:i * N + (g + 1) * 128, :])
                for g in range(4):
                    S.op("act", lambda e, g=g: e.activation(jks[g % 2][:], xg[g][:], AF.Square, accum_out=sp_[:, g:g + 1]),
                         [xg[g]], [jks[g % 2], (sp_, g)])
                S.op("act", lambda e: e.activation(sp_[:, 4:8], sp_[:, 0:4], AF.Ln, bias=EPS, scale=1.0 / D), [sp_], [sp_])
                S.op("act", lambda e: e.activation(sp_[:, 8:12], sp_[:, 4:8], AF.Exp, scale=-0.5), [sp_], [sp_])

            def A2a(i, g):
                sp_ = statp[i % 2]
                xn_ = xnp[g % 2]
                S.op("act", lambda e: e.activation(xn_[:], xg[g][:], AF.Copy, scale=sp_[:, 8 + g:9 + g]), [xg[g], sp_], [xn_])

            def A2b(i, g):
                xn_ = xnp[g % 2]
                hd = hTp[i % 2]
                p = npsb()
                for k in range(8):
                    S.op("pe", lambda e, k=k: e.transpose(p[:, k * 128:(k + 1) * 128], xn_[:, k * 128:(k + 1) * 128], identb[:]),
                         [xn_, identb], [p], signal=(k == 7))
                for k in range(8):
                    S.op("dve", lambda e, k=k: e.tensor_scalar(hd[:, k, g * 128:(g + 1) * 128], p[:, k * 128:(k + 1) * 128],
                                                               AB[:, 0, k, 0:1], AB[:, 1, k, 0:1], ALU.mult, ALU.add), [p, AB], [(hd, (k, g))])

            def B_pieces(i):
                hT_, ktm_, vtm_, alr_ = hTp[i % 2], ktmp[i % 2], vtmp[i % 2], alrp[i % 2]

                def Bk():
                    st, sv = slab("w_in", w_in, O_K, 512)
                    for g in range(4):
                        p = nps()
                        mm_tm(p, 128, g * 128, st, sv, 0, 512, lhs_t=hT_)
                        S.op("act", lambda e, p=p, g=g: e.activation(ktm_[:, g, :], p[:], AF.Copy), [p], [(ktm_, g)])

                def Bv(hf):
                    def f():
                        st, sv = slab("w_in", w_in, O_V + hf * 512, 512)
                        for g in range(4):
                            p = nps()
                            mm_tm(p, 128, g * 128, st, sv, 0, 512, lhs_t=hT_)
                            if g % 2 == 0:
                                S.op("act", lambda e, p=p, g=g: e.activation(vtm_[:, g, hf * 512:(hf + 1) * 512], p[:], AF.Copy,
                                                                             scale=flg[:, 1 + i // NT:2 + i // NT]),
                                     [p, flg], [(vtm_, (g, hf))])
                            else:
                                S.op("dve", lambda e, p=p, g=g: e.tensor_scalar(vtm_[:, g, hf * 512:(hf + 1) * 512], p[:],
                                                                                flg[:, 1 + i // NT:2 + i // NT], None, ALU.mult),
                                     [p, flg], [(vtm_, (g, hf))])
                    return f

                def Balr():
                    st, sv = slab("w_in", w_in, O_ALR, 16)
                    p = nps()
                    mm_fm(p, 16, st, sv, 0, N, rhs_t=hT_)
                    S.op("act", lambda e, p=p: e.activation(alr_[0:16, :], p[0:16, :], AF.Copy), [p], [alr_])
                return [Bk, Bv(0), Bv(1), Balr]

            def C_stages(i):
                ktm_, vtm_, alr_ = ktmp[i % 2], vtmp[i % 2], alrp[i % 2]
                j = i // NT
                pz = [None] * 4
                pr = [None] * 4
                pl = [None] * 4

                def Z():
                    for g in range(4):
                        pz[g] = nps()
                        S.op("pe", lambda e, g=g: e.matmul(pz[g][:, :], alr_[0:17, g * 128:(g + 1) * 128], walpha[0:17, :],
                                                           start=True, stop=True), [alr_, walpha], [pz[g]])

                def ACT1():
                    for g in range(4):
                        S.op("act", lambda e, g=g: e.activation(aptp[g][:], pz[g][:], AF.Exp, scale=-1.0), [pz[g]], [aptp[g]])
                        S.op("act", lambda e, g=g: e.activation(aptp[g][:], aptp[g][:], AF.Ln, bias=1.0), [aptp[g]], [aptp[g]])

                def R():
                    for g in range(4):
                        pr[g] = nps()
                        S.op("pe", lambda e, g=g: e.matmul(pr[g][:, :], REM, aptp[g][:], start=True, stop=True), [c128, aptp[g]], [pr[g]])
                    pl[0] = nps()
                    for g in range(4):
                        for h in range(4):
                            S.op("pe", lambda e, g=g, h=h: e.matmul(pl[0][:, g * 4 + h:g * 4 + h + 1], aptp[g][:, h * 128:(h + 1) * 128],
                                                                    ones_f[:, 0:1], start=True, stop=True), [aptp[g], ones_f], [pl[0]],
                                 signal=(h == 3))

                def ACT2():
                    for g in range(4):
                        S.op("act", lambda e, g=g: e.activation(E3p[g][:], pr[g][:], AF.Exp, scale=-1.0 / 16), [pr[g]], [E3p[g]])
                        S.op("act", lambda e, g=g: e.activation(eblp[g][:, 0:4], pl[0][:, g * 4:g * 4 + 4], AF.Exp, scale=-1.0 / 16),
                             [pl[0]], [eblp[g]])

                def KH():
                    for g in range(4):
                        S.op("pool", lambda e, g=g: e.tensor_tensor(khp[g][:], ktm_[:, g, :], E3p[g][:], ALU.mult),
                             [ktm_, E3p[g]], [khp[g]])

                def ST(gs):
                    def f():
                        for g in gs:
                            state_update(128, g, khp[g], khp[g], St, lambda h, g=g: (eblp[g], eblp[g][:, h:h + 1]), None, vtm=vtm_)
                    return f
                return [Z, ACT1, R, ACT2, KH, ST((0, 1)), ST((2, 3))]

            A1(0)
            for g in range(4):
                A2a(0, g)
                A2b(0, g)
            A1(1)
            for f in B_pieces(0):
                f()
            for i in range(NPRE):
                cs = C_stages(i)
                nxt = i + 1 < NPRE
                bp = B_pieces(i + 1) if nxt else [lambda: None] * 4
                cs[0]()
                if nxt:
                    A2a(i + 1, 0)
                cs[1]()
                if nxt:
                    A2b(i + 1, 0)
                    A2a(i + 1, 1)
                cs[2]()
                if nxt:
                    A2b(i + 1, 1)
                    A2a(i + 1, 2)
                cs[3]()
                if nxt:
                    A2b(i + 1, 2)
                    A2a(i + 1, 3)
                cs[4]()
                if nxt:
                    A2b(i + 1, 3)
                if i + 2 < NPRE:
                    A1(i + 2)
                bp[0]()
                cs[5]()
                bp[1]()
                cs[6]()
                bp[2]()
                bp[3]()
            S.op("act", lambda e: e.activation(Sb[0][:], St[:], AF.Copy), [St], [Sb[0]])
            fence()
            S.op("dve", lambda e: e.memset(alrT[0:32, :], 1.0), [], [alrT])
        else:
            S.op("dve", lambda e: e.memset(alrT[0:32, :], 1.0), [], [alrT])

        sbi = [0]

        def make_pre(kind, t):
            isp = kind == "p"
            G_ = 128 if isp else 64
            NG_ = 4 if isp else 1
            segs_ = [(0, 128, 0)] if isp else [(16 * s_, 16 * s_ + 16, 1 + s_) for s_ in range(SPC)]

            def load():
                if isp:
                    S.dma_load("sp", xnx, xnx[:], xp[t * N:(t + 1) * N, :].rearrange("(g p) d -> p g d", p=128))
                else:
                    S.dma_load("sp", xnx, xnx[0:64, 0, :], xs)

            def stats():
                for g in range(NG_):
                    S.op("act", lambda e, g=g: e.activation(jkm[g % 2][0:G_, :], xnx[0:G_, g, :], AF.Square,
                                                            accum_out=stat2[0:G_, g:g + 1]), [xnx], [jkm[g % 2], (stat2, g)])
                S.op("act", lambda e: e.activation(stat2[0:G_, 4:4 + NG_], stat2[0:G_, 0:NG_], AF.Ln, bias=EPS, scale=1.0 / D),
                     [stat2], [stat2])
                S.op("act", lambda e: e.activation(stat2[0:G_, 8:8 + NG_], stat2[0:G_, 4:4 + NG_], AF.Exp, scale=-0.5),
                     [stat2], [stat2])

            def a2a(g):
                xn_ = (xn, xn2)[g % 2]
                S.op("act", lambda e: e.activation(xn_[0:G_, :], xnx[0:G_, g, :], AF.Copy, scale=stat2[0:G_, 8 + g:9 + g]),
                     [xnx, stat2], [xn_])

            def a2b(g):
                xn_ = (xn, xn2)[g % 2]
                p = npsb()
                for k in range(8):
                    S.op("pe", lambda e, k=k: e.transpose(p[:, k * 128:k * 128 + G_], xn_[0:G_, k * 128:(k + 1) * 128],
                                                          identb[0:G_, 0:G_]), [xn_, identb], [p], signal=(k == 7))
                for k in range(8):
                    for (c0, c1, r) in segs_:
                        S.op("dve", lambda e, k=k, c0=c0, c1=c1, r=r: e.tensor_scalar(
                            hT[:, k, g * 128 + c0:g * 128 + c1], p[:, k * 128 + c0:k * 128 + c1],
                            AB[:, 0, k, r:r + 1], AB[:, 1, k, r:r + 1], ALU.mult, ALU.add), [p, AB], [(hT, (k, g * 128 + c0))])

            def copy():
                S.dma_load("sp", xt, xt[0:G_, 0:NG_, :], xnx[0:G_, 0:NG_, :], reads=[xnx])
            return dict(load=load, stats=stats, a2a=a2a, a2b=a2b, copy=copy, ng=NG_)

        def layer_tile(kind, t, pre_done=False, next_tile=None):
            is_p = kind == "p"
            NN = N if is_p else SPC * LS
            G = 128 if is_p else 64
            NGR = 4 if is_p else 1
            first_tile = is_p and t == 0
            last_tile = is_p and t == NT - 1
            segs = [(0, 128, 0)] if is_p else [(16 * s, 16 * s + 16, 1 + s) for s in range(SPC)]
            W = 16 + (N if is_p else LS)
            nseq = 1 if is_p else SPC
            U = uT.h[:, 0:4 * nseq * W].rearrange("p (g s w) -> p g s w", s=nseq, w=W)
            if not pre_done:
                if is_p:
                    S.dma_load("sp", xt, xt[:], xp[t * N:(t + 1) * N, :].rearrange("(g p) d -> p g d", p=128))
                else:
                    S.dma_load("sp", xt, xt[0:64, 0, :], xs)
                for g in range(NGR):
                    norm_to_hT((xt, xt[0:G, g, :]), G, g * 128, 1, segs)
            if first_tile:
                norm_to_hT((xht, xht[:]), 16, 0, 1, [(0, 16, 0)], hdst=hTh)
            st, sv = slab("w_in", w_in, O_U, 512)
            for m in range(4):
                p = nps()
                mm_fm(p, 128, st, sv, m * 128, NN)
                S.op("act", lambda e, p=p, m=m: e.activation(U[:, m, :, 16:W], p[:, 0:NN].rearrange("p (s l) -> p s l", s=nseq),
                                                             AF.Copy), [p], [(uT, m)])
            if first_tile:
                p = nps()
                for m in range(4):
                    for k in range(8):
                        S.op("pe", lambda e, m=m, k=k, p=p, sv=sv: e.matmul(p[:, m * 16:(m + 1) * 16], sv[:, k, m * 128:(m + 1) * 128],
                                                                      hTh[:, k, :], start=(k == 0), stop=(k == 7)),
                             [st, hTh], [p], signal=(k == 7))
                S.op("dve", lambda e, p=p: e.tensor_scalar(U[:, :, 0, 0:16], p[:, 0:64].rearrange("p (g w) -> p g w", w=16),
                                                           flg[:, 0:1], None, ALU.mult), [p, flg], [uT])
            if not is_p:
                S.dma_load("sp", cach, cach[0:60, :], cache)
                p = nps()
                for gi in range(4):
                    S.op("pe", lambda e, gi=gi, p=p: e.transpose(p[:, gi * 60:(gi + 1) * 60], cach[0:60, gi * 128:(gi + 1) * 128],
                                                                  identf[0:60, 0:60]), [cach, c128], [p], signal=(gi == 3))
                S.op("dve", lambda e, p=p: e.tensor_copy(U[:, :, :, 1:16], p[:, 0:240].rearrange("p (g s w) -> p g s w", s=4, w=15)),
                     [p], [uT])
            if (not is_p) or last_tile:
                p = nps()
                gl = 0 if not is_p else 3 * 128
                mm_tm(p, G, gl, st, sv, 0, 512)
                S.op("act", lambda e, p=p: e.activation(utm[0:G, :], p[0:G, :], AF.Copy), [p], [utm])
                if is_p:
                    S.dma_store("sp", utm, cp_o, utm[113:128, :])
                else:
                    for s in range(SPC):
                        S.dma_store("sp", utm, cs_o[s * 15:(s + 1) * 15, :], utm[16 * s + 1:16 * s + 16, :])
            st, sv = slab("w_in", w_in, O_Q, 512)
            for m in range(4):
                p = nps()
                mm_fm(p, 128, st, sv, m * 128, NN)
                S.op("act", lambda e, p=p, m=m: e.activation(qT[:, m, 0:NN], p[:, 0:NN], AF.Copy), [p], [(qT, m)])
            st, sv = slab("w_in", w_in, O_K, 512)
            for m in range(4):
                p = nps()
                mm_fm(p, 128, st, sv, m * 128, NN)
                S.op("act", lambda e, p=p, m=m: e.activation(kT[:, m, 0:NN], p[:, 0:NN], AF.Copy), [p], [(kT, m)])
            for g in range(NGR):
                p = nps()
                mm_tm(p, G, g * 128, st, sv, 0, 512)
                S.op("act", lambda e, p=p, g=g: e.activation(ktm[0:G, g, :], p[0:G, :], AF.Copy), [p], [(ktm, g)])
            for hf in range(2):
                st, sv = slab("w_in", w_in, O_V + hf * 512, 512)
                for g in range(NGR):
                    p = nps()
                    mm_tm(p, G, g * 128, st, sv, 0, 512)
                    S.op("act", lambda e, p=p, g=g, hf=hf: e.activation(vtm[0:G, g, hf * 512:(hf + 1) * 512], p[0:G, :], AF.Copy),
                         [p], [(vtm, (g, hf))])
            for hf in range(2):
                st, sv = slab("w_in", w_in, O_G + hf * 512, 512)
                for m in range(4):
                    p = nps()
                    mm_fm(p, 128, st, sv, m * 128, NN)
                    S.op("act", lambda e, p=p, m=m, hf=hf: e.activation(sgT[:, hf * 4 + m, 0:NN], p[:, 0:NN], AF.Silu), [p], [(sgT, hf * 4 + m)])
            st, sv = slab("w_in", w_in, O_ALR, 16)
            p = nps()
            mm_fm(p, 16, st, sv, 0, NN)
            S.op("dve", lambda e: e.memset(alrT[0:32, :], 1.0), [], [alrT])
            S.op("act", lambda e, p=p: e.activation(alrT[0:16, 0:NN], p[0:16, 0:NN], AF.Copy), [p], [alrT])

            pool_branch(nseq, N if is_p else LS, NN, first_tile)
            if is_p:
                S.op("dve", lambda e: e.tensor_copy(U[:, :, 0, 0:16], U[:, :, 0, N:N + 16]), [uT], [uT])
            for g in range(NGR):
                if is_p:
                    gla_decay(128, g * 128, CUM, REM, True)
                    cur = Sb[sbi[0] % 2]
                    nxt = Sb[(sbi[0] + 1) % 2]
                    sbi[0] += 1
                    gla_out(128, g * 128, g, CUM4[:], [(0, 128, cur)], None,
                            mid_hook=lambda g=g, nxt=nxt: state_update(128, g, khat, khat, St, lambda h: (E1, E1[:, h, 127:128]), nxt))
                else:
                    gla_decay(64, 0, CUMs, REMs, True)
                    for s in range(SPC):
                        S.dma_load("sp", St, St[:], s0[s].rearrange("h c v -> c h v"))
                        S.op("act", lambda e, s=s: e.activation(Sb[s][:], St[:], AF.Copy), [St], [Sb[s]])
                    gla_out(64, 0, 0, CUM4s[:], [(16 * s, 16 * s + 16, Sb[s]) for s in range(SPC)], None)
                    for s in range(SPC):
                        S.dma_load("sp", St, St[:], s0[s].rearrange("h c v -> c h v"))
                        S.op("dve", lambda e, s=s: e.tensor_scalar(khs[0:64, :], khat[0:64, :], SEGM[:, s:s + 1], None, ALU.mult),
                             [khat, c64], [khs])
                        state_update(64, 0, khs, khs, St, lambda h, s=s: (E1, E1[:, h, 16 * s + 15:16 * s + 16]), None)
                        S.dma_store("sp", St, ss_o[s].rearrange("h c v -> c h v"), St[:])
            if last_tile:
                S.dma_store("sp", St, sp_o.rearrange("h c v -> c h v"), St[:])
                out_tiles.append(St)
            fence()

            st, sv = slab("w_pa", w_pa, 0, D, K=4)
            sga = []
            for hf in range(2):
                stg, svg = slab("w_in", w_in, O_GA + hf * 512, 512)
                for m in range(4):
                    mm = hf * 4 + m
                    pg = nps()
                    mm_fm(pg, 128, stg, svg, m * 128, NN)
                    r = rl[mm % 2]
                    S.op("act", lambda e, pg=pg, r=r: e.activation(r[:, 0:NN], pg[:, 0:NN], AF.Sigmoid), [pg], [r])
                    pa = nps()
                    mm_fm(pa, 128, st, sv, mm * 128, NN, rhs_t=aoT, K=4)
                    S.op("dve", lambda e, pa=pa, r=r, mm=mm: e.tensor_tensor(mgA[:, mm, 0:NN], pa[:, 0:NN], r[:, 0:NN], ALU.mult),
                         [pa, r], [(mgA, mm)])
            wpb = [slab("w_pb", w_pb, hf * 512, 512) for hf in range(2)]
            for hf in range(2):
                stg, svg = slab("w_in", w_in, O_GB + hf * 512, 512)
                st, sv = wpb[hf]
                for m in range(4):
                    mm = hf * 4 + m
                    pg = nps()
                    mm_fm(pg, 128, stg, svg, m * 128, NN)
                    r = rl[mm % 2]
                    S.op("act", lambda e, pg=pg, r=r: e.activation(r[:, 0:NN], pg[:, 0:NN], AF.Sigmoid), [pg], [r])
                    pb = nps()
                    mm_fm(pb, 128, st, sv, m * 128, NN, rhs_t=boT)
                    r2 = rt[mm % 2]
                    S.op("dve", lambda e, pb=pb, r=r, r2=r2: e.tensor_tensor(r2[:, 0:NN], pb[:, 0:NN], r[:, 0:NN], ALU.mult),
                         [pb, r], [r2])
                    if MODE == "pool_only":
                        S.op("dve", lambda e, mm=mm: e.tensor_copy(mrg[:, mm, 0:NN], mgA[:, mm, 0:NN]), [mgA], [mrg])
                    elif MODE == "gla_only":
                        S.op("dve", lambda e, r2=r2, mm=mm: e.tensor_copy(mrg[:, mm, 0:NN], r2[:, 0:NN]), [r2], [mrg])
                    else:
                        S.op("dve", lambda e, r2=r2, mm=mm: e.tensor_tensor(mrg[:, mm, 0:NN], r2[:, 0:NN], mgA[:, mm, 0:NN], ALU.add),
                             [r2, mgA], [mrg])
            for hf in range(2):
                st, sv = slab("w_out", w_out, hf * 512, 512)
                for g in range(NGR):
                    p = nps()
                    mm_tm(p, G, g * 128, st, sv, 0, 512, lhs_t=mrg)
                    r2 = rt[(hf * NGR + g) % 2]
                    S.op("dve", lambda e, p=p, r2=r2, hf=hf: e.tensor_tensor(r2[0:G, :], p[0:G, :], G1[0:G, hf * 512:(hf + 1) * 512],
                                                                             ALU.mult), [p, G1], [r2])
                    if "nomix" not in MODE:
                        S.op("dve", lambda e, r2=r2, g=g, hf=hf: e.tensor_tensor(xt[0:G, g, hf * 512:(hf + 1) * 512],
                                                                                 xt[0:G, g, hf * 512:(hf + 1) * 512], r2[0:G, :], ALU.add),
                             [xt, r2], [xt])
            fence()

            pre = make_pre(*next_tile) if next_tile is not None else None
            if pre:
                pre["load"]()
            for g in range(NGR):
                norm_to_hT((xt, xt[0:G, g, :]), G, g * 128, 2, segs)
            if DEBUG and is_p and t == 0:
                dump(S, "h2T", hT, hT[:, :, 0:128], 128, 1024, inner=128)
            for sl in range(8):
                st, sv = slab("w_ff1", w_ff1, sl * 512, 512)
                for m in range(4):
                    p = nps()
                    mm_fm(p, 128, st, sv, m * 128, NN)
                    r = rl[m % 2]
                    S.op("act", lambda e, p=p, r=r: e.activation(r[:, 0:NN], p[:, 0:NN], AF.Relu), [p], [r])
                    S.op("dve", lambda e, r=r, sl=sl, m=m: e.tensor_tensor(ffT[:, sl * 4 + m, 0:NN], r[:, 0:NN], r[:, 0:NN], ALU.mult),
                         [r], [(ffT, sl * 4 + m)])
            if DEBUG and is_p and t == 0:
                dump(S, "ffT", ffT, ffT[:, 0:8, 0:128], 128, 1024, inner=128)
            if pre:
                pre["stats"]()
                pre["a2a"](0)
            pstep = [0]

            def pre_step():
                if not pre or pstep[0] >= pre["ng"]:
                    return
                g_ = pstep[0]
                pstep[0] += 1
                pre["a2b"](g_)
                if g_ + 1 < pre["ng"]:
                    pre["a2a"](g_ + 1)
            for hf in range(2):
                acc = [nps() for _ in range(NGR)]
                for kp in range(4):
                    st, sv = slab("w_ff2", w_ff2, hf * 512, 512, 8, kp * 1024)
                    for g in range(NGR):
                        mm_tm(acc[g], G, g * 128, st, sv, 0, 512, lhs_t=ffT, K=8, k0=kp * 8, first=(kp == 0), last=(kp == 3))
                    if kp % 2 == 1:
                        pre_step()
                for g in range(NGR):
                    r2 = rt[g % 2]
                    S.op("dve", lambda e, g=g, r2=r2, hf=hf, acc=acc: e.tensor_tensor(r2[0:G, :], acc[g][0:G, :],
                                                                             G2[0:G, hf * 512:(hf + 1) * 512], ALU.mult),
                         [acc[g], G2], [r2])
                    if "noffn" not in MODE:
                        S.op("dve", lambda e, r2=r2, g=g, hf=hf: e.tensor_tensor(xt[0:G, g, hf * 512:(hf + 1) * 512],
                                                                                 xt[0:G, g, hf * 512:(hf + 1) * 512], r2[0:G, :], ALU.add),
                             [xt, r2], [xt])
            for g in range(NGR):
                S.op("act", lambda e, g=g: e.activation(jkm[g % 2][0:G, :], xt[0:G, g, :], AF.Square, accum_out=stat[0:G, 4 + g:5 + g]),
                     [xt], [jkm[g % 2], (stat, g)])
            S.op("act", lambda e: e.activation(stat[0:G, 8:8 + NGR], stat[0:G, 4:4 + NGR], AF.Ln, bias=EPS, scale=1.0 / D), [stat], [stat])
            S.op("act", lambda e: e.activation(stat[0:G, 12:12 + NGR], stat[0:G, 8:8 + NGR], AF.Exp, scale=-0.5), [stat], [stat])
            for g in range(NGR):
                yo_ = (yo, yo2)[g % 2]
                S.op("dve", lambda e, g=g, yo_=yo_: e.scalar_tensor_tensor(yo_[0:G, :], xt[0:G, g, :], stat[0:G, 12 + g:13 + g], FG[0:G, :],
                                                                           ALU.mult, ALU.mult), [xt, stat, FG], [yo_])
                if is_p:
                    S.dma_store("sp", yo_, yp[t * N + g * 128:t * N + (g + 1) * 128, :], yo_[:])
                else:
                    S.dma_store("sp", yo_, ys, yo_[0:64, :])
            if pre:
                while pstep[0] < pre["ng"]:
                    pre_step()
                pre["copy"]()
            fence()

        tiles = [("p", t) for t in range(NT)] + [("s", 0)]
        for ti, (kind, t) in enumerate(tiles):
            if kind == "s":
                build_gates(selt.h[0:5, 128:192], 64)
            layer_tile(kind, t, pre_done=(ti > 0), next_tile=(tiles[ti + 1] if ti + 1 < len(tiles) else None))

        if DEBUG:
            dump(S, "gtm1", gtm1, gtm1[:], 5, D)
            dump(S, "selt", selt, selt[:], 5, 192)
            dump(S, "G1", G1, G1[:], 128, D)
            dump(S, "G2", G2, G2[:], 128, D)
            dump(S, "AB", AB, AB[:].rearrange("p a b c -> p (a b c)"), 128, 160)
            dump(S, "vecs", vecs, vecs[:], 128, 24)
            out_tiles.append(dbg_stage[0])
        out_tiles += [yo, yo2, utm, St]
        S.finish(out_tiles)
        S.emit_all()
        print("ops", S.nops, "waits", S.nwaits, "sems", len(S.sems))
    return nc


def _consts():
    ident = np.eye(128, dtype=np.float32)
    cum = np.triu(np.ones((128, 128), np.float32))
    rem = np.tril(np.ones((128, 128), np.float32), -1)
    cst128 = np.concatenate([ident, cum, rem], axis=1)
    seg = np.arange(64) // 16
    same = (seg[:, None] == seg[None, :]).astype(np.float32)
    cums = np.triu(np.ones((64, 64), np.float32)) * same
    rems = np.tril(np.ones((64, 64), np.float32), -1) * same
    segm = (seg[:, None] == np.arange(4)[None, :]).astype(np.float32)
    cst64 = np.concatenate([cums, rems, segm], axis=1)
    sel = np.zeros((5, 192), np.float32)
    sel[0, 0:128] = 1.0
    for s in range(4):
        sel[1 + s, 128 + 16 * s:128 + 16 * (s + 1)] = 1.0
    return cst128, cst64, sel


_NC_CACHE = {}


def kernel(x_prompt, x_sample, c_prompt, c_sample, state_gla, cache_pool, w_ada, b_ada,
           norm1_g, w_in, w_alpha, b_alpha, w_pool, pool_scale, gla_norm_g, w_pa, w_pb,
           w_out, norm2_g, w_ff1, w_ff2, final_g):
    f = lambda a: np.ascontiguousarray(np.asarray(a, dtype=np.float32))
    x_prompt, x_sample, c_prompt, c_sample = f(x_prompt), f(x_sample), f(c_prompt), f(c_sample)
    state_gla, cache_pool = f(state_gla), f(cache_pool)
    cst128, cst64, sel = _consts()
    shared = dict(
        w_ada=f(w_ada)[0], b_ada=f(b_ada), norm1_g=f(norm1_g), w_in=f(w_in)[0], w_alpha=f(w_alpha)[0],
        b_alpha=f(b_alpha), w_pool=f(w_pool)[0], pool_scale=f(pool_scale), gla_norm_g=f(gla_norm_g),
        w_pa=f(w_pa)[0], w_pb=f(w_pb)[0], w_out=f(w_out)[0], norm2_g=f(norm2_g), w_ff1=f(w_ff1)[0],
        w_ff2=f(w_ff2)[0], final_g=f(final_g).reshape(1, D), cst128=cst128, cst64=cst64, sel=sel)
    in_maps = []
    for c in range(NCORES):
        t0 = c * TPC
        xh = np.zeros((16, D), np.float32)
        if c > 0:
            xh[:] = x_prompt[0, t0 - 16:t0]
        invc = np.zeros((128, 64), np.float32)
        for gi in range(4):
            w = 2 << gi
            pos = t0 + np.arange(16)
            invc[:, gi * 16:(gi + 1) * 16] = (1.0 / np.minimum(pos + 1, w))[None, :]
        flags = np.zeros((128, 16), np.float32)
        flags[:, 0] = 0.0 if c == 0 else 1.0
        xpre = None
        if CARRY == "prefix":
            xpre = np.zeros((7 * TPC, D), np.float32)
            for j in range(7):
                b = c - 7 + j
                if b >= 0:
                    xpre[j * TPC:(j + 1) * TPC] = x_prompt[0, b * TPC:(b + 1) * TPC]
                    flags[:, 1 + j] = 1.0
        else:
            flags[:, 1 + c] = 1.0
        m = dict(shared)
        m.update(
            xp=x_prompt[0, t0:t0 + TPC], xh=xh, xs=x_sample[c * SPC:(c + 1) * SPC].reshape(SPC * LS, D),
            cvec=np.concatenate([c_prompt, c_sample[c * SPC:(c + 1) * SPC]], axis=0),
            s0=state_gla[0, c * SPC:(c + 1) * SPC], cache=cache_pool[0, c * SPC:(c + 1) * SPC].reshape(SPC * 15, 512),
            invcnt=invc, flags=flags)
        if xpre is not None:
            m["xpre"] = xpre
        in_maps.append({k: np.ascontiguousarray(v) for k, v in m.items()})
    if "nc" not in _NC_CACHE:
        _NC_CACHE["nc"] = build()
    nc = _NC_CACHE["nc"]
    res = run_bass_kernel_spmd(nc, in_maps, core_ids=list(range(NCORES)))
    R = res.results
    _NC_CACHE['last'] = R
    y_prompt = np.concatenate([R[c]["yp"] for c in range(NCORES)], axis=0)[None]
    y_sample = np.concatenate([R[c]["ys"].reshape(SPC, LS, D) for c in range(NCORES)], axis=0)
    st_p = R[NCORES - 1]["sp_o"][None, None]
    ch_p = R[NCORES - 1]["cp_o"][None, None]
    st_s = np.concatenate([R[c]["ss_o"] for c in range(NCORES)], axis=0)[None]
    ch_s = np.concatenate([R[c]["cs_o"].reshape(SPC, 15, 512) for c in range(NCORES)], axis=0)[None]
    return (y_prompt.astype(np.float32), y_sample.astype(np.float32), st_p.astype(np.float32),
            ch_p.astype(np.float32), st_s.astype(np.float32), ch_s.astype(np.float32))
```

```python
import contextlib
import numpy as np
import concourse.bass as bass
import concourse.mybir as mybir
from concourse.bass_utils import run_bass_kernel_spmd

F32 = mybir.dt.float32
BF16 = mybir.dt.bfloat16
ALU = mybir.AluOpType
AF = mybir.ActivationFunctionType

NCORES = 8
D = 1024
SEQ = 16384
TPC = SEQ // NCORES
NT = 4
N = 512
SPC = 4
LS = 16
INW = 5648
EPS = 1e-6
O_U, O_Q, O_K, O_V, O_G, O_ALR, O_GA, O_GB = 0, 512, 1024, 1536, 2560, 3584, 3600, 4624

ENGS = ("pe", "act", "dve", "pool", "sp")
WITH_EXCHANGE = False
STRICT_SAME_ENGINE = True
CARRY = "prefix"
NPRE = 7 * NT
DEBUG = False
MODE = "full"


class TT:
    def __init__(self, name, h):
        self.name = name
        self.h = h
        self.w = {}
        self.r = {}
        self.kw = {}
        self.kr = {}
        self.dsem = None
        self.dcnt = 0
        self.ssem = None
        self.scnt = 0

    def __getitem__(self, idx):
        return self.h[idx]


class Sched:
    def __init__(self, nc, es):
        self.nc = nc
        self.es = es
        self.sems = {}
        self.cnt = {e: 0 for e in ENGS}
        self.seen = {e: {} for e in ENGS}
        self.prog = {e: [] for e in ENGS}
        self.nwaits = 0
        self.nops = 0
        for e in ENGS:
            self._sem("E_" + e)

    def _sem(self, name):
        if name not in self.sems:
            self.sems[name] = self.es.enter_context(self.nc.semaphore(name))
        return name

    def sb(self, name, shape, dt=F32):
        h = self.es.enter_context(self.nc.sbuf_tensor(name, list(shape), dt))
        return TT(name, h)

    def ps(self, name, shape, dt=F32):
        h = self.es.enter_context(self.nc.psum_tensor(name, list(shape), dt))
        return TT(name, h)

    @staticmethod
    def _split(lst):
        tts, keys = [], []
        for x in lst:
            if isinstance(x, tuple):
                tts.append(x[0])
                keys.append(x[1])
            else:
                tts.append(x)
                keys.append(None)
        return tts, keys

    def _waits(self, e, reads, writes, rkeys=None, wkeys=None):
        waits = {}
        own = "E_" + e
        rkeys = rkeys or [None] * len(reads)
        wkeys = wkeys or [None] * len(writes)

        def need(nm, v):
            if self.seen[e].get(nm, 0) >= v:
                return
            if waits.get(nm, 0) < v:
                waits[nm] = v

        same = e not in ("pe", "sp")
        for t in reads:
            for nm, v in t.w.items():
                if nm == own:
                    if same:
                        need(nm, v)
                else:
                    need(nm, v)
        for t, key in zip(writes, wkeys):
            for dct, kd in ((t.w, t.kw), (t.r, t.kr)):
                for nm, v in dct.items():
                    if nm == own:
                        if same and STRICT_SAME_ENGINE:
                            if key is None:
                                need(nm, v)
                            else:
                                vv = max(kd.get(key, {}).get(own, 0), kd.get(None, {}).get(own, 0))
                                if vv:
                                    need(nm, vv)
                    else:
                        need(nm, v)
        for nm, v in waits.items():
            self.seen[e][nm] = v
        return list(waits.items())

    def op(self, e, fn, reads=(), writes=(), signal=True):
        reads, rkeys = self._split(reads)
        writes, wkeys = self._split(writes)
        waits = self._waits(e, reads, writes, rkeys, wkeys)
        own = "E_" + e
        if signal:
            self.cnt[e] += 1
            val = self.cnt[e]
        else:
            val = self.cnt[e] + 1
        for t, key in zip(reads, rkeys):
            if t.r.get(own, 0) < val:
                t.r[own] = val
            t.kr.setdefault(key, {})[own] = val
        for t, key in zip(writes, wkeys):
            if t.w.get(own, 0) < val:
                t.w[own] = val
            t.kw.setdefault(key, {})[own] = val
        sems = self.sems
        self.nwaits += len(waits)
        self.nops += 1

        def emit(eng):
            for nm, v in waits:
                eng.wait_ge(sems[nm], v)
            ins = fn(eng)
            if signal:
                ins.then_inc(sems[own], 1)

        self.prog[e].append(emit)

    def dma_load(self, q, dst, dst_ap, src_ap, reads=(), **kw):
        if dst.dsem is None:
            dst.dsem = self._sem("L_" + dst.name)
        waits = self._waits(q, reads, [dst])
        dst.dcnt += 1
        val = 16 * dst.dcnt
        dst.w[dst.dsem] = val
        for t in reads:
            t.r[dst.dsem] = val
        sems = self.sems
        sem = sems[dst.dsem]
        self.nwaits += len(waits)

        def emit(eng):
            for nm, v in waits:
                eng.wait_ge(sems[nm], v)
            eng.dma_start(out=dst_ap, in_=src_ap, **kw).then_inc(sem, 16)

        self.prog[q].append(emit)

    def dma_store(self, q, src, dst_ap, src_ap, dram_tt=None, **kw):
        if src.ssem is None:
            src.ssem = self._sem("S_" + src.name)
        waits = self._waits(q, [src], [])
        src.scnt += 1
        val = 16 * src.scnt
        src.r[src.ssem] = val
        if dram_tt is not None:
            dram_tt.w[src.ssem] = val
        sems = self.sems
        sem = sems[src.ssem]
        self.nwaits += len(waits)

        def emit(eng):
            for nm, v in waits:
                eng.wait_ge(sems[nm], v)
            eng.dma_start(out=dst_ap, in_=src_ap, **kw).then_inc(sem, 16)

        self.prog[q].append(emit)

    def fence(self, tiles):
        allev = {}
        for t in tiles:
            for dct in (t.w, t.r):
                for nm, v in dct.items():
                    if allev.get(nm, 0) < v:
                        allev[nm] = v
        for t in tiles:
            kwn = t.kw.setdefault(None, {})
            krn = t.kr.setdefault(None, {})
            for nm, v in allev.items():
                if t.r.get(nm, 0) < v:
                    t.r[nm] = v
                if t.w.get(nm, 0) < v:
                    t.w[nm] = v
                if kwn.get(nm, 0) < v:
                    kwn[nm] = v
                if krn.get(nm, 0) < v:
                    krn[nm] = v

    def raw(self, e, fn):
        self.prog[e].append(fn)

    def finish(self, out_tiles):
        sems = self.sems
        waits = []
        for t in out_tiles:
            if t.ssem is not None:
                waits.append((t.ssem, 16 * t.scnt))
        for e in ENGS:
            if e != "sp" and self.cnt[e] > 0:
                waits.append(("E_" + e, self.cnt[e]))

        def emit(eng):
            for nm, v in waits:
                eng.wait_ge(sems[nm], v)

        self.prog["sp"].append(emit)

    def emit_all(self):
        nc = self.nc
        prog = self.prog
        with nc.Block() as block:

            @block.tensor
            def _(eng):
                for f in prog["pe"]:
                    f(eng)

            @block.scalar
            def _(eng):
                for f in prog["act"]:
                    f(eng)

            @block.vector
            def _(eng):
                for f in prog["dve"]:
                    f(eng)

            @block.gpsimd
            def _(eng):
                for f in prog["pool"]:
                    f(eng)

            @block.sync
            def _(eng):
                for f in prog["sp"]:
                    f(eng)


def build():
    nc = bass.Bass("TRN2", target_bir_lowering=False)

    def din(name, shape):
        return nc.dram_tensor(name, list(shape), F32, kind="ExternalInput").ap()

    def dout(name, shape):
        return nc.dram_tensor(name, list(shape), F32, kind="ExternalOutput").ap()

    xp = din("xp", [TPC, D])
    xpre = din("xpre", [7 * TPC, D]) if CARRY == "prefix" else None
    xh = din("xh", [16, D])
    xs = din("xs", [SPC * LS, D])
    cvec = din("cvec", [5, D])
    s0 = din("s0", [SPC, 4, 128, 256])
    cache = din("cache", [SPC * 15, 512])
    w_ada = din("w_ada", [D, 6 * D])
    b_ada = din("b_ada", [1, 6 * D])
    norm1_g = din("norm1_g", [1, D])
    w_in = din("w_in", [D, INW])
    w_alpha = din("w_alpha", [16, 512])
    b_alpha = din("b_alpha", [1, 512])
    w_pool = din("w_pool", [4, 128, 128])
    pool_scale = din("pool_scale", [1, 512])
    gla_norm_g = din("gla_norm_g", [1, 256])
    w_pa = din("w_pa", [512, D])
    w_pb = din("w_pb", [D, D])
    w_out = din("w_out", [D, D])
    norm2_g = din("norm2_g", [1, D])
    w_ff1 = din("w_ff1", [D, 4 * D])
    w_ff2 = din("w_ff2", [4 * D, D])
    final_g = din("final_g", [1, D])
    cst128 = din("cst128", [128, 384])
    cst64 = din("cst64", [64, 132])
    sel = din("sel", [5, 192])
    invcnt = din("invcnt", [128, 64])
    flags = din("flags", [128, 16])

    yp = dout("yp", [TPC, D])
    ys = dout("ys", [SPC * LS, D])
    sp_o = dout("sp_o", [4, 128, 256])
    cp_o = dout("cp_o", [15, 512])
    ss_o = dout("ss_o", [SPC, 4, 128, 256])
    cs_o = dout("cs_o", [SPC * 15, 512])
    gin = nc.dram_tensor("gin", [128, 1028], F32, kind="Internal").ap()
    gout = nc.dram_tensor("gout", [NCORES * 128, 1028], F32, kind="Internal").ap()
    wscr = nc.dram_tensor("wscr", [40, 128, 4096], BF16, kind="Internal").ap()

    dbg_outs = {}

    def dump(S_, name, tt, ap, rows, cols, inner=None):
        o = nc.dram_tensor("dbg_" + name, [rows, cols], F32, kind="ExternalOutput").ap()
        dbg_outs[name] = o
        stg = dbg_stage[0]
        sv_ = stg[0:rows, 0:cols]
        if inner is not None:
            sv_ = sv_.rearrange("p (a b) -> p a b", b=inner)
        S_.op("dve", lambda e: e.tensor_copy(sv_, ap), [tt], [stg])
        S_.dma_store("sp", stg, o, stg[0:rows, 0:cols])

    dbg_stage = [None]
    with contextlib.ExitStack() as es:
        S = Sched(nc, es)
        es.enter_context(nc.allow_non_contiguous_dma("tiny vector re-layouts"))
        out_tiles = []

        xt = S.sb("xt", [128, 4, D])
        xn = S.sb("xn", [128, D], BF16)
        hT = S.sb("hT", [128, 8, N], BF16)
        NSLOT = 4
        ring = [S.sb("ring%d" % i, [128, 4096], BF16) for i in range(NSLOT)]
        G1 = S.sb("G1", [128, D])
        G2 = S.sb("G2", [128, D])
        FG = S.sb("FG", [128, D])
        St = S.sb("St", [128, 4, 256])
        Sb = [S.sb("Sb%d" % i, [128, 4, 256], BF16) for i in range(4)]
        yo = S.sb("yo", [128, D])
        if DEBUG:
            dbg_stage[0] = S.sb("dbgstg", [128, D])
        c128 = S.sb("c128", [128, 384])
        identb = S.sb("identb", [128, 128], BF16)
        CUM4 = S.sb("CUM4", [128, 4, 128])
        c64 = S.sb("c64", [64, 132])
        CUM4s = S.sb("CUM4s", [64, 4, 64])
        ones_f = S.sb("ones_f", [128, 128])
        ones_b = S.sb("ones_b", [128, 128], BF16)
        walpha = S.sb("walpha", [17, 512])
        balpha = S.sb("balpha", [1, 512])
        wpool = S.sb("wpool", [128, 4, 128], BF16)
        vecs = S.sb("vecs", [128, 24])
        modT = S.sb("modT", [128, 6, 8, 5])
        AB = S.sb("AB", [128, 4, 8, 5])
        gtm1 = S.sb("gtm1", [5, D])
        gtm2 = S.sb("gtm2", [5, D])
        selt = S.sb("selt", [5, 192])
        invc = S.sb("invc", [128, 64])
        flg = S.sb("flg", [128, 16])
        stat = S.sb("stat", [128, 16])
        stat2 = S.sb("stat2", [128, 16])
        Lsum = S.sb("Lsum", [128, 4])
        ebl = S.sb("ebl", [128, 8])
        xht = S.sb("xht", [16, D])
        hTh = S.sb("hTh", [128, 8, 16], BF16)

        AR_WORDS = 24200
        arena = S.sb("arena", [128, AR_WORDS])
        AA = arena.h[:]
        arena_tts = []
        off = [0]

        def carve(name, words, dt, shape=None, at=None):
            o = off[0] if at is None else at
            v = AA[:, o:o + words]
            if dt == BF16:
                v = v.bitcast(BF16)
            if shape is not None and len(shape) == 2:
                v = v.rearrange("p (a b) -> p a b", b=shape[1])
            elif shape is not None and len(shape) == 3:
                v = v.rearrange("p (a b c) -> p a b c", b=shape[1], c=shape[2])
            if at is None:
                off[0] += words
            t = TT(name, v)
            arena_tts.append(t)
            return t, o

        uT, _ = carve("uT", 2112, F32)
        qT, o_q = carve("qT", 2048, F32, (4, N))
        kT, _ = carve("kT", 2048, F32, (4, N))
        ktm, o_ktm = carve("ktm", 2048, F32, (4, 512))
        vtm, _ = carve("vtm", 2048, BF16, (4, 1024))
        sgT, o_sg = carve("sgT", 2048, BF16, (8, N))
        alrT, o_alr = carve("alrT", 512, F32)
        apt, _ = carve("apt", 512, F32)
        E3, _ = carve("E3", 512, F32)
        khat, _ = carve("khat", 256, BF16)
        khs, _ = carve("khs", 256, BF16)
        E1, _ = carve("E1", 512, F32, (4, 128))
        E2, _ = carve("E2", 512, F32, (4, 128))
        qtil, _ = carve("qtil", 256, BF16, (4, 128))
        ktil, _ = carve("ktil", 256, BF16, (4, 128))
        scm, _ = carve("scm", 256, BF16, (4, 128))
        sq, _ = carve("sq", 512, BF16, (8, 128))
        rstT, _ = carve("rstT", 512, F32, (4, 128))
        otmp, _ = carve("otmp", 256, F32, (2, 128))
        boT, o_bo = carve("boT", 2048, BF16, (8, N))
        dT, _ = carve("dT", 1024, BF16, (4, N))
        aoT, _ = carve("aoT", 1024, BF16, (4, N))
        tA, _ = carve("tA", 528, F32)
        tB, _ = carve("tB", 528, F32)
        utm, _ = carve("utm", 512, F32)
        assert off[0] <= AR_WORDS, off[0]
        mgA, _ = carve("mgA", 4096, F32, (8, N), at=o_q)
        mrg, _ = carve("mrg", 2048, BF16, (8, N), at=o_ktm)
        ffT, _ = carve("ffT", 8192, BF16, (32, N), at=o_q)
        rl = [carve("rl%d" % i, 512, F32, at=o_sg + 512 * i)[0] for i in range(2)]
        rt = [carve("rt%d" % i, 512, F32, at=o_sg + 1024 + 512 * i)[0] for i in range(2)]
        xnx, _ = carve("xnx", 4096, F32, (4, D), at=o_alr)
        xn2, _ = carve("xn2", 512, BF16, at=o_alr + 4096)
        jkm = [carve("jkm%d" % i, 512, BF16, at=o_alr + 4608 + 512 * i)[0] for i in range(2)]
        yo2, _ = carve("yo2", 1024, F32, at=o_alr + 5632)
        assert o_alr + 6656 <= AR_WORDS
        cl, _ = carve("cl", 1024, F32, at=0)
        scl, _ = carve("scl", 1024, F32, at=1024)
        mpart, _ = carve("mpart", 1024, F32, at=2048)
        badab, _ = carve("badab", 1024, F32, at=3072)
        scT, _ = carve("scT", 20, BF16, (8, 5), at=4096)
        gbuf, _ = carve("gbuf", 1028, F32, at=4200)
        gb2 = [carve("gb2%d" % i, 1028, F32, at=5300 + 1100 * i)[0] for i in range(2)]
        Tacc, _ = carve("Tacc", 1024, F32, (4, 256), at=7600)
        Dj, _ = carve("Dj", 4, F32, at=8700)
        cach, _ = carve("cach", 512, F32, at=o_bo)

        def fence():
            S.fence(arena_tts)

        PSB = [S.ps("psb%d" % i, [128, 2 * N], BF16) for i in range(2)]
        PS = [S.ps("ps%d" % i, [128, N]) for i in range(6)]
        psc = [0, 0]

        def nps():
            t = PS[psc[0] % 6]
            psc[0] += 1
            return t

        def npsb():
            t = PSB[psc[1] % 2]
            psc[1] += 1
            return t

        slotc = [0]

        scr = {}

        def slab(wname, w, c0, C, K=8, r0=0, cache=True):
            t = ring[slotc[0] % NSLOT]
            slotc[0] += 1
            flat = t.h[:, 0:K * C]
            v = flat.rearrange("p (k c) -> p k c", c=C)
            key = (wname, c0, C, K, r0)
            if not cache:
                S.dma_load("pool", t, v, wsl(w, c0, C, K, r0))
            elif key not in scr:
                idx = len(scr)
                dtt = TT("scr%d" % idx, None)
                scr[key] = (idx, dtt)
                S.dma_load("pool", t, v, wsl(w, c0, C, K, r0))
                S.dma_store("sp", t, wscr[idx, :, 0:K * C], flat, dram_tt=dtt)
            else:
                idx, dtt = scr[key]
                S.dma_load("pool", t, flat, wscr[idx, :, 0:K * C], reads=[dtt])
            return t, v

        def wsl(w, c0, C, K=8, r0=0):
            return w[r0:r0 + K * 128, c0:c0 + C].rearrange("(k p) n -> p k n", p=128)

        S.dma_load("sp", c128, c128[:], cst128)
        S.dma_load("sp", c64, c64[:], cst64)
        S.dma_load("sp", selt, selt[:], sel)
        S.dma_load("sp", invc, invc[:], invcnt)
        S.dma_load("sp", flg, flg[:], flags)
        S.dma_load("sp", walpha, walpha[0:16, :], w_alpha)
        S.dma_load("sp", walpha, walpha[16:17, :], b_alpha)
        S.dma_load("sp", balpha, balpha[:], b_alpha)
        S.dma_load("sp", FG, FG[:], final_g.partition_broadcast(128))
        S.dma_load("sp", vecs, vecs[:, 0:8], norm1_g.rearrange("o (k p) -> p (o k)", p=128))
        S.dma_load("sp", vecs, vecs[:, 8:16], norm2_g.rearrange("o (k p) -> p (o k)", p=128))
        S.dma_load("sp", vecs, vecs[:, 16:20], pool_scale.rearrange("o (k p) -> p (o k)", p=128))
        S.dma_load("sp", vecs, vecs[:, 20:22], gla_norm_g.rearrange("o (k p) -> p (o k)", p=128))
        S.dma_load("pool", wpool, wpool[:], w_pool.rearrange("g c d -> c g d"))
        S.dma_load("sp", cl, cl[0:5, :], cvec)
        S.dma_load("sp", xht, xht[:], xh)
        identf = c128.h[:, 0:128]
        CUM = c128.h[:, 128:256]
        REM = c128.h[:, 256:384]
        CUMs = c64.h[:, 0:64]
        REMs = c64.h[:, 64:128]
        SEGM = c64.h[:, 128:132]
        S.op("dve", lambda e: e.tensor_copy(identb[:], identf), [c128], [identb])
        for h in range(4):
            S.op("dve", lambda e, h=h: e.tensor_copy(CUM4[:, h, :], CUM), [c128], [CUM4])
            S.op("dve", lambda e, h=h: e.tensor_copy(CUM4s[:, h, :], CUMs), [c64], [CUM4s])
        S.op("dve", lambda e: e.memset(ones_f[:], 1.0), [], [ones_f])
        S.op("dve", lambda e: e.memset(ones_b[:], 1.0), [], [ones_b])
        S.op("dve", lambda e: e.memset(St[:], 0.0), [], [St])
        S.op("dve", lambda e: e.memset(Sb[0][:], 0.0), [], [Sb[0]])
        S.op("dve", lambda e: e.memset(Lsum[:], 0.0), [], [Lsum])

        S.op("act", lambda e: e.activation(scl[0:5, :], cl[0:5, :], AF.Silu), [cl], [scl])
        p = nps()
        for k in range(8):
            S.op("pe", lambda e, k=k, p=p: e.transpose(p[:, k * 5:(k + 1) * 5], scl[0:5, k * 128:(k + 1) * 128],
                                                        identf[0:5, 0:5]), [scl, c128], [p], signal=(k == 7))
        S.op("dve", lambda e, p=p: e.tensor_copy(scT[:].rearrange("p a b -> p (a b)"), p[:, 0:40]), [p], [scT])
        for j in range(6):
            S.dma_load("sp", badab, badab[0:5, :], b_ada[:, j * D:(j + 1) * D].partition_broadcast(5))
            dst = gtm1 if j == 2 else (gtm2 if j == 5 else mpart)
            for hf in range(2):
                st, sv = slab("w_ada", w_ada, j * D + hf * 512, 512, cache=False)
                p = nps()
                for k in range(8):
                    S.op("pe", lambda e, k=k, p=p, sv=sv: e.matmul(p[0:5, :], scT[:, k, :], sv[:, k, :],
                                                                    start=(k == 0), stop=(k == 7)),
                         [scT, st], [p], signal=(k == 7))
                S.op("dve", lambda e, p=p, hf=hf, dst=dst: e.tensor_tensor(
                    dst[0:5, hf * 512:(hf + 1) * 512], p[0:5, :], badab[0:5, hf * 512:(hf + 1) * 512], ALU.add),
                    [p, badab], [dst])
            p = nps()
            for k in range(8):
                S.op("pe", lambda e, k=k, p=p, dst=dst: e.transpose(p[:, k * 5:(k + 1) * 5], dst[0:5, k * 128:(k + 1) * 128],
                                                                      identf[0:5, 0:5]), [dst, c128], [p], signal=(k == 7))
            S.op("dve", lambda e, p=p, j=j: e.tensor_copy(modT[:, j, :, :].rearrange("p a b -> p (a b)"), p[:, 0:40]),
                 [p], [modT])
        for i, (jsh, jsc, vo) in enumerate(((0, 1, 0), (3, 4, 8))):
            S.op("dve", lambda e, i=i, jsc=jsc: e.tensor_scalar(AB[:, 2 * i, :, :], modT[:, jsc, :, :], 1.0, None, ALU.add),
                 [modT], [AB])
            for k in range(8):
                S.op("dve", lambda e, i=i, k=k, vo=vo: e.tensor_scalar(AB[:, 2 * i, k, :], AB[:, 2 * i, k, :],
                                                                        vecs[:, vo + k:vo + k + 1], None, ALU.mult),
                     [AB, vecs], [AB])
            S.op("dve", lambda e, i=i, jsh=jsh: e.tensor_copy(AB[:, 2 * i + 1, :, :], modT[:, jsh, :, :]), [modT], [AB])

        def build_gates(selv, G):
            for gt, Gd in ((gtm1, G1), (gtm2, G2)):
                for hf in range(2):
                    p = nps()
                    S.op("pe", lambda e, p=p, gt=gt, hf=hf: e.matmul(p[0:G, :], selv, gt[0:5, hf * 512:(hf + 1) * 512],
                                                                      start=True, stop=True), [selt, gt], [p])
                    S.op("dve", lambda e, p=p, Gd=Gd, hf=hf: e.tensor_copy(Gd[0:G, hf * 512:(hf + 1) * 512], p[0:G, :]),
                         [p], [Gd])

        build_gates(selt.h[0:5, 0:128], 128)
        fence()

        def norm_to_hT(xsrc, rows, gcols, which, segs, hdst=None, xn=xn, stc=0):
            xsrc_t, xa = xsrc
            hd = hT if hdst is None else hdst
            ncol = rows
            S.op("act", lambda e: e.activation(xn[0:rows, :], xa, AF.Square, accum_out=stat[0:rows, 0:1]),
                 [xsrc_t], [xn, stat])
            S.op("act", lambda e: e.activation(stat[0:rows, 1:2], stat[0:rows, 0:1], AF.Ln, bias=EPS, scale=1.0 / D),
                 [stat], [stat])
            S.op("act", lambda e: e.activation(stat[0:rows, 2:3], stat[0:rows, 1:2], AF.Exp, scale=-0.5), [stat], [stat])
            S.op("dve", lambda e: e.tensor_scalar(xn[0:rows, :], xa, stat[0:rows, 2:3], None, ALU.mult),
                 [xsrc_t, stat], [xn])
            p = npsb()
            for k in range(8):
                S.op("pe", lambda e, k=k, p=p: e.transpose(p[:, k * 128:k * 128 + rows], xn[0:rows, k * 128:(k + 1) * 128],
                                                            identb[0:rows, 0:rows]), [xn, identb], [p], signal=(k == 7))
            ia, ib = (0, 1) if which == 1 else (2, 3)
            for k in range(8):
                for (c0, c1, r) in segs:
                    S.op("dve", lambda e, k=k, p=p, c0=c0, c1=c1, r=r: e.tensor_scalar(
                        hd[:, k, gcols + c0:gcols + c1], p[:, k * 128 + c0:k * 128 + c1],
                        AB[:, ia, k, r:r + 1], AB[:, ib, k, r:r + 1], ALU.mult, ALU.add), [p, AB], [(hd, (k, gcols + c0))])

        def mm_fm(p, M, st, sv, c0, ncols, rhs_t=None, K=8):
            rt_ = hT if rhs_t is None else rhs_t
            for k in range(K):
                S.op("pe", lambda e, k=k: e.matmul(p[0:M, 0:ncols], sv[:, k, c0:c0 + M], rt_[:, k, 0:ncols],
                                                   start=(k == 0), stop=(k == K - 1)),
                     [st, rt_], [p], signal=(k == K - 1))

        def mm_tm(p, G, gcols, st, sv, c0, C, lhs_t=None, K=8, k0=0, first=True, last=True):
            lt = hT if lhs_t is None else lhs_t
            for k in range(K):
                S.op("pe", lambda e, k=k: e.matmul(p[0:G, 0:C], lt[:, k0 + k, gcols:gcols + G], sv[:, k, c0:c0 + C],
                                                   start=(first and k == 0), stop=(last and k == K - 1)),
                     [st, lt], [p], signal=(k == K - 1))

        def gla_decay(G, gcols, cumv, remv, need_full):
            p = nps()
            S.op("pe", lambda e: e.matmul(p[0:G, :], alrT[0:17, gcols:gcols + G], walpha[0:17, :], start=True, stop=True),
                 [alrT, walpha], [p])
            S.op("act", lambda e: e.activation(apt[0:G, :], p[0:G, :], AF.Exp, scale=-1.0), [p], [apt])
            S.op("act", lambda e: e.activation(apt[0:G, :], apt[0:G, :], AF.Ln, bias=1.0), [apt], [apt])
            p2 = nps()
            S.op("pe", lambda e: e.matmul(p2[0:G, :], remv, apt[0:G, :], start=True, stop=True), [c128, c64, apt], [p2])
            S.op("act", lambda e: e.activation(E3[0:G, :], p2[0:G, :], AF.Exp, scale=-1.0 / 16), [p2], [E3])
            S.op("dve", lambda e: e.tensor_tensor(khat[0:G, :], ktm[0:G, gcols // 128 if G == 128 else 0, :], E3[0:G, :],
                                                  ALU.mult), [ktm, E3], [khat])
            if not need_full:
                return
            p3 = nps()
            for h in range(4):
                S.op("pe", lambda e, h=h: e.matmul(p3[:, h * G:(h + 1) * G], apt[0:G, h * 128:(h + 1) * 128], cumv,
                                                   start=True, stop=True), [apt, c128, c64], [p3], signal=(h == 3))
            S.op("act", lambda e: e.activation(E1[:, :, 0:G], p3[:, 0:4 * G].rearrange("p (a b) -> p a b", b=G), AF.Exp,
                                               scale=-1.0 / 16), [p3], [E1])
            S.op("act", lambda e: e.activation(E2[:, :, 0:G], p3[:, 0:4 * G].rearrange("p (a b) -> p a b", b=G), AF.Exp,
                                               scale=1.0 / 16), [p3], [E2])
            S.op("dve", lambda e: e.scalar_tensor_tensor(qtil[:, :, 0:G], qT[:, :, gcols:gcols + G], 128.0 ** -0.5,
                                                         E1[:, :, 0:G], ALU.mult, ALU.mult), [qT, E1], [qtil])
            S.op("dve", lambda e: e.tensor_tensor(ktil[:, :, 0:G], kT[:, :, gcols:gcols + G], E2[:, :, 0:G], ALU.mult),
                 [kT, E2], [ktil])

        def pool_branch(nseq, L, NN, first_tile):
            W = 16 + L
            U = uT.h[:, 0:4 * nseq * W].rearrange("p (g s w) -> p g s w", s=nseq, w=W)
            A = tA.h[:, 0:nseq * W].rearrange("p (s w) -> p s w", w=W)
            B = tB.h[:, 0:nseq * W].rearrange("p (s w) -> p s w", w=W)
            dv = dT.h[:, :, 0:NN].rearrange("p g (s l) -> p g s l", l=L)
            for gi in range(4):
                src = U[:, gi]
                cur_t, cur = uT, src
                lo = 0
                for lev in range(gi + 1):
                    sh = 1 << lev
                    lo2 = lo + sh
                    dstt, dst = (tA, A) if lev % 2 == 0 else (tB, B)
                    S.op("dve", lambda e, dst=dst, cur=cur, lo2=lo2, sh=sh: e.tensor_tensor(
                        dst[:, :, lo2:W], cur[:, :, lo2:W], cur[:, :, lo2 - sh:W - sh], ALU.add), [cur_t], [dstt])
                    cur_t, cur, lo = dstt, dst, lo2
                wdt = float(1 << (gi + 1))
                S.op("dve", lambda e, cur=cur, gi=gi, wdt=wdt: e.scalar_tensor_tensor(
                    dv[:, gi], cur[:, :, 16:W], 1.0 / wdt, U[:, gi, :, 16:W], ALU.mult, ALU.subtract), [cur_t, uT], [dT])
                if first_tile:
                    S.op("dve", lambda e, cur=cur, gi=gi: e.tensor_tensor(
                        cur[:, 0, 16:32], cur[:, 0, 16:32], invc[:, gi * 16:(gi + 1) * 16], ALU.mult), [cur_t, invc], [cur_t])
                    S.op("dve", lambda e, cur=cur, gi=gi: e.tensor_tensor(
                        dT[:, gi, 0:16], cur[:, 0, 16:32], U[:, gi, 0, 16:32], ALU.subtract), [cur_t, uT], [dT])
            for gi in range(4):
                p = nps()
                S.op("pe", lambda e, gi=gi, p=p: e.matmul(p[:, 0:NN], wpool[:, gi, :], dT[:, gi, 0:NN], start=True, stop=True),
                     [wpool, dT], [p])
                S.op("dve", lambda e, gi=gi, p=p: e.tensor_scalar(aoT[:, gi, 0:NN], p[:, 0:NN], vecs[:, 16 + gi:17 + gi], None,
                                                                  ALU.mult), [p, vecs], [(aoT, gi)])

        def gla_out(G, gcols, vrow, mask4, segstates, sbsel):
            p = nps()
            for h in range(4):
                S.op("pe", lambda e, h=h: e.matmul(p[0:G, h * G:(h + 1) * G], ktil[:, h, 0:G], qtil[:, h, 0:G],
                                                   start=True, stop=True), [ktil, qtil], [p], signal=(h == 3))
            S.op("dve", lambda e: e.tensor_tensor(scm[0:G, :, 0:G], p[0:G, 0:4 * G].rearrange("p (a b) -> p a b", b=G),
                                                  mask4, ALU.mult), [p, CUM4, CUM4s], [scm])
            po = [nps(), nps()]
            for h in range(4):
                for vc in range(2):
                    pp = po[h // 2]
                    o0 = ((h % 2) * 2 + vc) * G
                    S.op("pe", lambda e, h=h, vc=vc, pp=pp, o0=o0: e.matmul(
                        pp[:, o0:o0 + G], vtm[0:G, vrow, h * 256 + vc * 128:h * 256 + (vc + 1) * 128], scm[0:G, h, 0:G],
                        start=True, stop=False), [vtm, scm], [pp], signal=False)
                    ns = len(segstates)
                    for si, (c0, c1, sbt) in enumerate(segstates):
                        S.op("pe", lambda e, h=h, vc=vc, pp=pp, o0=o0, c0=c0, c1=c1, sbt=sbt, si=si: e.matmul(
                            pp[:, o0 + c0:o0 + c1], sbt[:, h, vc * 128:(vc + 1) * 128], qtil[:, h, c0:c1],
                            start=False, stop=(si == ns - 1)), [sbt, qtil], [pp],
                            signal=(si == ns - 1 and vc == 1 and h % 2 == 1))
            for half in range(2):
                S.op("act", lambda e, half=half: e.activation(
                    sq[:, half * 4:(half + 1) * 4, 0:G], po[half][:, 0:4 * G].rearrange("p (a b) -> p a b", b=G), AF.Square),
                    [po[half]], [(sq, half)])
            pss = nps()
            for h in range(4):
                for vc in range(2):
                    S.op("pe", lambda e, h=h, vc=vc: e.matmul(pss[:, h * G:(h + 1) * G], ones_b[:], sq[:, h * 2 + vc, 0:G],
                                                              start=(vc == 0), stop=(vc == 1)), [ones_b, sq], [pss],
                         signal=(h == 3 and vc == 1))
            S.op("act", lambda e: e.activation(rstT[:, :, 0:G], pss[:, 0:4 * G].rearrange("p (a b) -> p a b", b=G), AF.Ln,
                                               bias=EPS, scale=1.0 / 256), [pss], [rstT])
            S.op("act", lambda e: e.activation(rstT[:, :, 0:G], rstT[:, :, 0:G], AF.Exp, scale=-0.5), [rstT], [rstT])
            for h in range(4):
                for vc in range(2):
                    pp = po[h // 2]
                    o0 = ((h % 2) * 2 + vc) * G
                    S.op("dve", lambda e, h=h, vc=vc, pp=pp, o0=o0: e.scalar_tensor_tensor(
                        otmp[:, vc, 0:G], pp[:, o0:o0 + G], vecs[:, 20 + vc:21 + vc], rstT[:, h, 0:G],
                        ALU.mult, ALU.mult), [pp, vecs, rstT], [(otmp, vc)])
                S.op("dve", lambda e, h=h: e.tensor_tensor(boT[:, 2 * h:2 * h + 2, gcols:gcols + G], otmp[:, :, 0:G],
                                                           sgT[:, 2 * h:2 * h + 2, gcols:gcols + G], ALU.mult), [otmp, sgT], [boT])

        def state_update(G, vrow, khv_t, khv, Sfp, ecol, Sbf_dst, vtm=vtm):
            pp = [nps(), nps()]
            for h in range(4):
                S.op("pe", lambda e, h=h: e.matmul(pp[h // 2][:, (h % 2) * 256:(h % 2 + 1) * 256], khv[0:G, h * 128:(h + 1) * 128],
                                                   vtm[0:G, vrow, h * 256:(h + 1) * 256], start=True, stop=True),
                     [khv_t, vtm], [pp[h // 2]], signal=(h % 2 == 1))
            for h in range(4):
                et, ea = ecol(h)
                S.op("dve", lambda e, h=h, ea=ea: e.scalar_tensor_tensor(
                    Sfp[:, h, :], Sfp[:, h, :], ea, pp[h // 2][:, (h % 2) * 256:(h % 2 + 1) * 256], ALU.mult, ALU.add),
                    [Sfp, et, pp[h // 2]], [Sfp])
            if Sbf_dst is not None:
                S.op("act", lambda e: e.activation(Sbf_dst[:], Sfp[:], AF.Copy), [Sfp], [Sbf_dst])

        if CARRY == "allgather":
            for t in range(NT):
                S.dma_load("sp", xt, xt[:], xp[t * N:(t + 1) * N, :].rearrange("(g p) d -> p g d", p=128))
                for g in range(4):
                    norm_to_hT((xt, xt[:, g, :]), 128, g * 128, 1, [(0, 128, 0)])
                st, sv = slab("w_in", w_in, O_K, 512)
                for g in range(4):
                    p = nps()
                    mm_tm(p, 128, g * 128, st, sv, 0, 512)
                    S.op("act", lambda e, p=p, g=g: e.activation(ktm[:, g, :], p[:], AF.Copy), [p], [ktm])
                for hf in range(2):
                    st, sv = slab("w_in", w_in, O_V + hf * 512, 512)
                    for g in range(4):
                        p = nps()
                        mm_tm(p, 128, g * 128, st, sv, 0, 512)
                        S.op("act", lambda e, p=p, g=g, hf=hf: e.activation(vtm[:, g, hf * 512:(hf + 1) * 512], p[:], AF.Copy),
                             [p], [vtm])
                st, sv = slab("w_in", w_in, O_ALR, 16)
                p = nps()
                mm_fm(p, 16, st, sv, 0, N)
                S.op("act", lambda e, p=p: e.activation(alrT[0:16, :], p[0:16, :], AF.Copy), [p], [alrT])
                for g in range(4):
                    gla_decay(128, g * 128, CUM, REM, False)
                    pl = nps()
                    for h in range(4):
                        S.op("pe", lambda e, h=h, pl=pl: e.matmul(pl[:, h:h + 1], apt[:, h * 128:(h + 1) * 128], ones_f[:, 0:1],
                                                                  start=True, stop=True), [apt, ones_f], [pl], signal=(h == 3))
                    S.op("act", lambda e, pl=pl: e.activation(ebl[:, 0:4], pl[:, 0:4], AF.Exp, scale=-1.0 / 16), [pl], [ebl])
                    S.op("dve", lambda e, pl=pl: e.tensor_tensor(Lsum[:], Lsum[:], pl[:, 0:4], ALU.add), [Lsum, pl], [Lsum])
                    state_update(128, g, khat, khat, St, lambda h: (ebl, ebl[:, h:h + 1]), None)
                fence()
            S.op("act", lambda e: e.activation(gbuf[:, 0:1024], St[:].rearrange("p a b -> p (a b)"), AF.Copy), [St], [gbuf])
            S.op("act", lambda e: e.activation(gbuf[:, 1024:1028], Lsum[:], AF.Copy), [Lsum], [gbuf])
            S.dma_store("sp", gbuf, gin, gbuf[:])
            ssem = S.sems[gbuf.ssem]
            sval = 16 * gbuf.scnt
            ccs = S.sems[S._sem("CC")]

            def cc(eng):
                eng.wait_ge(ssem, sval)
                eng.collective_compute("AllGather", ALU.bypass, replica_groups=[list(range(NCORES))],
                                       ins=[gin], outs=[gout]).then_inc(ccs, 1)
                eng.wait_ge(ccs, 1)
            S.raw("pool", cc)
            S.op("dve", lambda e: e.memset(St[:], 0.0), [], [St])
            S.op("dve", lambda e: e.memset(Tacc[:], 0.0), [], [Tacc])
            for j in range(NCORES):
                gj = gb2[j % 2]
                gj.w["CC"] = 1
                S.dma_load("sp", gj, gj[:], gout[j * 128:(j + 1) * 128, :])
                S.op("dve", lambda e, j=j: e.scalar_tensor_tensor(St[:], Tacc[:], flg[:, 1 + j:2 + j], St[:], ALU.mult, ALU.add),
                     [Tacc, flg, St], [St])
                S.op("act", lambda e, gj=gj: e.activation(Dj[:, 0:4], gj[:, 1024:1028], AF.Exp, scale=-1.0 / 16), [gj], [Dj])
                for h in range(4):
                    S.op("dve", lambda e, h=h, gj=gj: e.scalar_tensor_tensor(
                        Tacc[:, h, :], Tacc[:, h, :], Dj[:, h:h + 1], gj[:, h * 256:(h + 1) * 256], ALU.mult, ALU.add),
                        [Tacc, Dj, gj], [Tacc])
            S.op("act", lambda e: e.activation(Sb[0][:], St[:], AF.Copy), [St], [Sb[0]])
            fence()

        if CARRY == "prefix":
            PW = [0]

            def pcarve(name, words, dt, shape=None):
                t, o = carve(name, words, dt, shape, at=PW[0])
                PW[0] += words
                return t
            xg = [pcarve("xg%d" % i, 1024, F32) for i in range(4)]
            xnp = [xn, pcarve("xnp1", 512, BF16)]
            jks = [pcarve("jk%d" % i, 512, BF16) for i in range(2)]
            hTp = [hT, pcarve("hTp1", 2048, BF16, (8, N))]
            ktmp = [pcarve("ktmp%d" % i, 2048, F32, (4, 512)) for i in range(2)]
            vtmp = [pcarve("vtmp%d" % i, 2048, BF16, (4, 1024)) for i in range(2)]
            alrp = [pcarve("alrp%d" % i, 512, F32) for i in range(2)]
            aptp = [pcarve("aptp%d" % i, 512, F32) for i in range(4)]
            E3p = [pcarve("E3p%d" % i, 512, F32) for i in range(4)]
            khp = [pcarve("khp%d" % i, 256, BF16) for i in range(4)]
            eblp = [pcarve("eblp%d" % i, 4, F32) for i in range(4)]
            statp = [pcarve("statp%d" % i, 12, F32) for i in range(2)]
            assert PW[0] <= AR_WORDS, PW[0]
            fence()
            for a_ in alrp:
                S.op("dve", lambda e, a_=a_: e.memset(a_[0:32, :], 1.0), [], [a_])

            def A1(i):
                sp_ = statp[i % 2]
                for g in range(4):
                    S.dma_load("sp", xg[g], xg[g][:], xpre[i * N + g * 128:i * N + (g + 1) * 128, :])
                for g in range(4):
                    S.op("act", lambda e, g=g: e.activation(jks[g % 2][:], xg[g][:], AF.Square, accum_out=sp_[:, g:g + 1]),
                         [xg[g]], [jks[g % 2], (sp_, g)])
                S.op("act", lambda e: e.activation(sp_[:, 4:8], sp_[:, 0:4], AF.Ln, bias=EPS, scale=1.0 / D), [sp_], [sp_])
                S.op("act", lambda e: e.activation(sp_[:, 8:12], sp_[:, 4:8], AF.Exp, scale=-0.5), [sp_], [sp_])

            def A2a(i, g):
                sp_ = statp[i % 2]
                xn_ = xnp[g % 2]
                S.op("act", lambda e: e.activation(xn_[:], xg[g][:], AF.Copy, scale=sp_[:, 8 + g:9 + g]), [xg[g], sp_], [xn_])

            def A2b(i, g):
                xn_ = xnp[g % 2]
                hd = hTp[i % 2]
                p = npsb()
                for k in range(8):
                    S.op("pe", lambda e, k=k: e.transpose(p[:, k * 128:(k + 1) * 128], xn_[:, k * 128:(k + 1) * 128], identb[:]),
                         [xn_, identb], [p], signal=(k == 7))
                for k in range(8):
                    S.op("dve", lambda e, k=k: e.tensor_scalar(hd[:, k, g * 128:(g + 1) * 128], p[:, k * 128:(k + 1) * 128],
                                                               AB[:, 0, k, 0:1], AB[:, 1, k, 0:1], ALU.mult, ALU.add), [p, AB], [(hd, (k, g))])

            def B_pieces(i):
                hT_, ktm_, vtm_, alr_ = hTp[i % 2], ktmp[i % 2], vtmp[i % 2], alrp[i % 2]

                def Bk():
                    st, sv = slab("w_in", w_in, O_K, 512)
                    for g in range(4):
                        p = nps()
                        mm_tm(p, 128, g * 128, st, sv, 0, 512, lhs_t=hT_)
                        S.op("act", lambda e, p=p, g=g: e.activation(ktm_[:, g, :], p[:], AF.Copy), [p], [(ktm_, g)])

                def Bv(hf):
                    def f():
                        st, sv = slab("w_in", w_in, O_V + hf * 512, 512)
                        for g in range(4):
                            p = nps()
                            mm_tm(p, 128, g * 128, st, sv, 0, 512, lhs_t=hT_)
                            if g % 2 == 0:
                                S.op("act", lambda e, p=p, g=g: e.activation(vtm_[:, g, hf * 512:(hf + 1) * 512], p[:], AF.Copy,
                                                                             scale=flg[:, 1 + i // NT:2 + i // NT]),
                                     [p, flg], [(vtm_, (g, hf))])
                            else:
                                S.op("dve", lambda e, p=p, g=g: e.tensor_scalar(vtm_[:, g, hf * 512:(hf + 1) * 512], p[:],
                                                                                flg[:, 1 + i // NT:2 + i // NT], None, ALU.mult),
                                     [p, flg], [(vtm_, (g, hf))])
                    return f

                def Balr():
                    st, sv = slab("w_in", w_in, O_ALR, 16)
                    p = nps()
                    mm_fm(p, 16, st, sv, 0, N, rhs_t=hT_)
                    S.op("act", lambda e, p=p: e.activation(alr_[0:16, :], p[0:16, :], AF.Copy), [p], [alr_])
                return [Bk, Bv(0), Bv(1), Balr]

            def C_stages(i):
                ktm_, vtm_, alr_ = ktmp[i % 2], vtmp[i % 2], alrp[i % 2]
                j = i // NT
                pz = [None] * 4
                pr = [None] * 4
                pl = [None] * 4

                def Z():
                    for g in range(4):
                        pz[g] = nps()
                        S.op("pe", lambda e, g=g: e.matmul(pz[g][:, :], alr_[0:17, g * 128:(g + 1) * 128], walpha[0:17, :],
                                                           start=True, stop=True), [alr_, walpha], [pz[g]])

                def ACT1():
                    for g in range(4):
                        S.op("act", lambda e, g=g: e.activation(aptp[g][:], pz[g][:], AF.Exp, scale=-1.0), [pz[g]], [aptp[g]])
                        S.op("act", lambda e, g=g: e.activation(aptp[g][:], aptp[g][:], AF.Ln, bias=1.0), [aptp[g]], [aptp[g]])

                def R():
                    for g in range(4):
                        pr[g] = nps()
                        S.op("pe", lambda e, g=g: e.matmul(pr[g][:, :], REM, aptp[g][:], start=True, stop=True), [c128, aptp[g]], [pr[g]])
                    pl[0] = nps()
                    for g in range(4):
                        for h in range(4):
                            S.op("pe", lambda e, g=g, h=h: e.matmul(pl[0][:, g * 4 + h:g * 4 + h + 1], aptp[g][:, h * 128:(h + 1) * 128],
                                                                    ones_f[:, 0:1], start=True, stop=True), [aptp[g], ones_f], [pl[0]],
                                 signal=(h == 3))

                def ACT2():
                    for g in range(4):
                        S.op("act", lambda e, g=g: e.activation(E3p[g][:], pr[g][:], AF.Exp, scale=-1.0 / 16), [pr[g]], [E3p[g]])
                        S.op("act", lambda e, g=g: e.activation(eblp[g][:, 0:4], pl[0][:, g * 4:g * 4 + 4], AF.Exp, scale=-1.0 / 16),
                             [pl[0]], [eblp[g]])

                def KH():
                    for g in range(4):
                        S.op("pool", lambda e, g=g: e.tensor_tensor(khp[g][:], ktm_[:, g, :], E3p[g][:], ALU.mult),
                             [ktm_, E3p[g]], [khp[g]])

                def ST(gs):
                    def f():
                        for g in gs:
                            state_update(128, g, khp[g], khp[g], St, lambda h, g=g: (eblp[g], eblp[g][:, h:h + 1]), None, vtm=vtm_)
                    return f
                return [Z, ACT1, R, ACT2, KH, ST((0, 1)), ST((2, 3))]

            A1(0)
            for g in range(4):
                A2a(0, g)
                A2b(0, g)
            A1(1)
            for f in B_pieces(0):
                f()
            for i in range(NPRE):
                cs = C_stages(i)
                nxt = i + 1 < NPRE
                bp = B_pieces(i + 1) if nxt else [lambda: None] * 4
                cs[0]()
                if nxt:
                    A2a(i + 1, 0)
                cs[1]()
                if nxt:
                    A2b(i + 1, 0)
                    A2a(i + 1, 1)
                cs[2]()
                if nxt:
                    A2b(i + 1, 1)
                    A2a(i + 1, 2)
                cs[3]()
                if nxt:
                    A2b(i + 1, 2)
                    A2a(i + 1, 3)
                cs[4]()
                if nxt:
                    A2b(i + 1, 3)
                if i + 2 < NPRE:
                    A1(i + 2)
                bp[0]()
                cs[5]()
                bp[1]()
                cs[6]()
                bp[2]()
                bp[3]()
            S.op("act", lambda e: e.activation(Sb[0][:], St[:], AF.Copy), [St], [Sb[0]])
            fence()
            S.op("dve", lambda e: e.memset(alrT[0:32, :], 1.0), [], [alrT])
        else:
            S.op("dve", lambda e: e.memset(alrT[0:32, :], 1.0), [], [alrT])

        sbi = [0]

        def make_pre(kind, t):
            isp = kind == "p"
            G_ = 128 if isp else 64
            NG_ = 4 if isp else 1
            segs_ = [(0, 128, 0)] if isp else [(16 * s_, 16 * s_ + 16, 1 + s_) for s_ in range(SPC)]

            def load():
                if isp:
                    S.dma_load("sp", xnx, xnx[:], xp[t * N:(t + 1) * N, :].rearrange("(g p) d -> p g d", p=128))
                else:
                    S.dma_load("sp", xnx, xnx[0:64, 0, :], xs)

            def stats():
                for g in range(NG_):
                    S.op("act", lambda e, g=g: e.activation(jkm[g % 2][0:G_, :], xnx[0:G_, g, :], AF.Square,
                                                            accum_out=stat2[0:G_, g:g + 1]), [xnx], [jkm[g % 2], (stat2, g)])
                S.op("act", lambda e: e.activation(stat2[0:G_, 4:4 + NG_], stat2[0:G_, 0:NG_], AF.Ln, bias=EPS, scale=1.0 / D),
                     [stat2], [stat2])
                S.op("act", lambda e: e.activation(stat2[0:G_, 8:8 + NG_], stat2[0:G_, 4:4 + NG_], AF.Exp, scale=-0.5),
                     [stat2], [stat2])

            def a2a(g):
                xn_ = (xn, xn2)[g % 2]
                S.op("act", lambda e: e.activation(xn_[0:G_, :], xnx[0:G_, g, :], AF.Copy, scale=stat2[0:G_, 8 + g:9 + g]),
                     [xnx, stat2], [xn_])

            def a2b(g):
                xn_ = (xn, xn2)[g % 2]
                p = npsb()
                for k in range(8):
                    S.op("pe", lambda e, k=k: e.transpose(p[:, k * 128:k * 128 + G_], xn_[0:G_, k * 128:(k + 1) * 128],
                                                          identb[0:G_, 0:G_]), [xn_, identb], [p], signal=(k == 7))
                for k in range(8):
                    for (c0, c1, r) in segs_:
                        S.op("dve", lambda e, k=k, c0=c0, c1=c1, r=r: e.tensor_scalar(
                            hT[:, k, g * 128 + c0:g * 128 + c1], p[:, k * 128 + c0:k * 128 + c1],
                            AB[:, 0, k, r:r + 1], AB[:, 1, k, r:r + 1], ALU.mult, ALU.add), [p, AB], [(hT, (k, g * 128 + c0))])

            def copy():
                S.dma_load("sp", xt, xt[0:G_, 0:NG_, :], xnx[0:G_, 0:NG_, :], reads=[xnx])
            return dict(load=load, stats=stats, a2a=a2a, a2b=a2b, copy=copy, ng=NG_)

        def layer_tile(kind, t, pre_done=False, next_tile=None):
            is_p = kind == "p"
            NN = N if is_p else SPC * LS
            G = 128 if is_p else 64
            NGR = 4 if is_p else 1
            first_tile = is_p and t == 0
            last_tile = is_p and t == NT - 1
            segs = [(0, 128, 0)] if is_p else [(16 * s, 16 * s + 16, 1 + s) for s in range(SPC)]
            W = 16 + (N if is_p else LS)
            nseq = 1 if is_p else SPC
            U = uT.h[:, 0:4 * nseq * W].rearrange("p (g s w) -> p g s w", s=nseq, w=W)
            if not pre_done:
                if is_p:
                    S.dma_load("sp", xt, xt[:], xp[t * N:(t + 1) * N, :].rearrange("(g p) d -> p g d", p=128))
                else:
                    S.dma_load("sp", xt, xt[0:64, 0, :], xs)
                for g in range(NGR):
                    norm_to_hT((xt, xt[0:G, g, :]), G, g * 128, 1, segs)
            if first_tile:
                norm_to_hT((xht, xht[:]), 16, 0, 1, [(0, 16, 0)], hdst=hTh)
            st, sv = slab("w_in", w_in, O_U, 512)
            for m in range(4):
                p = nps()
                mm_fm(p, 128, st, sv, m * 128, NN)
                S.op("act", lambda e, p=p, m=m: e.activation(U[:, m, :, 16:W], p[:, 0:NN].rearrange("p (s l) -> p s l", s=nseq),
                                                             AF.Copy), [p], [(uT, m)])
            if first_tile:
                p = nps()
                for m in range(4):
                    for k in range(8):
                        S.op("pe", lambda e, m=m, k=k, p=p, sv=sv: e.matmul(p[:, m * 16:(m + 1) * 16], sv[:, k, m * 128:(m + 1) * 128],
                                                                      hTh[:, k, :], start=(k == 0), stop=(k == 7)),
                             [st, hTh], [p], signal=(k == 7))
                S.op("dve", lambda e, p=p: e.tensor_scalar(U[:, :, 0, 0:16], p[:, 0:64].rearrange("p (g w) -> p g w", w=16),
                                                           flg[:, 0:1], None, ALU.mult), [p, flg], [uT])
            if not is_p:
                S.dma_load("sp", cach, cach[0:60, :], cache)
                p = nps()
                for gi in range(4):
                    S.op("pe", lambda e, gi=gi, p=p: e.transpose(p[:, gi * 60:(gi + 1) * 60], cach[0:60, gi * 128:(gi + 1) * 128],
                                                                  identf[0:60, 0:60]), [cach, c128], [p], signal=(gi == 3))
                S.op("dve", lambda e, p=p: e.tensor_copy(U[:, :, :, 1:16], p[:, 0:240].rearrange("p (g s w) -> p g s w", s=4, w=15)),
                     [p], [uT])
            if (not is_p) or last_tile:
                p = nps()
                gl = 0 if not is_p else 3 * 128
                mm_tm(p, G, gl, st, sv, 0, 512)
                S.op("act", lambda e, p=p: e.activation(utm[0:G, :], p[0:G, :], AF.Copy), [p], [utm])
                if is_p:
                    S.dma_store("sp", utm, cp_o, utm[113:128, :])
                else:
                    for s in range(SPC):
                        S.dma_store("sp", utm, cs_o[s * 15:(s + 1) * 15, :], utm[16 * s + 1:16 * s + 16, :])
            st, sv = slab("w_in", w_in, O_Q, 512)
            for m in range(4):
                p = nps()
                mm_fm(p, 128, st, sv, m * 128, NN)
                S.op("act", lambda e, p=p, m=m: e.activation(qT[:, m, 0:NN], p[:, 0:NN], AF.Copy), [p], [(qT, m)])
            st, sv = slab("w_in", w_in, O_K, 512)
            for m in range(4):
                p = nps()
                mm_fm(p, 128, st, sv, m * 128, NN)
                S.op("act", lambda e, p=p, m=m: e.activation(kT[:, m, 0:NN], p[:, 0:NN], AF.Copy), [p], [(kT, m)])
            for g in range(NGR):
                p = nps()
                mm_tm(p, G, g * 128, st, sv, 0, 512)
                S.op("act", lambda e, p=p, g=g: e.activation(ktm[0:G, g, :], p[0:G, :], AF.Copy), [p], [(ktm, g)])
            for hf in range(2):
                st, sv = slab("w_in", w_in, O_V + hf * 512, 512)
                for g in range(NGR):
                    p = nps()
                    mm_tm(p, G, g * 128, st, sv, 0, 512)
                    S.op("act", lambda e, p=p, g=g, hf=hf: e.activation(vtm[0:G, g, hf * 512:(hf + 1) * 512], p[0:G, :], AF.Copy),
                         [p], [(vtm, (g, hf))])
            for hf in range(2):
                st, sv = slab("w_in", w_in, O_G + hf * 512, 512)
                for m in range(4):
                    p = nps()
                    mm_fm(p, 128, st, sv, m * 128, NN)
                    S.op("act", lambda e, p=p, m=m, hf=hf: e.activation(sgT[:, hf * 4 + m, 0:NN], p[:, 0:NN], AF.Silu), [p], [(sgT, hf * 4 + m)])
            st, sv = slab("w_in", w_in, O_ALR, 16)
            p = nps()
            mm_fm(p, 16, st, sv, 0, NN)
            S.op("dve", lambda e: e.memset(alrT[0:32, :], 1.0), [], [alrT])
            S.op("act", lambda e, p=p: e.activation(alrT[0:16, 0:NN], p[0:16, 0:NN], AF.Copy), [p], [alrT])

            pool_branch(nseq, N if is_p else LS, NN, first_tile)
            if is_p:
                S.op("dve", lambda e: e.tensor_copy(U[:, :, 0, 0:16], U[:, :, 0, N:N + 16]), [uT], [uT])
            for g in range(NGR):
                if is_p:
                    gla_decay(128, g * 128, CUM, REM, True)
                    cur = Sb[sbi[0] % 2]
                    nxt = Sb[(sbi[0] + 1) % 2]
                    sbi[0] += 1
                    gla_out(128, g * 128, g, CUM4[:], [(0, 128, cur)], None)
                    state_update(128, g, khat, khat, St, lambda h: (E1, E1[:, h, 127:128]), nxt)
                else:
                    gla_decay(64, 0, CUMs, REMs, True)
                    for s in range(SPC):
                        S.dma_load("sp", St, St[:], s0[s].rearrange("h c v -> c h v"))
                        S.op("act", lambda e, s=s: e.activation(Sb[s][:], St[:], AF.Copy), [St], [Sb[s]])
                    gla_out(64, 0, 0, CUM4s[:], [(16 * s, 16 * s + 16, Sb[s]) for s in range(SPC)], None)
                    for s in range(SPC):
                        S.dma_load("sp", St, St[:], s0[s].rearrange("h c v -> c h v"))
                        S.op("dve", lambda e, s=s: e.tensor_scalar(khs[0:64, :], khat[0:64, :], SEGM[:, s:s + 1], None, ALU.mult),
                             [khat, c64], [khs])
                        state_update(64, 0, khs, khs, St, lambda h, s=s: (E1, E1[:, h, 16 * s + 15:16 * s + 16]), None)
                        S.dma_store("sp", St, ss_o[s].rearrange("h c v -> c h v"), St[:])
            if last_tile:
                S.dma_store("sp", St, sp_o.rearrange("h c v -> c h v"), St[:])
                out_tiles.append(St)
            fence()

            st, sv = slab("w_pa", w_pa, 0, D, K=4)
            sga = []
            for hf in range(2):
                stg, svg = slab("w_in", w_in, O_GA + hf * 512, 512)
                for m in range(4):
                    mm = hf * 4 + m
                    pg = nps()
                    mm_fm(pg, 128, stg, svg, m * 128, NN)
                    r = rl[mm % 2]
                    S.op("act", lambda e, pg=pg, r=r: e.activation(r[:, 0:NN], pg[:, 0:NN], AF.Sigmoid), [pg], [r])
                    pa = nps()
                    mm_fm(pa, 128, st, sv, mm * 128, NN, rhs_t=aoT, K=4)
                    S.op("dve", lambda e, pa=pa, r=r, mm=mm: e.tensor_tensor(mgA[:, mm, 0:NN], pa[:, 0:NN], r[:, 0:NN], ALU.mult),
                         [pa, r], [(mgA, mm)])
            wpb = [slab("w_pb", w_pb, hf * 512, 512) for hf in range(2)]
            for hf in range(2):
                stg, svg = slab("w_in", w_in, O_GB + hf * 512, 512)
                st, sv = wpb[hf]
                for m in range(4):
                    mm = hf * 4 + m
                    pg = nps()
                    mm_fm(pg, 128, stg, svg, m * 128, NN)
                    r = rl[mm % 2]
                    S.op("act", lambda e, pg=pg, r=r: e.activation(r[:, 0:NN], pg[:, 0:NN], AF.Sigmoid), [pg], [r])
                    pb = nps()
                    mm_fm(pb, 128, st, sv, m * 128, NN, rhs_t=boT)
                    r2 = rt[mm % 2]
                    S.op("dve", lambda e, pb=pb, r=r, r2=r2: e.tensor_tensor(r2[:, 0:NN], pb[:, 0:NN], r[:, 0:NN], ALU.mult),
                         [pb, r], [r2])
                    if MODE == "pool_only":
                        S.op("dve", lambda e, mm=mm: e.tensor_copy(mrg[:, mm, 0:NN], mgA[:, mm, 0:NN]), [mgA], [mrg])
                    elif MODE == "gla_only":
                        S.op("dve", lambda e, r2=r2, mm=mm: e.tensor_copy(mrg[:, mm, 0:NN], r2[:, 0:NN]), [r2], [mrg])
                    else:
                        S.op("dve", lambda e, r2=r2, mm=mm: e.tensor_tensor(mrg[:, mm, 0:NN], r2[:, 0:NN], mgA[:, mm, 0:NN], ALU.add),
                             [r2, mgA], [mrg])
            for hf in range(2):
                st, sv = slab("w_out", w_out, hf * 512, 512)
                for g in range(NGR):
                    p = nps()
                    mm_tm(p, G, g * 128, st, sv, 0, 512, lhs_t=mrg)
                    r2 = rt[(hf * NGR + g) % 2]
                    S.op("dve", lambda e, p=p, r2=r2, hf=hf: e.tensor_tensor(r2[0:G, :], p[0:G, :], G1[0:G, hf * 512:(hf + 1) * 512],
                                                                             ALU.mult), [p, G1], [r2])
                    if "nomix" not in MODE:
                        S.op("dve", lambda e, r2=r2, g=g, hf=hf: e.tensor_tensor(xt[0:G, g, hf * 512:(hf + 1) * 512],
                                                                                 xt[0:G, g, hf * 512:(hf + 1) * 512], r2[0:G, :], ALU.add),
                             [xt, r2], [xt])
            fence()

            pre = make_pre(*next_tile) if next_tile is not None else None
            if pre:
                pre["load"]()
            for g in range(NGR):
                norm_to_hT((xt, xt[0:G, g, :]), G, g * 128, 2, segs)
            if DEBUG and is_p and t == 0:
                dump(S, "h2T", hT, hT[:, :, 0:128], 128, 1024, inner=128)
            for sl in range(8):
                st, sv = slab("w_ff1", w_ff1, sl * 512, 512)
                for m in range(4):
                    p = nps()
                    mm_fm(p, 128, st, sv, m * 128, NN)
                    r = rl[m % 2]
                    S.op("act", lambda e, p=p, r=r: e.activation(r[:, 0:NN], p[:, 0:NN], AF.Relu), [p], [r])
                    S.op("dve", lambda e, r=r, sl=sl, m=m: e.tensor_tensor(ffT[:, sl * 4 + m, 0:NN], r[:, 0:NN], r[:, 0:NN], ALU.mult),
                         [r], [(ffT, sl * 4 + m)])
            if DEBUG and is_p and t == 0:
                dump(S, "ffT", ffT, ffT[:, 0:8, 0:128], 128, 1024, inner=128)
            if pre:
                pre["stats"]()
                pre["a2a"](0)
            pstep = [0]

            def pre_step():
                if not pre or pstep[0] >= pre["ng"]:
                    return
                g_ = pstep[0]
                pstep[0] += 1
                pre["a2b"](g_)
                if g_ + 1 < pre["ng"]:
                    pre["a2a"](g_ + 1)
            for hf in range(2):
                acc = [nps() for _ in range(NGR)]
                for kp in range(4):
                    st, sv = slab("w_ff2", w_ff2, hf * 512, 512, 8, kp * 1024)
                    for g in range(NGR):
                        mm_tm(acc[g], G, g * 128, st, sv, 0, 512, lhs_t=ffT, K=8, k0=kp * 8, first=(kp == 0), last=(kp == 3))
                    if kp % 2 == 1:
                        pre_step()
                for g in range(NGR):
                    r2 = rt[g % 2]
                    S.op("dve", lambda e, g=g, r2=r2, hf=hf, acc=acc: e.tensor_tensor(r2[0:G, :], acc[g][0:G, :],
                                                                             G2[0:G, hf * 512:(hf + 1) * 512], ALU.mult),
                         [acc[g], G2], [r2])
                    if "noffn" not in MODE:
                        S.op("dve", lambda e, r2=r2, g=g, hf=hf: e.tensor_tensor(xt[0:G, g, hf * 512:(hf + 1) * 512],
                                                                                 xt[0:G, g, hf * 512:(hf + 1) * 512], r2[0:G, :], ALU.add),
                             [xt, r2], [xt])
            for g in range(NGR):
                S.op("act", lambda e, g=g: e.activation(jkm[g % 2][0:G, :], xt[0:G, g, :], AF.Square, accum_out=stat[0:G, 4 + g:5 + g]),
                     [xt], [jkm[g % 2], (stat, g)])
            S.op("act", lambda e: e.activation(stat[0:G, 8:8 + NGR], stat[0:G, 4:4 + NGR], AF.Ln, bias=EPS, scale=1.0 / D), [stat], [stat])
            S.op("act", lambda e: e.activation(stat[0:G, 12:12 + NGR], stat[0:G, 8:8 + NGR], AF.Exp, scale=-0.5), [stat], [stat])
            for g in range(NGR):
                yo_ = (yo, yo2)[g % 2]
                S.op("dve", lambda e, g=g, yo_=yo_: e.scalar_tensor_tensor(yo_[0:G, :], xt[0:G, g, :], stat[0:G, 12 + g:13 + g], FG[0:G, :],
                                                                           ALU.mult, ALU.mult), [xt, stat, FG], [yo_])
                if is_p:
                    S.dma_store("sp", yo_, yp[t * N + g * 128:t * N + (g + 1) * 128, :], yo_[:])
                else:
                    S.dma_store("sp", yo_, ys, yo_[0:64, :])
            if pre:
                while pstep[0] < pre["ng"]:
                    pre_step()
                pre["copy"]()
            fence()

        tiles = [("p", t) for t in range(NT)] + [("s", 0)]
        for ti, (kind, t) in enumerate(tiles):
            if kind == "s":
                build_gates(selt.h[0:5, 128:192], 64)
            layer_tile(kind, t, pre_done=(ti > 0), next_tile=(tiles[ti + 1] if ti + 1 < len(tiles) else None))

        if DEBUG:
            dump(S, "gtm1", gtm1, gtm1[:], 5, D)
            dump(S, "selt", selt, selt[:], 5, 192)
            dump(S, "G1", G1, G1[:], 128, D)
            dump(S, "G2", G2, G2[:], 128, D)
            dump(S, "AB", AB, AB[:].rearrange("p a b c -> p (a b c)"), 128, 160)
            dump(S, "vecs", vecs, vecs[:], 128, 24)
            out_tiles.append(dbg_stage[0])
        out_tiles += [yo, yo2, utm, St]
        S.finish(out_tiles)
        S.emit_all()
        print("ops", S.nops, "waits", S.nwaits, "sems", len(S.sems))
    return nc


def _consts():
    ident = np.eye(128, dtype=np.float32)
    cum = np.triu(np.ones((128, 128), np.float32))
    rem = np.tril(np.ones((128, 128), np.float32), -1)
    cst128 = np.concatenate([ident, cum, rem], axis=1)
    seg = np.arange(64) // 16
    same = (seg[:, None] == seg[None, :]).astype(np.float32)
    cums = np.triu(np.ones((64, 64), np.float32)) * same
    rems = np.tril(np.ones((64, 64), np.float32), -1) * same
    segm = (seg[:, None] == np.arange(4)[None, :]).astype(np.float32)
    cst64 = np.concatenate([cums, rems, segm], axis=1)
    sel = np.zeros((5, 192), np.float32)
    sel[0, 0:128] = 1.0
    for s in range(4):
        sel[1 + s, 128 + 16 * s:128 + 16 * (s + 1)] = 1.0
    return cst128, cst64, sel


_NC_CACHE = {}


def kernel(x_prompt, x_sample, c_prompt, c_sample, state_gla, cache_pool, w_ada, b_ada,
           norm1_g, w_in, w_alpha, b_alpha, w_pool, pool_scale, gla_norm_g, w_pa, w_pb,
           w_out, norm2_g, w_ff1, w_ff2, final_g):
    f = lambda a: np.ascontiguousarray(np.asarray(a, dtype=np.float32))
    x_prompt, x_sample, c_prompt, c_sample = f(x_prompt), f(x_sample), f(c_prompt), f(c_sample)
    state_gla, cache_pool = f(state_gla), f(cache_pool)
    cst128, cst64, sel = _consts()
    shared = dict(
        w_ada=f(w_ada)[0], b_ada=f(b_ada), norm1_g=f(norm1_g), w_in=f(w_in)[0], w_alpha=f(w_alpha)[0],
        b_alpha=f(b_alpha), w_pool=f(w_pool)[0], pool_scale=f(pool_scale), gla_norm_g=f(gla_norm_g),
        w_pa=f(w_pa)[0], w_pb=f(w_pb)[0], w_out=f(w_out)[0], norm2_g=f(norm2_g), w_ff1=f(w_ff1)[0],
        w_ff2=f(w_ff2)[0], final_g=f(final_g).reshape(1, D), cst128=cst128, cst64=cst64, sel=sel)
    in_maps = []
    for c in range(NCORES):
        t0 = c * TPC
        xh = np.zeros((16, D), np.float32)
        if c > 0:
            xh[:] = x_prompt[0, t0 - 16:t0]
        invc = np.zeros((128, 64), np.float32)
        for gi in range(4):
            w = 2 << gi
            pos = t0 + np.arange(16)
            invc[:, gi * 16:(gi + 1) * 16] = (1.0 / np.minimum(pos + 1, w))[None, :]
        flags = np.zeros((128, 16), np.float32)
        flags[:, 0] = 0.0 if c == 0 else 1.0
        xpre = None
        if CARRY == "prefix":
            xpre = np.zeros((7 * TPC, D), np.float32)
            for j in range(7):
                b = c - 7 + j
                if b >= 0:
                    xpre[j * TPC:(j + 1) * TPC] = x_prompt[0, b * TPC:(b + 1) * TPC]
                    flags[:, 1 + j] = 1.0
        else:
            flags[:, 1 + c] = 1.0
        m = dict(shared)
        m.update(
            xp=x_prompt[0, t0:t0 + TPC], xh=xh, xs=x_sample[c * SPC:(c + 1) * SPC].reshape(SPC * LS, D),
            cvec=np.concatenate([c_prompt, c_sample[c * SPC:(c + 1) * SPC]], axis=0),
            s0=state_gla[0, c * SPC:(c + 1) * SPC], cache=cache_pool[0, c * SPC:(c + 1) * SPC].reshape(SPC * 15, 512),
            invcnt=invc, flags=flags)
        if xpre is not None:
            m["xpre"] = xpre
        in_maps.append({k: np.ascontiguousarray(v) for k, v in m.items()})
    if "nc" not in _NC_CACHE:
        _NC_CACHE["nc"] = build()
    nc = _NC_CACHE["nc"]
    res = run_bass_kernel_spmd(nc, in_maps, core_ids=list(range(NCORES)))
    R = res.results
    _NC_CACHE['last'] = R
    y_prompt = np.concatenate([R[c]["yp"] for c in range(NCORES)], axis=0)[None]
    y_sample = np.concatenate([R[c]["ys"].reshape(SPC, LS, D) for c in range(NCORES)], axis=0)
    st_p = R[NCORES - 1]["sp_o"][None, None]
    ch_p = R[NCORES - 1]["cp_o"][None, None]
    st_s = np.concatenate([R[c]["ss_o"] for c in range(NCORES)], axis=0)[None]
    ch_s = np.concatenate([R[c]["cs_o"].reshape(SPC, 15, 512) for c in range(NCORES)], axis=0)[None]
    return (y_prompt.astype(np.float32), y_sample.astype(np.float32), st_p.astype(np.float32),
            ch_p.astype(np.float32), st_s.astype(np.float32), ch_s.astype(np.float32))
```

```python
import contextlib
import numpy as np
import concourse.bass as bass
import concourse.mybir as mybir
from concourse.bass_utils import run_bass_kernel_spmd

F32 = mybir.dt.float32
BF16 = mybir.dt.bfloat16
ALU = mybir.AluOpType
AF = mybir.ActivationFunctionType

NCORES = 8
D = 1024
SEQ = 16384
TPC = SEQ // NCORES
NT = 4
N = 512
SPC = 4
LS = 16
INW = 5648
EPS = 1e-6
O_U, O_Q, O_K, O_V, O_G, O_ALR, O_GA, O_GB = 0, 512, 1024, 1536, 2560, 3584, 3600, 4624

ENGS = ("pe", "act", "dve", "pool", "sp")
WITH_EXCHANGE = False
STRICT_SAME_ENGINE = True
CARRY = "prefix"
NPRE = 7 * NT
DEBUG = False
MODE = "full"


class TT:
    def __init__(self, name, h):
        self.name = name
        self.h = h
        self.w = {}
        self.r = {}
        self.kw = {}
        self.kr = {}
        self.dsem = None
        self.dcnt = 0
        self.ssem = None
        self.scnt = 0

    def __getitem__(self, idx):
        return self.h[idx]


class Sched:
    def __init__(self, nc, es):
        self.nc = nc
        self.es = es
        self.sems = {}
        self.cnt = {e: 0 for e in ENGS}
        self.seen = {e: {} for e in ENGS}
        self.prog = {e: [] for e in ENGS}
        self.nwaits = 0
        self.nops = 0
        for e in ENGS:
            self._sem("E_" + e)

    def _sem(self, name):
        if name not in self.sems:
            self.sems[name] = self.es.enter_context(self.nc.semaphore(name))
        return name

    def sb(self, name, shape, dt=F32):
        h = self.es.enter_context(self.nc.sbuf_tensor(name, list(shape), dt))
        return TT(name, h)

    def ps(self, name, shape, dt=F32):
        h = self.es.enter_context(self.nc.psum_tensor(name, list(shape), dt))
        return TT(name, h)

    @staticmethod
    def _split(lst):
        tts, keys = [], []
        for x in lst:
            if isinstance(x, tuple):
                tts.append(x[0])
                keys.append(x[1])
            else:
                tts.append(x)
                keys.append(None)
        return tts, keys

    def _waits(self, e, reads, writes, rkeys=None, wkeys=None):
        waits = {}
        own = "E_" + e
        rkeys = rkeys or [None] * len(reads)
        wkeys = wkeys or [None] * len(writes)

        def need(nm, v):
            if self.seen[e].get(nm, 0) >= v:
                return
            if waits.get(nm, 0) < v:
                waits[nm] = v

        same = e not in ("pe", "sp")
        for t in reads:
            for nm, v in t.w.items():
                if nm == own:
                    if same:
                        need(nm, v)
                else:
                    need(nm, v)
        for t, key in zip(writes, wkeys):
            for dct, kd in ((t.w, t.kw), (t.r, t.kr)):
                for nm, v in dct.items():
                    if nm == own:
                        if same and STRICT_SAME_ENGINE:
                            if key is None:
                                need(nm, v)
                            else:
                                vv = max(kd.get(key, {}).get(own, 0), kd.get(None, {}).get(own, 0))
                                if vv:
                                    need(nm, vv)
                    else:
                        need(nm, v)
        for nm, v in waits.items():
            self.seen[e][nm] = v
        return list(waits.items())

    def op(self, e, fn, reads=(), writes=(), signal=True):
        reads, rkeys = self._split(reads)
        writes, wkeys = self._split(writes)
        waits = self._waits(e, reads, writes, rkeys, wkeys)
        own = "E_" + e
        if signal:
            self.cnt[e] += 1
            val = self.cnt[e]
        else:
            val = self.cnt[e] + 1
        for t, key in zip(reads, rkeys):
            if t.r.get(own, 0) < val:
                t.r[own] = val
            t.kr.setdefault(key, {})[own] = val
        for t, key in zip(writes, wkeys):
            if t.w.get(own, 0) < val:
                t.w[own] = val
            t.kw.setdefault(key, {})[own] = val
        sems = self.sems
        self.nwaits += len(waits)
        self.nops += 1

        def emit(eng):
            for nm, v in waits:
                eng.wait_ge(sems[nm], v)
            ins = fn(eng)
            if signal:
                ins.then_inc(sems[own], 1)

        self.prog[e].append(emit)

    def dma_load(self, q, dst, dst_ap, src_ap, reads=(), **kw):
        if dst.dsem is None:
            dst.dsem = self._sem("L_" + dst.name)
        waits = self._waits(q, reads, [dst])
        dst.dcnt += 1
        val = 16 * dst.dcnt
        dst.w[dst.dsem] = val
        for t in reads:
            t.r[dst.dsem] = val
        sems = self.sems
        sem = sems[dst.dsem]
        self.nwaits += len(waits)

        def emit(eng):
            for nm, v in waits:
                eng.wait_ge(sems[nm], v)
            eng.dma_start(out=dst_ap, in_=src_ap, **kw).then_inc(sem, 16)

        self.prog[q].append(emit)

    def dma_store(self, q, src, dst_ap, src_ap, dram_tt=None, **kw):
        if src.ssem is None:
            src.ssem = self._sem("S_" + src.name)
        waits = self._waits(q, [src], [])
        src.scnt += 1
        val = 16 * src.scnt
        src.r[src.ssem] = val
        if dram_tt is not None:
            dram_tt.w[src.ssem] = val
        sems = self.sems
        sem = sems[src.ssem]
        self.nwaits += len(waits)

        def emit(eng):
            for nm, v in waits:
                eng.wait_ge(sems[nm], v)
            eng.dma_start(out=dst_ap, in_=src_ap, **kw).then_inc(sem, 16)

        self.prog[q].append(emit)

    def fence(self, tiles):
        allev = {}
        for t in tiles:
            for dct in (t.w, t.r):
                for nm, v in dct.items():
                    if allev.get(nm, 0) < v:
                        allev[nm] = v
        for t in tiles:
            kwn = t.kw.setdefault(None, {})
            krn = t.kr.setdefault(None, {})
            for nm, v in allev.items():
                if t.r.get(nm, 0) < v:
                    t.r[nm] = v
                if t.w.get(nm, 0) < v:
                    t.w[nm] = v
                if kwn.get(nm, 0) < v:
                    kwn[nm] = v
                if krn.get(nm, 0) < v:
                    krn[nm] = v

    def raw(self, e, fn):
        self.prog[e].append(fn)

    def finish(self, out_tiles):
        sems = self.sems
        waits = []
        for t in out_tiles:
            if t.ssem is not None:
                waits.append((t.ssem, 16 * t.scnt))
        for e in ENGS:
            if e != "sp" and self.cnt[e] > 0:
                waits.append(("E_" + e, self.cnt[e]))

        def emit(eng):
            for nm, v in waits:
                eng.wait_ge(sems[nm], v)

        self.prog["sp"].append(emit)

    def emit_all(self):
        nc = self.nc
        prog = self.prog
        with nc.Block() as block:

            @block.tensor
            def _(eng):
                for f in prog["pe"]:
                    f(eng)

            @block.scalar
            def _(eng):
                for f in prog["act"]:
                    f(eng)

            @block.vector
            def _(eng):
                for f in prog["dve"]:
                    f(eng)

            @block.gpsimd
            def _(eng):
                for f in prog["pool"]:
                    f(eng)

            @block.sync
            def _(eng):
                for f in prog["sp"]:
                    f(eng)


def build():
    nc = bass.Bass("TRN2", target_bir_lowering=False)

    def din(name, shape):
        return nc.dram_tensor(name, list(shape), F32, kind="ExternalInput").ap()

    def dout(name, shape):
        return nc.dram_tensor(name, list(shape), F32, kind="ExternalOutput").ap()

    xp = din("xp", [TPC, D])
    xpre = din("xpre", [7 * TPC, D]) if CARRY == "prefix" else None
    xh = din("xh", [16, D])
    xs = din("xs", [SPC * LS, D])
    cvec = din("cvec", [5, D])
    s0 = din("s0", [SPC, 4, 128, 256])
    cache = din("cache", [SPC * 15, 512])
    w_ada = din("w_ada", [D, 6 * D])
    b_ada = din("b_ada", [1, 6 * D])
    norm1_g = din("norm1_g", [1, D])
    w_in = din("w_in", [D, INW])
    w_alpha = din("w_alpha", [16, 512])
    b_alpha = din("b_alpha", [1, 512])
    w_pool = din("w_pool", [4, 128, 128])
    pool_scale = din("pool_scale", [1, 512])
    gla_norm_g = din("gla_norm_g", [1, 256])
    w_pa = din("w_pa", [512, D])
    w_pb = din("w_pb", [D, D])
    w_out = din("w_out", [D, D])
    norm2_g = din("norm2_g", [1, D])
    w_ff1 = din("w_ff1", [D, 4 * D])
    w_ff2 = din("w_ff2", [4 * D, D])
    final_g = din("final_g", [1, D])
    cst128 = din("cst128", [128, 384])
    cst64 = din("cst64", [64, 132])
    sel = din("sel", [5, 192])
    invcnt = din("invcnt", [128, 64])
    flags = din("flags", [128, 16])

    yp = dout("yp", [TPC, D])
    ys = dout("ys", [SPC * LS, D])
    sp_o = dout("sp_o", [4, 128, 256])
    cp_o = dout("cp_o", [15, 512])
    ss_o = dout("ss_o", [SPC, 4, 128, 256])
    cs_o = dout("cs_o", [SPC * 15, 512])
    gin = nc.dram_tensor("gin", [128, 1028], F32, kind="Internal").ap()
    gout = nc.dram_tensor("gout", [NCORES * 128, 1028], F32, kind="Internal").ap()
    wscr = nc.dram_tensor("wscr", [40, 128, 4096], BF16, kind="Internal").ap()

    dbg_outs = {}

    def dump(S_, name, tt, ap, rows, cols, inner=None):
        o = nc.dram_tensor("dbg_" + name, [rows, cols], F32, kind="ExternalOutput").ap()
        dbg_outs[name] = o
        stg = dbg_stage[0]
        sv_ = stg[0:rows, 0:cols]
        if inner is not None:
            sv_ = sv_.rearrange("p (a b) -> p a b", b=inner)
        S_.op("dve", lambda e: e.tensor_copy(sv_, ap), [tt], [stg])
        S_.dma_store("sp", stg, o, stg[0:rows, 0:cols])

    dbg_stage = [None]
    with contextlib.ExitStack() as es:
        S = Sched(nc, es)
        es.enter_context(nc.allow_non_contiguous_dma("tiny vector re-layouts"))
        out_tiles = []

        xt = S.sb("xt", [128, 4, D])
        xn = S.sb("xn", [128, D], BF16)
        hT = S.sb("hT", [128, 8, N], BF16)
        NSLOT = 4
        ring = [S.sb("ring%d" % i, [128, 4096], BF16) for i in range(NSLOT)]
        G1 = S.sb("G1", [128, D])
        G2 = S.sb("G2", [128, D])
        FG = S.sb("FG", [128, D])
        St = S.sb("St", [128, 4, 256])
        Sb = [S.sb("Sb%d" % i, [128, 4, 256], BF16) for i in range(4)]
        yo = S.sb("yo", [128, D])
        if DEBUG:
            dbg_stage[0] = S.sb("dbgstg", [128, D])
        c128 = S.sb("c128", [128, 384])
        identb = S.sb("identb", [128, 128], BF16)
        CUM4 = S.sb("CUM4", [128, 4, 128])
        c64 = S.sb("c64", [64, 132])
        CUM4s = S.sb("CUM4s", [64, 4, 64])
        ones_f = S.sb("ones_f", [128, 128])
        ones_b = S.sb("ones_b", [128, 128], BF16)
        walpha = S.sb("walpha", [17, 512])
        balpha = S.sb("balpha", [1, 512])
        wpool = S.sb("wpool", [128, 4, 128], BF16)
        vecs = S.sb("vecs", [128, 24])
        modT = S.sb("modT", [128, 6, 8, 5])
        AB = S.sb("AB", [128, 4, 8, 5])
        gtm1 = S.sb("gtm1", [5, D])
        gtm2 = S.sb("gtm2", [5, D])
        selt = S.sb("selt", [5, 192])
        invc = S.sb("invc", [128, 64])
        flg = S.sb("flg", [128, 16])
        stat = S.sb("stat", [128, 16])
        stat2 = S.sb("stat2", [128, 16])
        Lsum = S.sb("Lsum", [128, 4])
        ebl = S.sb("ebl", [128, 8])
        xht = S.sb("xht", [16, D])
        hTh = S.sb("hTh", [128, 8, 16], BF16)

        AR_WORDS = 24200
        arena = S.sb("arena", [128, AR_WORDS])
        AA = arena.h[:]
        arena_tts = []
        off = [0]

        def carve(name, words, dt, shape=None, at=None):
            o = off[0] if at is None else at
            v = AA[:, o:o + words]
            if dt == BF16:
                v = v.bitcast(BF16)
            if shape is not None and len(shape) == 2:
                v = v.rearrange("p (a b) -> p a b", b=shape[1])
            elif shape is not None and len(shape) == 3:
                v = v.rearrange("p (a b c) -> p a b c", b=shape[1], c=shape[2])
            if at is None:
                off[0] += words
            t = TT(name, v)
            arena_tts.append(t)
            return t, o

        uT, _ = carve("uT", 2112, F32)
        qT, o_q = carve("qT", 2048, F32, (4, N))
        kT, _ = carve("kT", 2048, F32, (4, N))
        ktm, o_ktm = carve("ktm", 2048, F32, (4, 512))
        vtm, _ = carve("vtm", 2048, BF16, (4, 1024))
        sgT, o_sg = carve("sgT", 2048, BF16, (8, N))
        alrT, o_alr = carve("alrT", 512, F32)
        apt, _ = carve("apt", 512, F32)
        E3, _ = carve("E3", 512, F32)
        khat, _ = carve("khat", 256, BF16)
        khs, _ = carve("khs", 256, BF16)
        E1, _ = carve("E1", 512, F32, (4, 128))
        E2, _ = carve("E2", 512, F32, (4, 128))
        qtil, _ = carve("qtil", 256, BF16, (4, 128))
        ktil, _ = carve("ktil", 256, BF16, (4, 128))
        scm, _ = carve("scm", 256, BF16, (4, 128))
        sq, _ = carve("sq", 512, BF16, (8, 128))
        rstT, _ = carve("rstT", 512, F32, (4, 128))
        otmp, _ = carve("otmp", 256, F32, (2, 128))
        boT, o_bo = carve("boT", 2048, BF16, (8, N))
        dT, _ = carve("dT", 1024, BF16, (4, N))
        aoT, _ = carve("aoT", 1024, BF16, (4, N))
        tA, _ = carve("tA", 528, F32)
        tB, _ = carve("tB", 528, F32)
        utm, _ = carve("utm", 512, F32)
        assert off[0] <= AR_WORDS, off[0]
        mgA, _ = carve("mgA", 4096, F32, (8, N), at=o_q)
        mrg, _ = carve("mrg", 2048, BF16, (8, N), at=o_ktm)
        ffT, _ = carve("ffT", 8192, BF16, (32, N), at=o_q)
        rl = [carve("rl%d" % i, 512, F32, at=o_sg + 512 * i)[0] for i in range(2)]
        rt = [carve("rt%d" % i, 512, F32, at=o_sg + 1024 + 512 * i)[0] for i in range(2)]
        xnx, _ = carve("xnx", 4096, F32, (4, D), at=o_alr)
        xn2, _ = carve("xn2", 512, BF16, at=o_alr + 4096)
        jkm = [carve("jkm%d" % i, 512, BF16, at=o_alr + 4608 + 512 * i)[0] for i in range(2)]
        yo2, _ = carve("yo2", 1024, F32, at=o_alr + 5632)
        assert o_alr + 6656 <= AR_WORDS
        cl, _ = carve("cl", 1024, F32, at=0)
        scl, _ = carve("scl", 1024, F32, at=1024)
        mpart, _ = carve("mpart", 1024, F32, at=2048)
        badab, _ = carve("badab", 1024, F32, at=3072)
        scT, _ = carve("scT", 20, BF16, (8, 5), at=4096)
        gbuf, _ = carve("gbuf", 1028, F32, at=4200)
        gb2 = [carve("gb2%d" % i, 1028, F32, at=5300 + 1100 * i)[0] for i in range(2)]
        Tacc, _ = carve("Tacc", 1024, F32, (4, 256), at=7600)
        Dj, _ = carve("Dj", 4, F32, at=8700)
        cach, _ = carve("cach", 512, F32, at=o_bo)

        def fence():
            S.fence(arena_tts)

        PSB = [S.ps("psb%d" % i, [128, 2 * N], BF16) for i in range(2)]
        PS = [S.ps("ps%d" % i, [128, N]) for i in range(6)]
        psc = [0, 0]

        def nps():
            t = PS[psc[0] % 6]
            psc[0] += 1
            return t

        def npsb():
            t = PSB[psc[1] % 2]
            psc[1] += 1
            return t

        slotc = [0]

        scr = {}

        def slab(wname, w, c0, C, K=8, r0=0, cache=True):
            t = ring[slotc[0] % NSLOT]
            slotc[0] += 1
            flat = t.h[:, 0:K * C]
            v = flat.rearrange("p (k c) -> p k c", c=C)
            key = (wname, c0, C, K, r0)
            if not cache:
                S.dma_load("pool", t, v, wsl(w, c0, C, K, r0))
            elif key not in scr:
                idx = len(scr)
                dtt = TT("scr%d" % idx, None)
                scr[key] = (idx, dtt)
                S.dma_load("pool", t, v, wsl(w, c0, C, K, r0))
                S.dma_store("sp", t, wscr[idx, :, 0:K * C], flat, dram_tt=dtt)
            else:
                idx, dtt = scr[key]
                S.dma_load("pool", t, flat, wscr[idx, :, 0:K * C], reads=[dtt])
            return t, v

        def wsl(w, c0, C, K=8, r0=0):
            return w[r0:r0 + K * 128, c0:c0 + C].rearrange("(k p) n -> p k n", p=128)

        S.dma_load("sp", c128, c128[:], cst128)
        S.dma_load("sp", c64, c64[:], cst64)
        S.dma_load("sp", selt, selt[:], sel)
        S.dma_load("sp", invc, invc[:], invcnt)
        S.dma_load("sp", flg, flg[:], flags)
        S.dma_load("sp", walpha, walpha[0:16, :], w_alpha)
        S.dma_load("sp", walpha, walpha[16:17, :], b_alpha)
        S.dma_load("sp", balpha, balpha[:], b_alpha)
        S.dma_load("sp", FG, FG[:], final_g.partition_broadcast(128))
        S.dma_load("sp", vecs, vecs[:, 0:8], norm1_g.rearrange("o (k p) -> p (o k)", p=128))
        S.dma_load("sp", vecs, vecs[:, 8:16], norm2_g.rearrange("o (k p) -> p (o k)", p=128))
        S.dma_load("sp", vecs, vecs[:, 16:20], pool_scale.rearrange("o (k p) -> p (o k)", p=128))
        S.dma_load("sp", vecs, vecs[:, 20:22], gla_norm_g.rearrange("o (k p) -> p (o k)", p=128))
        S.dma_load("pool", wpool, wpool[:], w_pool.rearrange("g c d -> c g d"))
        S.dma_load("sp", cl, cl[0:5, :], cvec)
        S.dma_load("sp", xht, xht[:], xh)
        identf = c128.h[:, 0:128]
        CUM = c128.h[:, 128:256]
        REM = c128.h[:, 256:384]
        CUMs = c64.h[:, 0:64]
        REMs = c64.h[:, 64:128]
        SEGM = c64.h[:, 128:132]
        S.op("dve", lambda e: e.tensor_copy(identb[:], identf), [c128], [identb])
        for h in range(4):
            S.op("dve", lambda e, h=h: e.tensor_copy(CUM4[:, h, :], CUM), [c128], [CUM4])
            S.op("dve", lambda e, h=h: e.tensor_copy(CUM4s[:, h, :], CUMs), [c64], [CUM4s])
        S.op("dve", lambda e: e.memset(ones_f[:], 1.0), [], [ones_f])
        S.op("dve", lambda e: e.memset(ones_b[:], 1.0), [], [ones_b])
        S.op("dve", lambda e: e.memset(St[:], 0.0), [], [St])
        S.op("dve", lambda e: e.memset(Sb[0][:], 0.0), [], [Sb[0]])
        S.op("dve", lambda e: e.memset(Lsum[:], 0.0), [], [Lsum])

        S.op("act", lambda e: e.activation(scl[0:5, :], cl[0:5, :], AF.Silu), [cl], [scl])
        p = nps()
        for k in range(8):
            S.op("pe", lambda e, k=k, p=p: e.transpose(p[:, k * 5:(k + 1) * 5], scl[0:5, k * 128:(k + 1) * 128],
                                                        identf[0:5, 0:5]), [scl, c128], [p], signal=(k == 7))
        S.op("dve", lambda e, p=p: e.tensor_copy(scT[:].rearrange("p a b -> p (a b)"), p[:, 0:40]), [p], [scT])
        for j in range(6):
            S.dma_load("sp", badab, badab[0:5, :], b_ada[:, j * D:(j + 1) * D].partition_broadcast(5))
            dst = gtm1 if j == 2 else (gtm2 if j == 5 else mpart)
            for hf in range(2):
                st, sv = slab("w_ada", w_ada, j * D + hf * 512, 512, cache=False)
                p = nps()
                for k in range(8):
                    S.op("pe", lambda e, k=k, p=p, sv=sv: e.matmul(p[0:5, :], scT[:, k, :], sv[:, k, :],
                                                                    start=(k == 0), stop=(k == 7)),
                         [scT, st], [p], signal=(k == 7))
                S.op("dve", lambda e, p=p, hf=hf, dst=dst: e.tensor_tensor(
                    dst[0:5, hf * 512:(hf + 1) * 512], p[0:5, :], badab[0:5, hf * 512:(hf + 1) * 512], ALU.add),
                    [p, badab], [dst])
            p = nps()
            for k in range(8):
                S.op("pe", lambda e, k=k, p=p, dst=dst: e.transpose(p[:, k * 5:(k + 1) * 5], dst[0:5, k * 128:(k + 1) * 128],
                                                                      identf[0:5, 0:5]), [dst, c128], [p], signal=(k == 7))
            S.op("dve", lambda e, p=p, j=j: e.tensor_copy(modT[:, j, :, :].rearrange("p a b -> p (a b)"), p[:, 0:40]),
                 [p], [modT])
        for i, (jsh, jsc, vo) in enumerate(((0, 1, 0), (3, 4, 8))):
            S.op("dve", lambda e, i=i, jsc=jsc: e.tensor_scalar(AB[:, 2 * i, :, :], modT[:, jsc, :, :], 1.0, None, ALU.add),
                 [modT], [AB])
            for k in range(8):
                S.op("dve", lambda e, i=i, k=k, vo=vo: e.tensor_scalar(AB[:, 2 * i, k, :], AB[:, 2 * i, k, :],
                                                                        vecs[:, vo + k:vo + k + 1], None, ALU.mult),
                     [AB, vecs], [AB])
            S.op("dve", lambda e, i=i, jsh=jsh: e.tensor_copy(AB[:, 2 * i + 1, :, :], modT[:, jsh, :, :]), [modT], [AB])

        def build_gates(selv, G):
            for gt, Gd in ((gtm1, G1), (gtm2, G2)):
                for hf in range(2):
                    p = nps()
                    S.op("pe", lambda e, p=p, gt=gt, hf=hf: e.matmul(p[0:G, :], selv, gt[0:5, hf * 512:(hf + 1) * 512],
                                                                      start=True, stop=True), [selt, gt], [p])
                    S.op("dve", lambda e, p=p, Gd=Gd, hf=hf: e.tensor_copy(Gd[0:G, hf * 512:(hf + 1) * 512], p[0:G, :]),
                         [p], [Gd])

        build_gates(selt.h[0:5, 0:128], 128)
        fence()

        def norm_to_hT(xsrc, rows, gcols, which, segs, hdst=None, xn=xn, stc=0):
            xsrc_t, xa = xsrc
            hd = hT if hdst is None else hdst
            ncol = rows
            S.op("act", lambda e: e.activation(xn[0:rows, :], xa, AF.Square, accum_out=stat[0:rows, 0:1]),
                 [xsrc_t], [xn, stat])
            S.op("act", lambda e: e.activation(stat[0:rows, 1:2], stat[0:rows, 0:1], AF.Ln, bias=EPS, scale=1.0 / D),
                 [stat], [stat])
            S.op("act", lambda e: e.activation(stat[0:rows, 2:3], stat[0:rows, 1:2], AF.Exp, scale=-0.5), [stat], [stat])
            S.op("dve", lambda e: e.tensor_scalar(xn[0:rows, :], xa, stat[0:rows, 2:3], None, ALU.mult),
                 [xsrc_t, stat], [xn])
            p = npsb()
            for k in range(8):
                S.op("pe", lambda e, k=k, p=p: e.transpose(p[:, k * 128:k * 128 + rows], xn[0:rows, k * 128:(k + 1) * 128],
                                                            identb[0:rows, 0:rows]), [xn, identb], [p], signal=(k == 7))
            ia, ib = (0, 1) if which == 1 else (2, 3)
            for k in range(8):
                for (c0, c1, r) in segs:
                    S.op("dve", lambda e, k=k, p=p, c0=c0, c1=c1, r=r: e.tensor_scalar(
                        hd[:, k, gcols + c0:gcols + c1], p[:, k * 128 + c0:k * 128 + c1],
                        AB[:, ia, k, r:r + 1], AB[:, ib, k, r:r + 1], ALU.mult, ALU.add), [p, AB], [(hd, (k, gcols + c0))])

        def mm_fm(p, M, st, sv, c0, ncols, rhs_t=None, K=8):
            rt_ = hT if rhs_t is None else rhs_t
            for k in range(K):
                S.op("pe", lambda e, k=k: e.matmul(p[0:M, 0:ncols], sv[:, k, c0:c0 + M], rt_[:, k, 0:ncols],
                                                   start=(k == 0), stop=(k == K - 1)),
                     [st, rt_], [p], signal=(k == K - 1))

        def mm_tm(p, G, gcols, st, sv, c0, C, lhs_t=None, K=8, k0=0, first=True, last=True):
            lt = hT if lhs_t is None else lhs_t
            for k in range(K):
                S.op("pe", lambda e, k=k: e.matmul(p[0:G, 0:C], lt[:, k0 + k, gcols:gcols + G], sv[:, k, c0:c0 + C],
                                                   start=(first and k == 0), stop=(last and k == K - 1)),
                     [st, lt], [p], signal=(k == K - 1))

        def gla_decay(G, gcols, cumv, remv, need_full):
            p = nps()
            S.op("pe", lambda e: e.matmul(p[0:G, :], alrT[0:17, gcols:gcols + G], walpha[0:17, :], start=True, stop=True),
                 [alrT, walpha], [p])
            S.op("act", lambda e: e.activation(apt[0:G, :], p[0:G, :], AF.Exp, scale=-1.0), [p], [apt])
            S.op("act", lambda e: e.activation(apt[0:G, :], apt[0:G, :], AF.Ln, bias=1.0), [apt], [apt])
            p2 = nps()
            S.op("pe", lambda e: e.matmul(p2[0:G, :], remv, apt[0:G, :], start=True, stop=True), [c128, c64, apt], [p2])
            S.op("act", lambda e: e.activation(E3[0:G, :], p2[0:G, :], AF.Exp, scale=-1.0 / 16), [p2], [E3])
            S.op("dve", lambda e: e.tensor_tensor(khat[0:G, :], ktm[0:G, gcols // 128 if G == 128 else 0, :], E3[0:G, :],
                                                  ALU.mult), [ktm, E3], [khat])
            if not need_full:
                return
            p3 = nps()
            for h in range(4):
                S.op("pe", lambda e, h=h: e.matmul(p3[:, h * G:(h + 1) * G], apt[0:G, h * 128:(h + 1) * 128], cumv,
                                                   start=True, stop=True), [apt, c128, c64], [p3], signal=(h == 3))
            S.op("act", lambda e: e.activation(E1[:, :, 0:G], p3[:, 0:4 * G].rearrange("p (a b) -> p a b", b=G), AF.Exp,
                                               scale=-1.0 / 16), [p3], [E1])
            S.op("act", lambda e: e.activation(E2[:, :, 0:G], p3[:, 0:4 * G].rearrange("p (a b) -> p a b", b=G), AF.Exp,
                                               scale=1.0 / 16), [p3], [E2])
            S.op("dve", lambda e: e.scalar_tensor_tensor(qtil[:, :, 0:G], qT[:, :, gcols:gcols + G], 128.0 ** -0.5,
                                                         E1[:, :, 0:G], ALU.mult, ALU.mult), [qT, E1], [qtil])
            S.op("dve", lambda e: e.tensor_tensor(ktil[:, :, 0:G], kT[:, :, gcols:gcols + G], E2[:, :, 0:G], ALU.mult),
                 [kT, E2], [ktil])

        def pool_branch(nseq, L, NN, first_tile):
            W = 16 + L
            U = uT.h[:, 0:4 * nseq * W].rearrange("p (g s w) -> p g s w", s=nseq, w=W)
            A = tA.h[:, 0:nseq * W].rearrange("p (s w) -> p s w", w=W)
            B = tB.h[:, 0:nseq * W].rearrange("p (s w) -> p s w", w=W)
            dv = dT.h[:, :, 0:NN].rearrange("p g (s l) -> p g s l", l=L)
            for gi in range(4):
                src = U[:, gi]
                cur_t, cur = uT, src
                lo = 0
                for lev in range(gi + 1):
                    sh = 1 << lev
                    lo2 = lo + sh
                    dstt, dst = (tA, A) if lev % 2 == 0 else (tB, B)
                    S.op("dve", lambda e, dst=dst, cur=cur, lo2=lo2, sh=sh: e.tensor_tensor(
                        dst[:, :, lo2:W], cur[:, :, lo2:W], cur[:, :, lo2 - sh:W - sh], ALU.add), [cur_t], [dstt])
                    cur_t, cur, lo = dstt, dst, lo2
                wdt = float(1 << (gi + 1))
                S.op("dve", lambda e, cur=cur, gi=gi, wdt=wdt: e.scalar_tensor_tensor(
                    dv[:, gi], cur[:, :, 16:W], 1.0 / wdt, U[:, gi, :, 16:W], ALU.mult, ALU.subtract), [cur_t, uT], [dT])
                if first_tile:
                    S.op("dve", lambda e, cur=cur, gi=gi: e.tensor_tensor(
                        cur[:, 0, 16:32], cur[:, 0, 16:32], invc[:, gi * 16:(gi + 1) * 16], ALU.mult), [cur_t, invc], [cur_t])
                    S.op("dve", lambda e, cur=cur, gi=gi: e.tensor_tensor(
                        dT[:, gi, 0:16], cur[:, 0, 16:32], U[:, gi, 0, 16:32], ALU.subtract), [cur_t, uT], [dT])
            for gi in range(4):
                p = nps()
                S.op("pe", lambda e, gi=gi, p=p: e.matmul(p[:, 0:NN], wpool[:, gi, :], dT[:, gi, 0:NN], start=True, stop=True),
                     [wpool, dT], [p])
                S.op("dve", lambda e, gi=gi, p=p: e.tensor_scalar(aoT[:, gi, 0:NN], p[:, 0:NN], vecs[:, 16 + gi:17 + gi], None,
                                                                  ALU.mult), [p, vecs], [(aoT, gi)])

        def gla_out(G, gcols, vrow, mask4, segstates, sbsel):
            p = nps()
            for h in range(4):
                S.op("pe", lambda e, h=h: e.matmul(p[0:G, h * G:(h + 1) * G], ktil[:, h, 0:G], qtil[:, h, 0:G],
                                                   start=True, stop=True), [ktil, qtil], [p], signal=(h == 3))
            S.op("dve", lambda e: e.tensor_tensor(scm[0:G, :, 0:G], p[0:G, 0:4 * G].rearrange("p (a b) -> p a b", b=G),
                                                  mask4, ALU.mult), [p, CUM4, CUM4s], [scm])
            po = [nps(), nps()]
            for h in range(4):
                for vc in range(2):
                    pp = po[h // 2]
                    o0 = ((h % 2) * 2 + vc) * G
                    S.op("pe", lambda e, h=h, vc=vc, pp=pp, o0=o0: e.matmul(
                        pp[:, o0:o0 + G], vtm[0:G, vrow, h * 256 + vc * 128:h * 256 + (vc + 1) * 128], scm[0:G, h, 0:G],
                        start=True, stop=False), [vtm, scm], [pp], signal=False)
                    ns = len(segstates)
                    for si, (c0, c1, sbt) in enumerate(segstates):
                        S.op("pe", lambda e, h=h, vc=vc, pp=pp, o0=o0, c0=c0, c1=c1, sbt=sbt, si=si: e.matmul(
                            pp[:, o0 + c0:o0 + c1], sbt[:, h, vc * 128:(vc + 1) * 128], qtil[:, h, c0:c1],
                            start=False, stop=(si == ns - 1)), [sbt, qtil], [pp],
                            signal=(si == ns - 1 and vc == 1 and h % 2 == 1))
            for half in range(2):
                S.op("act", lambda e, half=half: e.activation(
                    sq[:, half * 4:(half + 1) * 4, 0:G], po[half][:, 0:4 * G].rearrange("p (a b) -> p a b", b=G), AF.Square),
                    [po[half]], [(sq, half)])
            pss = nps()
            for h in range(4):
                for vc in range(2):
                    S.op("pe", lambda e, h=h, vc=vc: e.matmul(pss[:, h * G:(h + 1) * G], ones_b[:], sq[:, h * 2 + vc, 0:G],
                                                              start=(vc == 0), stop=(vc == 1)), [ones_b, sq], [pss],
                         signal=(h == 3 and vc == 1))
            S.op("act", lambda e: e.activation(rstT[:, :, 0:G], pss[:, 0:4 * G].rearrange("p (a b) -> p a b", b=G), AF.Ln,
                                               bias=EPS, scale=1.0 / 256), [pss], [rstT])
            S.op("act", lambda e: e.activation(rstT[:, :, 0:G], rstT[:, :, 0:G], AF.Exp, scale=-0.5), [rstT], [rstT])
            for h in range(4):
                for vc in range(2):
                    pp = po[h // 2]
                    o0 = ((h % 2) * 2 + vc) * G
                    S.op("dve", lambda e, h=h, vc=vc, pp=pp, o0=o0: e.scalar_tensor_tensor(
                        otmp[:, vc, 0:G], pp[:, o0:o0 + G], vecs[:, 20 + vc:21 + vc], rstT[:, h, 0:G],
                        ALU.mult, ALU.mult), [pp, vecs, rstT], [(otmp, vc)])
                S.op("dve", lambda e, h=h: e.tensor_tensor(boT[:, 2 * h:2 * h + 2, gcols:gcols + G], otmp[:, :, 0:G],
                                                           sgT[:, 2 * h:2 * h + 2, gcols:gcols + G], ALU.mult), [otmp, sgT], [boT])

        def state_update(G, vrow, khv_t, khv, Sfp, ecol, Sbf_dst, vtm=vtm):
            pp = [nps(), nps()]
            for h in range(4):
                S.op("pe", lambda e, h=h: e.matmul(pp[h // 2][:, (h % 2) * 256:(h % 2 + 1) * 256], khv[0:G, h * 128:(h + 1) * 128],
                                                   vtm[0:G, vrow, h * 256:(h + 1) * 256], start=True, stop=True),
                     [khv_t, vtm], [pp[h // 2]], signal=(h % 2 == 1))
            for h in range(4):
                et, ea = ecol(h)
                S.op("dve", lambda e, h=h, ea=ea: e.scalar_tensor_tensor(
                    Sfp[:, h, :], Sfp[:, h, :], ea, pp[h // 2][:, (h % 2) * 256:(h % 2 + 1) * 256], ALU.mult, ALU.add),
                    [Sfp, et, pp[h // 2]], [Sfp])
            if Sbf_dst is not None:
                S.op("act", lambda e: e.activation(Sbf_dst[:], Sfp[:], AF.Copy), [Sfp], [Sbf_dst])

        if CARRY == "allgather":
            for t in range(NT):
                S.dma_load("sp", xt, xt[:], xp[t * N:(t + 1) * N, :].rearrange("(g p) d -> p g d", p=128))
                for g in range(4):
                    norm_to_hT((xt, xt[:, g, :]), 128, g * 128, 1, [(0, 128, 0)])
                st, sv = slab("w_in", w_in, O_K, 512)
                for g in range(4):
                    p = nps()
                    mm_tm(p, 128, g * 128, st, sv, 0, 512)
                    S.op("act", lambda e, p=p, g=g: e.activation(ktm[:, g, :], p[:], AF.Copy), [p], [ktm])
                for hf in range(2):
                    st, sv = slab("w_in", w_in, O_V + hf * 512, 512)
                    for g in range(4):
                        p = nps()
                        mm_tm(p, 128, g * 128, st, sv, 0, 512)
                        S.op("act", lambda e, p=p, g=g, hf=hf: e.activation(vtm[:, g, hf * 512:(hf + 1) * 512], p[:], AF.Copy),
                             [p], [vtm])
                st, sv = slab("w_in", w_in, O_ALR, 16)
                p = nps()
                mm_fm(p, 16, st, sv, 0, N)
                S.op("act", lambda e, p=p: e.activation(alrT[0:16, :], p[0:16, :], AF.Copy), [p], [alrT])
                for g in range(4):
                    gla_decay(128, g * 128, CUM, REM, False)
                    pl = nps()
                    for h in range(4):
                        S.op("pe", lambda e, h=h, pl=pl: e.matmul(pl[:, h:h + 1], apt[:, h * 128:(h + 1) * 128], ones_f[:, 0:1],
                                                                  start=True, stop=True), [apt, ones_f], [pl], signal=(h == 3))
                    S.op("act", lambda e, pl=pl: e.activation(ebl[:, 0:4], pl[:, 0:4], AF.Exp, scale=-1.0 / 16), [pl], [ebl])
                    S.op("dve", lambda e, pl=pl: e.tensor_tensor(Lsum[:], Lsum[:], pl[:, 0:4], ALU.add), [Lsum, pl], [Lsum])
                    state_update(128, g, khat, khat, St, lambda h: (ebl, ebl[:, h:h + 1]), None)
                fence()
            S.op("act", lambda e: e.activation(gbuf[:, 0:1024], St[:].rearrange("p a b -> p (a b)"), AF.Copy), [St], [gbuf])
            S.op("act", lambda e: e.activation(gbuf[:, 1024:1028], Lsum[:], AF.Copy), [Lsum], [gbuf])
            S.dma_store("sp", gbuf, gin, gbuf[:])
            ssem = S.sems[gbuf.ssem]
            sval = 16 * gbuf.scnt
            ccs = S.sems[S._sem("CC")]

            def cc(eng):
                eng.wait_ge(ssem, sval)
                eng.collective_compute("AllGather", ALU.bypass, replica_groups=[list(range(NCORES))],
                                       ins=[gin], outs=[gout]).then_inc(ccs, 1)
                eng.wait_ge(ccs, 1)
            S.raw("pool", cc)
            S.op("dve", lambda e: e.memset(St[:], 0.0), [], [St])
            S.op("dve", lambda e: e.memset(Tacc[:], 0.0), [], [Tacc])
            for j in range(NCORES):
                gj = gb2[j % 2]
                gj.w["CC"] = 1
                S.dma_load("sp", gj, gj[:], gout[j * 128:(j + 1) * 128, :])
                S.op("dve", lambda e, j=j: e.scalar_tensor_tensor(St[:], Tacc[:], flg[:, 1 + j:2 + j], St[:], ALU.mult, ALU.add),
                     [Tacc, flg, St], [St])
                S.op("act", lambda e, gj=gj: e.activation(Dj[:, 0:4], gj[:, 1024:1028], AF.Exp, scale=-1.0 / 16), [gj], [Dj])
                for h in range(4):
                    S.op("dve", lambda e, h=h, gj=gj: e.scalar_tensor_tensor(
                        Tacc[:, h, :], Tacc[:, h, :], Dj[:, h:h + 1], gj[:, h * 256:(h + 1) * 256], ALU.mult, ALU.add),
                        [Tacc, Dj, gj], [Tacc])
            S.op("act", lambda e: e.activation(Sb[0][:], St[:], AF.Copy), [St], [Sb[0]])
            fence()

        if CARRY == "prefix":
            PW = [0]

            def pcarve(name, words, dt, shape=None):
                t, o = carve(name, words, dt, shape, at=PW[0])
                PW[0] += words
                return t
            xg = [pcarve("xg%d" % i, 1024, F32) for i in range(4)]
            xnp = [xn, pcarve("xnp1", 512, BF16)]
            jks = [pcarve("jk%d" % i, 512, BF16) for i in range(2)]
            hTp = [hT, pcarve("hTp1", 2048, BF16, (8, N))]
            ktmp = [pcarve("ktmp%d" % i, 2048, F32, (4, 512)) for i in range(2)]
            vtmp = [pcarve("vtmp%d" % i, 2048, BF16, (4, 1024)) for i in range(2)]
            alrp = [pcarve("alrp%d" % i, 512, F32) for i in range(2)]
            aptp = [pcarve("aptp%d" % i, 512, F32) for i in range(4)]
            E3p = [pcarve("E3p%d" % i, 512, F32) for i in range(4)]
            khp = [pcarve("khp%d" % i, 256, BF16) for i in range(4)]
            eblp = [pcarve("eblp%d" % i, 4, F32) for i in range(4)]
            statp = [pcarve("statp%d" % i, 12, F32) for i in range(2)]
            assert PW[0] <= AR_WORDS, PW[0]
            fence()
            for a_ in alrp:
                S.op("dve", lambda e, a_=a_: e.memset(a_[0:32, :], 1.0), [], [a_])

            def A1(i):
                sp_ = statp[i % 2]
                for g in range(4):
                    S.dma_load("sp", xg[g], xg[g][:], xpre[i * N + g * 128:i * N + (g + 1) * 128, :])
                for g in range(4):
                    S.op("act", lambda e, g=g: e.activation(jks[g % 2][:], xg[g][:], AF.Square, accum_out=sp_[:, g:g + 1]),
                         [xg[g]], [jks[g % 2], (sp_, g)])
                S.op("act", lambda e: e.activation(sp_[:, 4:8], sp_[:, 0:4], AF.Ln, bias=EPS, scale=1.0 / D), [sp_], [sp_])
                S.op("act", lambda e: e.activation(sp_[:, 8:12], sp_[:, 4:8], AF.Exp, scale=-0.5), [sp_], [sp_])

            def A2a(i, g):
                sp_ = statp[i % 2]
                xn_ = xnp[g % 2]
                S.op("act", lambda e: e.activation(xn_[:], xg[g][:], AF.Copy, scale=sp_[:, 8 + g:9 + g]), [xg[g], sp_], [xn_])

            def A2b(i, g):
                xn_ = xnp[g % 2]
                hd = hTp[i % 2]
                p = npsb()
                for k in range(8):
                    S.op("pe", lambda e, k=k: e.transpose(p[:, k * 128:(k + 1) * 128], xn_[:, k * 128:(k + 1) * 128], identb[:]),
                         [xn_, identb], [p], signal=(k == 7))
                for k in range(8):
                    S.op("dve", lambda e, k=k: e.tensor_scalar(hd[:, k, g * 128:(g + 1) * 128], p[:, k * 128:(k + 1) * 128],
                                                               AB[:, 0, k, 0:1], AB[:, 1, k, 0:1], ALU.mult, ALU.add), [p, AB], [(hd, (k, g))])

            pref_slabs = {}

            def pslab(c0, C):
                if (c0, C) not in pref_slabs:
                    pref_slabs[(c0, C)] = slab("w_in", w_in, c0, C)
                return pref_slabs[(c0, C)]

            def B_pieces(i):
                hT_, ktm_, vtm_, alr_ = hTp[i % 2], ktmp[i % 2], vtmp[i % 2], alrp[i % 2]

                def Bk():
                    st, sv = pslab(O_K, 512)
                    for g in range(4):
                        p = nps()
                        mm_tm(p, 128, g * 128, st, sv, 0, 512, lhs_t=hT_)
                        S.op("act", lambda e, p=p, g=g: e.activation(ktm_[:, g, :], p[:], AF.Copy), [p], [(ktm_, g)])

                def Bv(hf):
                    def f():
                        st, sv = pslab(O_V + hf * 512, 512)
                        for g in range(4):
                            p = nps()
                            mm_tm(p, 128, g * 128, st, sv, 0, 512, lhs_t=hT_)
                            if g % 2 == 0:
                                S.op("act", lambda e, p=p, g=g: e.activation(vtm_[:, g, hf * 512:(hf + 1) * 512], p[:], AF.Copy,
                                                                             scale=flg[:, 1 + i // NT:2 + i // NT]),
                                     [p, flg], [(vtm_, (g, hf))])
                            else:
                                S.op("dve", lambda e, p=p, g=g: e.tensor_scalar(vtm_[:, g, hf * 512:(hf + 1) * 512], p[:],
                                                                                flg[:, 1 + i // NT:2 + i // NT], None, ALU.mult),
                                     [p, flg], [(vtm_, (g, hf))])
                    return f

                def Balr():
                    st, sv = pslab(O_ALR, 16)
                    p = nps()
                    mm_fm(p, 16, st, sv, 0, N, rhs_t=hT_)
                    S.op("act", lambda e, p=p: e.activation(alr_[0:16, :], p[0:16, :], AF.Copy), [p], [alr_])
                return [Bk, Bv(0), Bv(1), Balr]

            def C_stages(i):
                ktm_, vtm_, alr_ = ktmp[i % 2], vtmp[i % 2], alrp[i % 2]
                j = i // NT
                pz = [None] * 4
                pr = [None] * 4
                pl = [None] * 4

                def Z():
                    for g in range(4):
                        pz[g] = nps()
                        S.op("pe", lambda e, g=g: e.matmul(pz[g][:, :], alr_[0:17, g * 128:(g + 1) * 128], walpha[0:17, :],
                                                           start=True, stop=True), [alr_, walpha], [pz[g]])

                def ACT1():
                    for g in range(4):
                        S.op("act", lambda e, g=g: e.activation(aptp[g][:], pz[g][:], AF.Exp, scale=-1.0), [pz[g]], [aptp[g]])
                        S.op("act", lambda e, g=g: e.activation(aptp[g][:], aptp[g][:], AF.Ln, bias=1.0), [aptp[g]], [aptp[g]])

                def R():
                    for g in range(4):
                        pr[g] = nps()
                        S.op("pe", lambda e, g=g: e.matmul(pr[g][:, :], REM, aptp[g][:], start=True, stop=True), [c128, aptp[g]], [pr[g]])
                    pl[0] = nps()
                    for g in range(4):
                        for h in range(4):
                            S.op("pe", lambda e, g=g, h=h: e.matmul(pl[0][:, g * 4 + h:g * 4 + h + 1], aptp[g][:, h * 128:(h + 1) * 128],
                                                                    ones_f[:, 0:1], start=True, stop=True), [aptp[g], ones_f], [pl[0]],
                                 signal=(h == 3))

                def ACT2():
                    for g in range(4):
                        S.op("act", lambda e, g=g: e.activation(E3p[g][:], pr[g][:], AF.Exp, scale=-1.0 / 16), [pr[g]], [E3p[g]])
                        S.op("act", lambda e, g=g: e.activation(eblp[g][:, 0:4], pl[0][:, g * 4:g * 4 + 4], AF.Exp, scale=-1.0 / 16),
                             [pl[0]], [eblp[g]])

                def KH():
                    for g in range(4):
                        S.op("pool", lambda e, g=g: e.tensor_tensor(khp[g][:], ktm_[:, g, :], E3p[g][:], ALU.mult),
                             [ktm_, E3p[g]], [khp[g]])

                def ST(gs):
                    def f():
                        for g in gs:
                            state_update(128, g, khp[g], khp[g], St, lambda h, g=g: (eblp[g], eblp[g][:, h:h + 1]), None, vtm=vtm_)
                    return f
                return [Z, ACT1, R, ACT2, KH, ST((0, 1)), ST((2, 3))]

            A1(0)
            for g in range(4):
                A2a(0, g)
                A2b(0, g)
            A1(1)
            for f in B_pieces(0):
                f()
            for i in range(NPRE):
                cs = C_stages(i)
                nxt = i + 1 < NPRE
                bp = B_pieces(i + 1) if nxt else [lambda: None] * 4
                cs[0]()
                if nxt:
                    A2a(i + 1, 0)
                cs[1]()
                if nxt:
                    A2b(i + 1, 0)
                    A2a(i + 1, 1)
                cs[2]()
                if nxt:
                    A2b(i + 1, 1)
                    A2a(i + 1, 2)
                cs[3]()
                if nxt:
                    A2b(i + 1, 2)
                    A2a(i + 1, 3)
                cs[4]()
                if nxt:
                    A2b(i + 1, 3)
                if i + 2 < NPRE:
                    A1(i + 2)
                bp[0]()
                cs[5]()
                bp[1]()
                cs[6]()
                bp[2]()
                bp[3]()
            S.op("act", lambda e: e.activation(Sb[0][:], St[:], AF.Copy), [St], [Sb[0]])
            fence()
            S.op("dve", lambda e: e.memset(alrT[0:32, :], 1.0), [], [alrT])
        else:
            S.op("dve", lambda e: e.memset(alrT[0:32, :], 1.0), [], [alrT])

        sbi = [0]

        def make_pre(kind, t):
            isp = kind == "p"
            G_ = 128 if isp else 64
            NG_ = 4 if isp else 1
            segs_ = [(0, 128, 0)] if isp else [(16 * s_, 16 * s_ + 16, 1 + s_) for s_ in range(SPC)]

            def load():
                if isp:
                    S.dma_load("sp", xnx, xnx[:], xp[t * N:(t + 1) * N, :].rearrange("(g p) d -> p g d", p=128))
                else:
                    S.dma_load("sp", xnx, xnx[0:64, 0, :], xs)

            def stats():
                for g in range(NG_):
                    S.op("act", lambda e, g=g: e.activation(jkm[g % 2][0:G_, :], xnx[0:G_, g, :], AF.Square,
                                                            accum_out=stat2[0:G_, g:g + 1]), [xnx], [jkm[g % 2], (stat2, g)])
                S.op("act", lambda e: e.activation(stat2[0:G_, 4:4 + NG_], stat2[0:G_, 0:NG_], AF.Ln, bias=EPS, scale=1.0 / D),
                     [stat2], [stat2])
                S.op("act", lambda e: e.activation(stat2[0:G_, 8:8 + NG_], stat2[0:G_, 4:4 + NG_], AF.Exp, scale=-0.5),
                     [stat2], [stat2])

            def a2a(g):
                xn_ = (xn, xn2)[g % 2]
                S.op("act", lambda e: e.activation(xn_[0:G_, :], xnx[0:G_, g, :], AF.Copy, scale=stat2[0:G_, 8 + g:9 + g]),
                     [xnx, stat2], [xn_])

            def a2b(g):
                xn_ = (xn, xn2)[g % 2]
                p = npsb()
                for k in range(8):
                    S.op("pe", lambda e, k=k: e.transpose(p[:, k * 128:k * 128 + G_], xn_[0:G_, k * 128:(k + 1) * 128],
                                                          identb[0:G_, 0:G_]), [xn_, identb], [p], signal=(k == 7))
                for k in range(8):
                    for (c0, c1, r) in segs_:
                        S.op("dve", lambda e, k=k, c0=c0, c1=c1, r=r: e.tensor_scalar(
                            hT[:, k, g * 128 + c0:g * 128 + c1], p[:, k * 128 + c0:k * 128 + c1],
                            AB[:, 0, k, r:r + 1], AB[:, 1, k, r:r + 1], ALU.mult, ALU.add), [p, AB], [(hT, (k, g * 128 + c0))])

            def copy():
                S.dma_load("sp", xt, xt[0:G_, 0:NG_, :], xnx[0:G_, 0:NG_, :], reads=[xnx])
            return dict(load=load, stats=stats, a2a=a2a, a2b=a2b, copy=copy, ng=NG_)

        def layer_tile(kind, t, pre_done=False, next_tile=None):
            is_p = kind == "p"
            NN = N if is_p else SPC * LS
            G = 128 if is_p else 64
            NGR = 4 if is_p else 1
            first_tile = is_p and t == 0
            last_tile = is_p and t == NT - 1
            segs = [(0, 128, 0)] if is_p else [(16 * s, 16 * s + 16, 1 + s) for s in range(SPC)]
            W = 16 + (N if is_p else LS)
            nseq = 1 if is_p else SPC
            U = uT.h[:, 0:4 * nseq * W].rearrange("p (g s w) -> p g s w", s=nseq, w=W)
            if not pre_done:
                if is_p:
                    S.dma_load("sp", xt, xt[:], xp[t * N:(t + 1) * N, :].rearrange("(g p) d -> p g d", p=128))
                else:
                    S.dma_load("sp", xt, xt[0:64, 0, :], xs)
                for g in range(NGR):
                    norm_to_hT((xt, xt[0:G, g, :]), G, g * 128, 1, segs)
            if first_tile:
                norm_to_hT((xht, xht[:]), 16, 0, 1, [(0, 16, 0)], hdst=hTh)
            st, sv = slab("w_in", w_in, O_U, 512)
            for m in range(4):
                p = nps()
                mm_fm(p, 128, st, sv, m * 128, NN)
                S.op("act", lambda e, p=p, m=m: e.activation(U[:, m, :, 16:W], p[:, 0:NN].rearrange("p (s l) -> p s l", s=nseq),
                                                             AF.Copy), [p], [(uT, m)])
            if first_tile:
                p = nps()
                for m in range(4):
                    for k in range(8):
                        S.op("pe", lambda e, m=m, k=k, p=p, sv=sv: e.matmul(p[:, m * 16:(m + 1) * 16], sv[:, k, m * 128:(m + 1) * 128],
                                                                      hTh[:, k, :], start=(k == 0), stop=(k == 7)),
                             [st, hTh], [p], signal=(k == 7))
                S.op("dve", lambda e, p=p: e.tensor_scalar(U[:, :, 0, 0:16], p[:, 0:64].rearrange("p (g w) -> p g w", w=16),
                                                           flg[:, 0:1], None, ALU.mult), [p, flg], [uT])
            if not is_p:
                S.dma_load("sp", cach, cach[0:60, :], cache)
                p = nps()
                for gi in range(4):
                    S.op("pe", lambda e, gi=gi, p=p: e.transpose(p[:, gi * 60:(gi + 1) * 60], cach[0:60, gi * 128:(gi + 1) * 128],
                                                                  identf[0:60, 0:60]), [cach, c128], [p], signal=(gi == 3))
                S.op("dve", lambda e, p=p: e.tensor_copy(U[:, :, :, 1:16], p[:, 0:240].rearrange("p (g s w) -> p g s w", s=4, w=15)),
                     [p], [uT])
            if (not is_p) or last_tile:
                p = nps()
                gl = 0 if not is_p else 3 * 128
                mm_tm(p, G, gl, st, sv, 0, 512)
                S.op("act", lambda e, p=p: e.activation(utm[0:G, :], p[0:G, :], AF.Copy), [p], [utm])
                if is_p:
                    S.dma_store("sp", utm, cp_o, utm[113:128, :])
                else:
                    for s in range(SPC):
                        S.dma_store("sp", utm, cs_o[s * 15:(s + 1) * 15, :], utm[16 * s + 1:16 * s + 16, :])
            st, sv = slab("w_in", w_in, O_Q, 512)
            for m in range(4):
                p = nps()
                mm_fm(p, 128, st, sv, m * 128, NN)
                S.op("act", lambda e, p=p, m=m: e.activation(qT[:, m, 0:NN], p[:, 0:NN], AF.Copy), [p], [(qT, m)])
            st, sv = slab("w_in", w_in, O_K, 512)
            for m in range(4):
                p = nps()
                mm_fm(p, 128, st, sv, m * 128, NN)
                S.op("act", lambda e, p=p, m=m: e.activation(kT[:, m, 0:NN], p[:, 0:NN], AF.Copy), [p], [(kT, m)])
            for g in range(NGR):
                p = nps()
                mm_tm(p, G, g * 128, st, sv, 0, 512)
                S.op("act", lambda e, p=p, g=g: e.activation(ktm[0:G, g, :], p[0:G, :], AF.Copy), [p], [(ktm, g)])
            for hf in range(2):
                st, sv = slab("w_in", w_in, O_V + hf * 512, 512)
                for g in range(NGR):
                    p = nps()
                    mm_tm(p, G, g * 128, st, sv, 0, 512)
                    S.op("act", lambda e, p=p, g=g, hf=hf: e.activation(vtm[0:G, g, hf * 512:(hf + 1) * 512], p[0:G, :], AF.Copy),
                         [p], [(vtm, (g, hf))])
            for hf in range(2):
                st, sv = slab("w_in", w_in, O_G + hf * 512, 512)
                for m in range(4):
                    p = nps()
                    mm_fm(p, 128, st, sv, m * 128, NN)
                    S.op("act", lambda e, p=p, m=m, hf=hf: e.activation(sgT[:, hf * 4 + m, 0:NN], p[:, 0:NN], AF.Silu), [p], [(sgT, hf * 4 + m)])
            st, sv = slab("w_in", w_in, O_ALR, 16)
            p = nps()
            mm_fm(p, 16, st, sv, 0, NN)
            S.op("dve", lambda e: e.memset(alrT[0:32, :], 1.0), [], [alrT])
            S.op("act", lambda e, p=p: e.activation(alrT[0:16, 0:NN], p[0:16, 0:NN], AF.Copy), [p], [alrT])

            pool_branch(nseq, N if is_p else LS, NN, first_tile)
            if is_p:
                S.op("dve", lambda e: e.tensor_copy(U[:, :, 0, 0:16], U[:, :, 0, N:N + 16]), [uT], [uT])
            for g in range(NGR):
                if is_p:
                    gla_decay(128, g * 128, CUM, REM, True)
                    cur = Sb[sbi[0] % 2]
                    nxt = Sb[(sbi[0] + 1) % 2]
                    sbi[0] += 1
                    gla_out(128, g * 128, g, CUM4[:], [(0, 128, cur)], None)
                    state_update(128, g, khat, khat, St, lambda h: (E1, E1[:, h, 127:128]), nxt)
                else:
                    gla_decay(64, 0, CUMs, REMs, True)
                    for s in range(SPC):
                        S.dma_load("sp", St, St[:], s0[s].rearrange("h c v -> c h v"))
                        S.op("act", lambda e, s=s: e.activation(Sb[s][:], St[:], AF.Copy), [St], [Sb[s]])
                    gla_out(64, 0, 0, CUM4s[:], [(16 * s, 16 * s + 16, Sb[s]) for s in range(SPC)], None)
                    for s in range(SPC):
                        S.dma_load("sp", St, St[:], s0[s].rearrange("h c v -> c h v"))
                        S.op("dve", lambda e, s=s: e.tensor_scalar(khs[0:64, :], khat[0:64, :], SEGM[:, s:s + 1], None, ALU.mult),
                             [khat, c64], [khs])
                        state_update(64, 0, khs, khs, St, lambda h, s=s: (E1, E1[:, h, 16 * s + 15:16 * s + 16]), None)
                        S.dma_store("sp", St, ss_o[s].rearrange("h c v -> c h v"), St[:])
            if last_tile:
                S.dma_store("sp", St, sp_o.rearrange("h c v -> c h v"), St[:])
                out_tiles.append(St)
            fence()

            st, sv = slab("w_pa", w_pa, 0, D, K=4)
            sga = []
            for hf in range(2):
                stg, svg = slab("w_in", w_in, O_GA + hf * 512, 512)
                for m in range(4):
                    mm = hf * 4 + m
                    pg = nps()
                    mm_fm(pg, 128, stg, svg, m * 128, NN)
                    r = rl[mm % 2]
                    S.op("act", lambda e, pg=pg, r=r: e.activation(r[:, 0:NN], pg[:, 0:NN], AF.Sigmoid), [pg], [r])
                    pa = nps()
                    mm_fm(pa, 128, st, sv, mm * 128, NN, rhs_t=aoT, K=4)
                    S.op("dve", lambda e, pa=pa, r=r, mm=mm: e.tensor_tensor(mgA[:, mm, 0:NN], pa[:, 0:NN], r[:, 0:NN], ALU.mult),
                         [pa, r], [(mgA, mm)])
            wpb = [slab("w_pb", w_pb, hf * 512, 512) for hf in range(2)]
            for hf in range(2):
                stg, svg = slab("w_in", w_in, O_GB + hf * 512, 512)
                st, sv = wpb[hf]
                for m in range(4):
                    mm = hf * 4 + m
                    pg = nps()
                    mm_fm(pg, 128, stg, svg, m * 128, NN)
                    r = rl[mm % 2]
                    S.op("act", lambda e, pg=pg, r=r: e.activation(r[:, 0:NN], pg[:, 0:NN], AF.Sigmoid), [pg], [r])
                    pb = nps()
                    mm_fm(pb, 128, st, sv, m * 128, NN, rhs_t=boT)
                    r2 = rt[mm % 2]
                    S.op("dve", lambda e, pb=pb, r=r, r2=r2: e.tensor_tensor(r2[:, 0:NN], pb[:, 0:NN], r[:, 0:NN], ALU.mult),
                         [pb, r], [r2])
                    if MODE == "pool_only":
                        S.op("dve", lambda e, mm=mm: e.tensor_copy(mrg[:, mm, 0:NN], mgA[:, mm, 0:NN]), [mgA], [mrg])
                    elif MODE == "gla_only":
                        S.op("dve", lambda e, r2=r2, mm=mm: e.tensor_copy(mrg[:, mm, 0:NN], r2[:, 0:NN]), [r2], [mrg])
                    else:
                        S.op("dve", lambda e, r2=r2, mm=mm: e.tensor_tensor(mrg[:, mm, 0:NN], r2[:, 0:NN], mgA[:, mm, 0:NN], ALU.add),
                             [r2, mgA], [mrg])
            for hf in range(2):
                st, sv = slab("w_out", w_out, hf * 512, 512)
                for g in range(NGR):
                    p = nps()
                    mm_tm(p, G, g * 128, st, sv, 0, 512, lhs_t=mrg)
                    r2 = rt[(hf * NGR + g) % 2]
                    S.op("dve", lambda e, p=p, r2=r2, hf=hf: e.tensor_tensor(r2[0:G, :], p[0:G, :], G1[0:G, hf * 512:(hf + 1) * 512],
                                                                             ALU.mult), [p, G1], [r2])
                    if "nomix" not in MODE:
                        S.op("dve", lambda e, r2=r2, g=g, hf=hf: e.tensor_tensor(xt[0:G, g, hf * 512:(hf + 1) * 512],
                                                                                 xt[0:G, g, hf * 512:(hf + 1) * 512], r2[0:G, :], ALU.add),
                             [xt, r2], [xt])
            fence()

            pre = make_pre(*next_tile) if next_tile is not None else None
            if pre:
                pre["load"]()
            for g in range(NGR):
                norm_to_hT((xt, xt[0:G, g, :]), G, g * 128, 2, segs)
            if DEBUG and is_p and t == 0:
                dump(S, "h2T", hT, hT[:, :, 0:128], 128, 1024, inner=128)
            for sl in range(8):
                st, sv = slab("w_ff1", w_ff1, sl * 512, 512)
                for m in range(4):
                    p = nps()
                    mm_fm(p, 128, st, sv, m * 128, NN)
                    r = rl[m % 2]
                    S.op("act", lambda e, p=p, r=r: e.activation(r[:, 0:NN], p[:, 0:NN], AF.Relu), [p], [r])
                    S.op("dve", lambda e, r=r, sl=sl, m=m: e.tensor_tensor(ffT[:, sl * 4 + m, 0:NN], r[:, 0:NN], r[:, 0:NN], ALU.mult),
                         [r], [(ffT, sl * 4 + m)])
            if DEBUG and is_p and t == 0:
                dump(S, "ffT", ffT, ffT[:, 0:8, 0:128], 128, 1024, inner=128)
            if pre:
                pre["stats"]()
                pre["a2a"](0)
            pstep = [0]

            def pre_step():
                if not pre or pstep[0] >= pre["ng"]:
                    return
                g_ = pstep[0]
                pstep[0] += 1
                pre["a2b"](g_)
                if g_ + 1 < pre["ng"]:
                    pre["a2a"](g_ + 1)
            for hf in range(2):
                acc = [nps() for _ in range(NGR)]
                for kp in range(4):
                    st, sv = slab("w_ff2", w_ff2, hf * 512, 512, 8, kp * 1024)
                    for g in range(NGR):
                        mm_tm(acc[g], G, g * 128, st, sv, 0, 512, lhs_t=ffT, K=8, k0=kp * 8, first=(kp == 0), last=(kp == 3))
                    if kp % 2 == 1:
                        pre_step()
                for g in range(NGR):
                    r2 = rt[g % 2]
                    S.op("dve", lambda e, g=g, r2=r2, hf=hf, acc=acc: e.tensor_tensor(r2[0:G, :], acc[g][0:G, :],
                                                                             G2[0:G, hf * 512:(hf + 1) * 512], ALU.mult),
                         [acc[g], G2], [r2])
                    if "noffn" not in MODE:
                        S.op("dve", lambda e, r2=r2, g=g, hf=hf: e.tensor_tensor(xt[0:G, g, hf * 512:(hf + 1) * 512],
                                                                                 xt[0:G, g, hf * 512:(hf + 1) * 512], r2[0:G, :], ALU.add),
                             [xt, r2], [xt])
            for g in range(NGR):
                S.op("act", lambda e, g=g: e.activation(jkm[g % 2][0:G, :], xt[0:G, g, :], AF.Square, accum_out=stat[0:G, 4 + g:5 + g]),
                     [xt], [jkm[g % 2], (stat, g)])
            S.op("act", lambda e: e.activation(stat[0:G, 8:8 + NGR], stat[0:G, 4:4 + NGR], AF.Ln, bias=EPS, scale=1.0 / D), [stat], [stat])
            S.op("act", lambda e: e.activation(stat[0:G, 12:12 + NGR], stat[0:G, 8:8 + NGR], AF.Exp, scale=-0.5), [stat], [stat])
            for g in range(NGR):
                yo_ = (yo, yo2)[g % 2]
                S.op("dve", lambda e, g=g, yo_=yo_: e.scalar_tensor_tensor(yo_[0:G, :], xt[0:G, g, :], stat[0:G, 12 + g:13 + g], FG[0:G, :],
                                                                           ALU.mult, ALU.mult), [xt, stat, FG], [yo_])
                if is_p:
                    S.dma_store("sp", yo_, yp[t * N + g * 128:t * N + (g + 1) * 128, :], yo_[:])
                else:
                    S.dma_store("sp", yo_, ys, yo_[0:64, :])
            if pre:
                while pstep[0] < pre["ng"]:
                    pre_step()
                pre["copy"]()
            fence()

        tiles = [("p", t) for t in range(NT)] + [("s", 0)]
        for ti, (kind, t) in enumerate(tiles):
            if kind == "s":
                build_gates(selt.h[0:5, 128:192], 64)
            layer_tile(kind, t, pre_done=(ti > 0), next_tile=(tiles[ti + 1] if ti + 1 < len(tiles) else None))

        if DEBUG:
            dump(S, "gtm1", gtm1, gtm1[:], 5, D)
            dump(S, "selt", selt, selt[:], 5, 192)
            dump(S, "G1", G1, G1[:], 128, D)
            dump(S, "G2", G2, G2[:], 128, D)
            dump(S, "AB", AB, AB[:].rearrange("p a b c -> p (a b c)"), 128, 160)
            dump(S, "vecs", vecs, vecs[:], 128, 24)
            out_tiles.append(dbg_stage[0])
        out_tiles += [yo, yo2, utm, St]
        S.finish(out_tiles)
        S.emit_all()
        print("ops", S.nops, "waits", S.nwaits, "sems", len(S.sems))
    return nc


def _consts():
    ident = np.eye(128, dtype=np.float32)
    cum = np.triu(np.ones((128, 128), np.float32))
    rem = np.tril(np.ones((128, 128), np.float32), -1)
    cst128 = np.concatenate([ident, cum, rem], axis=1)
    seg = np.arange(64) // 16
    same = (seg[:, None] == seg[None, :]).astype(np.float32)
    cums = np.triu(np.ones((64, 64), np.float32)) * same
    rems = np.tril(np.ones((64, 64), np.float32), -1) * same
    segm = (seg[:, None] == np.arange(4)[None, :]).astype(np.float32)
    cst64 = np.concatenate([cums, rems, segm], axis=1)
    sel = np.zeros((5, 192), np.float32)
    sel[0, 0:128] = 1.0
    for s in range(4):
        sel[1 + s, 128 + 16 * s:128 + 16 * (s + 1)] = 1.0
    return cst128, cst64, sel


_NC_CACHE = {}


def kernel(x_prompt, x_sample, c_prompt, c_sample, state_gla, cache_pool, w_ada, b_ada,
           norm1_g, w_in, w_alpha, b_alpha, w_pool, pool_scale, gla_norm_g, w_pa, w_pb,
           w_out, norm2_g, w_ff1, w_ff2, final_g):
    f = lambda a: np.ascontiguousarray(np.asarray(a, dtype=np.float32))
    x_prompt, x_sample, c_prompt, c_sample = f(x_prompt), f(x_sample), f(c_prompt), f(c_sample)
    state_gla, cache_pool = f(state_gla), f(cache_pool)
    cst128, cst64, sel = _consts()
    shared = dict(
        w_ada=f(w_ada)[0], b_ada=f(b_ada), norm1_g=f(norm1_g), w_in=f(w_in)[0], w_alpha=f(w_alpha)[0],
        b_alpha=f(b_alpha), w_pool=f(w_pool)[0], pool_scale=f(pool_scale), gla_norm_g=f(gla_norm_g),
        w_pa=f(w_pa)[0], w_pb=f(w_pb)[0], w_out=f(w_out)[0], norm2_g=f(norm2_g), w_ff1=f(w_ff1)[0],
        w_ff2=f(w_ff2)[0], final_g=f(final_g).reshape(1, D), cst128=cst128, cst64=cst64, sel=sel)
    in_maps = []
    for c in range(NCORES):
        t0 = c * TPC
        xh = np.zeros((16, D), np.float32)
        if c > 0:
            xh[:] = x_prompt[0, t0 - 16:t0]
        invc = np.zeros((128, 64), np.float32)
        for gi in range(4):
            w = 2 << gi
            pos = t0 + np.arange(16)
            invc[:, gi * 16:(gi + 1) * 16] = (1.0 / np.minimum(pos + 1, w))[None, :]
        flags = np.zeros((128, 16), np.float32)
        flags[:, 0] = 0.0 if c == 0 else 1.0
        xpre = None
        if CARRY == "prefix":
            xpre = np.zeros((7 * TPC, D), np.float32)
            for j in range(7):
                b = c - 7 + j
                if b >= 0:
                    xpre[j * TPC:(j + 1) * TPC] = x_prompt[0, b * TPC:(b + 1) * TPC]
                    flags[:, 1 + j] = 1.0
        else:
            flags[:, 1 + c] = 1.0
        m = dict(shared)
        m.update(
            xp=x_prompt[0, t0:t0 + TPC], xh=xh, xs=x_sample[c * SPC:(c + 1) * SPC].reshape(SPC * LS, D),
            cvec=np.concatenate([c_prompt, c_sample[c * SPC:(c + 1) * SPC]], axis=0),
            s0=state_gla[0, c * SPC:(c + 1) * SPC], cache=cache_pool[0, c * SPC:(c + 1) * SPC].reshape(SPC * 15, 512),
            invcnt=invc, flags=flags)
        if xpre is not None:
            m["xpre"] = xpre
        in_maps.append({k: np.ascontiguousarray(v) for k, v in m.items()})
    if "nc" not in _NC_CACHE:
        _NC_CACHE["nc"] = build()
    nc = _NC_CACHE["nc"]
    res = run_bass_kernel_spmd(nc, in_maps, core_ids=list(range(NCORES)))
    R = res.results
    _NC_CACHE['last'] = R
    y_prompt = np.concatenate([R[c]["yp"] for c in range(NCORES)], axis=0)[None]
    y_sample = np.concatenate([R[c]["ys"].reshape(SPC, LS, D) for c in range(NCORES)], axis=0)
    st_p = R[NCORES - 1]["sp_o"][None, None]
    ch_p = R[NCORES - 1]["cp_o"][None, None]
    st_s = np.concatenate([R[c]["ss_o"] for c in range(NCORES)], axis=0)[None]
    ch_s = np.concatenate([R[c]["cs_o"].reshape(SPC, 15, 512) for c in range(NCORES)], axis=0)[None]
    return (y_prompt.astype(np.float32), y_sample.astype(np.float32), st_p.astype(np.float32),
            ch_p.astype(np.float32), st_s.astype(np.float32), ch_s.astype(np.float32))
```

```python
import contextlib
import numpy as np
import concourse.bass as bass
import concourse.mybir as mybir
from concourse.bass_utils import run_bass_kernel_spmd

F32 = mybir.dt.float32
BF16 = mybir.dt.bfloat16
ALU = mybir.AluOpType
AF = mybir.ActivationFunctionType

NCORES = 8
D = 1024
SEQ = 16384
TPC = SEQ // NCORES
NT = 4
N = 512
SPC = 4
LS = 16
INW = 5648
EPS = 1e-6
O_U, O_Q, O_K, O_V, O_G, O_ALR, O_GA, O_GB = 0, 512, 1024, 1536, 2560, 3584, 3600, 4624

ENGS = ("pe", "act", "dve", "pool", "sp")
WITH_EXCHANGE = False
STRICT_SAME_ENGINE = True
CARRY = "prefix"
NPRE = 7 * NT
DEBUG = False
MODE = "full"


class TT:
    def __init__(self, name, h):
        self.name = name
        self.h = h
        self.w = {}
        self.r = {}
        self.kw = {}
        self.kr = {}
        self.dsem = None
        self.dcnt = 0
        self.ssem = None
        self.scnt = 0

    def __getitem__(self, idx):
        return self.h[idx]


class TTv:
    def __init__(self, base):
        self.__dict__ = base.__dict__.copy()
        self.base = base
        self.h = base.h[:].bitcast(F32)
        for k in ("w", "r", "kw", "kr"):
            setattr(self, k, getattr(base, k))

    def __getitem__(self, idx):
        return self.h[idx]


class Sched:
    def __init__(self, nc, es):
        self.nc = nc
        self.es = es
        self.sems = {}
        self.cnt = {e: 0 for e in ENGS}
        self.seen = {e: {} for e in ENGS}
        self.prog = {e: [] for e in ENGS}
        self.nwaits = 0
        self.nops = 0
        for e in ENGS:
            self._sem("E_" + e)

    def _sem(self, name):
        if name not in self.sems:
            self.sems[name] = self.es.enter_context(self.nc.semaphore(name))
        return name

    def sb(self, name, shape, dt=F32):
        h = self.es.enter_context(self.nc.sbuf_tensor(name, list(shape), dt))
        return TT(name, h)

    def ps(self, name, shape, dt=F32):
        h = self.es.enter_context(self.nc.psum_tensor(name, list(shape), dt))
        return TT(name, h)

    @staticmethod
    def _split(lst):
        tts, keys = [], []
        for x in lst:
            if isinstance(x, tuple):
                tts.append(x[0])
                keys.append(x[1])
            else:
                tts.append(x)
                keys.append(None)
        return tts, keys

    def _waits(self, e, reads, writes, rkeys=None, wkeys=None):
        waits = {}
        own = "E_" + e
        rkeys = rkeys or [None] * len(reads)
        wkeys = wkeys or [None] * len(writes)

        def need(nm, v):
            if self.seen[e].get(nm, 0) >= v:
                return
            if waits.get(nm, 0) < v:
                waits[nm] = v

        same = e not in ("pe", "sp")
        for t in reads:
            for nm, v in t.w.items():
                if nm == own:
                    if same:
                        need(nm, v)
                else:
                    need(nm, v)
        for t, key in zip(writes, wkeys):
            for dct, kd in ((t.w, t.kw), (t.r, t.kr)):
                for nm, v in dct.items():
                    if nm == own:
                        if same and STRICT_SAME_ENGINE:
                            if key is None:
                                need(nm, v)
                            else:
                                vv = max(kd.get(key, {}).get(own, 0), kd.get(None, {}).get(own, 0))
                                if vv:
                                    need(nm, vv)
                    else:
                        need(nm, v)
        for nm, v in waits.items():
            self.seen[e][nm] = v
        return list(waits.items())

    def op(self, e, fn, reads=(), writes=(), signal=True):
        reads, rkeys = self._split(reads)
        writes, wkeys = self._split(writes)
        waits = self._waits(e, reads, writes, rkeys, wkeys)
        own = "E_" + e
        if signal:
            self.cnt[e] += 1
            val = self.cnt[e]
        else:
            val = self.cnt[e] + 1
        for t, key in zip(reads, rkeys):
            if t.r.get(own, 0) < val:
                t.r[own] = val
            t.kr.setdefault(key, {})[own] = val
        for t, key in zip(writes, wkeys):
            if t.w.get(own, 0) < val:
                t.w[own] = val
            t.kw.setdefault(key, {})[own] = val
        sems = self.sems
        self.nwaits += len(waits)
        self.nops += 1

        def emit(eng):
            for nm, v in waits:
                eng.wait_ge(sems[nm], v)
            ins = fn(eng)
            if signal:
                ins.then_inc(sems[own], 1)

        self.prog[e].append(emit)

    def dma_load(self, q, dst, dst_ap, src_ap, reads=(), **kw):
        if dst.dsem is None:
            dst.dsem = self._sem("L_" + dst.name)
        waits = self._waits(q, reads, [dst])
        dst.dcnt += 1
        val = 16 * dst.dcnt
        dst.w[dst.dsem] = val
        for t in reads:
            t.r[dst.dsem] = val
        sems = self.sems
        sem = sems[dst.dsem]
        self.nwaits += len(waits)

        def emit(eng):
            for nm, v in waits:
                eng.wait_ge(sems[nm], v)
            eng.dma_start(out=dst_ap, in_=src_ap, **kw).then_inc(sem, 16)

        self.prog[q].append(emit)

    def dma_store(self, q, src, dst_ap, src_ap, dram_tt=None, **kw):
        if src.ssem is None:
            src.ssem = self._sem("S_" + src.name)
        waits = self._waits(q, [src], [])
        src.scnt += 1
        val = 16 * src.scnt
        src.r[src.ssem] = val
        if dram_tt is not None:
            dram_tt.w[src.ssem] = val
        sems = self.sems
        sem = sems[src.ssem]
        self.nwaits += len(waits)

        def emit(eng):
            for nm, v in waits:
                eng.wait_ge(sems[nm], v)
            eng.dma_start(out=dst_ap, in_=src_ap, **kw).then_inc(sem, 16)

        self.prog[q].append(emit)

    def fence(self, tiles):
        allev = {}
        for t in tiles:
            for dct in (t.w, t.r):
                for nm, v in dct.items():
                    if allev.get(nm, 0) < v:
                        allev[nm] = v
        for t in tiles:
            kwn = t.kw.setdefault(None, {})
            krn = t.kr.setdefault(None, {})
            for nm, v in allev.items():
                if t.r.get(nm, 0) < v:
                    t.r[nm] = v
                if t.w.get(nm, 0) < v:
                    t.w[nm] = v
                if kwn.get(nm, 0) < v:
                    kwn[nm] = v
                if krn.get(nm, 0) < v:
                    krn[nm] = v

    def raw(self, e, fn):
        self.prog[e].append(fn)

    def finish(self, out_tiles):
        sems = self.sems
        waits = []
        for t in out_tiles:
            if t.ssem is not None:
                waits.append((t.ssem, 16 * t.scnt))
        for e in ENGS:
            if e != "sp" and self.cnt[e] > 0:
                waits.append(("E_" + e, self.cnt[e]))

        def emit(eng):
            for nm, v in waits:
                eng.wait_ge(sems[nm], v)

        self.prog["sp"].append(emit)

    def emit_all(self):
        nc = self.nc
        prog = self.prog
        with nc.Block() as block:

            @block.tensor
            def _(eng):
                for f in prog["pe"]:
                    f(eng)

            @block.scalar
            def _(eng):
                for f in prog["act"]:
                    f(eng)

            @block.vector
            def _(eng):
                for f in prog["dve"]:
                    f(eng)

            @block.gpsimd
            def _(eng):
                for f in prog["pool"]:
                    f(eng)

            @block.sync
            def _(eng):
                for f in prog["sp"]:
                    f(eng)


def build():
    nc = bass.Bass("TRN2", target_bir_lowering=False)

    def din(name, shape):
        return nc.dram_tensor(name, list(shape), F32, kind="ExternalInput").ap()

    def dout(name, shape):
        return nc.dram_tensor(name, list(shape), F32, kind="ExternalOutput").ap()

    xp = din("xp", [TPC, D])
    xpre = din("xpre", [7 * TPC, D]) if CARRY == "prefix" else None
    xh = din("xh", [16, D])
    xs = din("xs", [SPC * LS, D])
    cvec = din("cvec", [5, D])
    s0 = din("s0", [SPC, 4, 128, 256])
    cache = din("cache", [SPC * 15, 512])
    w_ada = din("w_ada", [D, 6 * D])
    b_ada = din("b_ada", [1, 6 * D])
    norm1_g = din("norm1_g", [1, D])
    w_in = din("w_in", [D, INW])
    w_alpha = din("w_alpha", [16, 512])
    b_alpha = din("b_alpha", [1, 512])
    w_pool = din("w_pool", [4, 128, 128])
    pool_scale = din("pool_scale", [1, 512])
    gla_norm_g = din("gla_norm_g", [1, 256])
    w_pa = din("w_pa", [512, D])
    w_pb = din("w_pb", [D, D])
    w_out = din("w_out", [D, D])
    norm2_g = din("norm2_g", [1, D])
    w_ff1 = din("w_ff1", [D, 4 * D])
    w_ff2 = din("w_ff2", [4 * D, D])
    final_g = din("final_g", [1, D])
    cst128 = din("cst128", [128, 384])
    cst64 = din("cst64", [64, 132])
    sel = din("sel", [5, 192])
    invcnt = din("invcnt", [128, 64])
    flags = din("flags", [128, 16])

    yp = dout("yp", [TPC, D])
    ys = dout("ys", [SPC * LS, D])
    sp_o = dout("sp_o", [4, 128, 256])
    cp_o = dout("cp_o", [15, 512])
    ss_o = dout("ss_o", [SPC, 4, 128, 256])
    cs_o = dout("cs_o", [SPC * 15, 512])
    gin = nc.dram_tensor("gin", [128, 1028], F32, kind="Internal").ap()
    gout = nc.dram_tensor("gout", [NCORES * 128, 1028], F32, kind="Internal").ap()
    wscr = nc.dram_tensor("wscr", [40, 128, 4096], BF16, kind="Internal").ap()

    dbg_outs = {}

    def dump(S_, name, tt, ap, rows, cols, inner=None):
        o = nc.dram_tensor("dbg_" + name, [rows, cols], F32, kind="ExternalOutput").ap()
        dbg_outs[name] = o
        stg = dbg_stage[0]
        sv_ = stg[0:rows, 0:cols]
        if inner is not None:
            sv_ = sv_.rearrange("p (a b) -> p a b", b=inner)
        S_.op("dve", lambda e: e.tensor_copy(sv_, ap), [tt], [stg])
        S_.dma_store("sp", stg, o, stg[0:rows, 0:cols])

    dbg_stage = [None]
    with contextlib.ExitStack() as es:
        S = Sched(nc, es)
        es.enter_context(nc.allow_non_contiguous_dma("tiny vector re-layouts"))
        out_tiles = []

        xt = S.sb("xt", [128, 4, D])
        xn = S.sb("xn", [128, D], BF16)
        hT = S.sb("hT", [128, 8, N], BF16)
        NSLOT = 4
        ring = [S.sb("ring%d" % i, [128, 4096], BF16) for i in range(NSLOT)]
        G1 = S.sb("G1", [128, D])
        G2 = S.sb("G2", [128, D])
        FG = S.sb("FG", [128, D])
        St = S.sb("St", [128, 4, 256])
        Sb = [S.sb("Sb%d" % i, [128, 4, 256], BF16) for i in range(4)]
        yo = S.sb("yo", [128, D])
        if DEBUG:
            dbg_stage[0] = S.sb("dbgstg", [128, D])
        c128 = S.sb("c128", [128, 384])
        identb = S.sb("identb", [128, 128], BF16)
        CUM4 = S.sb("CUM4", [128, 4, 128])
        c64 = S.sb("c64", [64, 132])
        CUM4s = S.sb("CUM4s", [64, 4, 64])
        ones_f = S.sb("ones_f", [128, 128])
        ones_b = S.sb("ones_b", [128, 128], BF16)
        walpha = S.sb("walpha", [17, 512])
        balpha = S.sb("balpha", [1, 512])
        wpool = S.sb("wpool", [128, 4, 128], BF16)
        vecs = S.sb("vecs", [128, 24])
        modT = S.sb("modT", [128, 6, 8, 5])
        AB = S.sb("AB", [128, 4, 8, 5])
        gtm1 = S.sb("gtm1", [5, D])
        gtm2 = S.sb("gtm2", [5, D])
        selt = S.sb("selt", [5, 192])
        invc = S.sb("invc", [128, 64])
        flg = S.sb("flg", [128, 16])
        stat = S.sb("stat", [128, 16])
        stat2 = S.sb("stat2", [128, 16])
        Lsum = S.sb("Lsum", [128, 4])
        ebl = S.sb("ebl", [128, 8])
        xht = S.sb("xht", [16, D])
        hTh = S.sb("hTh", [128, 8, 16], BF16)

        AR_WORDS = 24480
        arena = S.sb("arena", [128, AR_WORDS])
        AA = arena.h[:]
        arena_tts = []
        off = [0]

        def carve(name, words, dt, shape=None, at=None):
            o = off[0] if at is None else at
            v = AA[:, o:o + words]
            if dt == BF16:
                v = v.bitcast(BF16)
            if shape is not None and len(shape) == 2:
                v = v.rearrange("p (a b) -> p a b", b=shape[1])
            elif shape is not None and len(shape) == 3:
                v = v.rearrange("p (a b c) -> p a b c", b=shape[1], c=shape[2])
            if at is None:
                off[0] += words
            t = TT(name, v)
            arena_tts.append(t)
            return t, o

        uT, _ = carve("uT", 2112, F32)
        qT, o_q = carve("qT", 2048, F32, (4, N))
        kT, _ = carve("kT", 2048, F32, (4, N))
        ktm, o_ktm = carve("ktm", 2048, F32, (4, 512))
        vtm, _ = carve("vtm", 2048, BF16, (4, 1024))
        sgT, o_sg = carve("sgT", 2048, BF16, (8, N))
        alrT, o_alr = carve("alrT", 512, F32)
        apt, _ = carve("apt", 512, F32)
        E3, _ = carve("E3", 512, F32)
        khat, _ = carve("khat", 256, BF16)
        khs, _ = carve("khs", 256, BF16)
        E1, _ = carve("E1", 512, F32, (4, 128))
        E2, _ = carve("E2", 512, F32, (4, 128))
        qtil, _ = carve("qtil", 256, BF16, (4, 128))
        ktil, _ = carve("ktil", 256, BF16, (4, 128))
        scm, _ = carve("scm", 256, BF16, (4, 128))
        sq, _ = carve("sq", 512, BF16, (8, 128))
        rstT, _ = carve("rstT", 512, F32, (4, 128))
        otmp, _ = carve("otmp", 256, F32, (2, 128))
        boT, o_bo = carve("boT", 2048, BF16, (8, N))
        dT, _ = carve("dT", 1024, BF16, (4, N))
        aoT, _ = carve("aoT", 1024, BF16, (4, N))
        tA, _ = carve("tA", 528, F32)
        tB, _ = carve("tB", 528, F32)
        utm, _ = carve("utm", 512, F32)
        aptB, _ = carve("aptB", 512, F32)
        khatB, _ = carve("khatB", 256, BF16)
        qtilB, _ = carve("qtilB", 256, BF16, (4, 128))
        ktilB, _ = carve("ktilB", 256, BF16, (4, 128))
        eblA, _ = carve("eblA", 4, F32)
        eblB, _ = carve("eblB", 4, F32)
        assert off[0] <= AR_WORDS, off[0]
        mgA, _ = carve("mgA", 4096, F32, (8, N), at=o_q)
        mrg, _ = carve("mrg", 2048, BF16, (8, N), at=o_ktm)
        ffT, _ = carve("ffT", 8192, BF16, (32, N), at=o_q)
        rl = [carve("rl%d" % i, 512, F32, at=o_sg + 512 * i)[0] for i in range(2)]
        rt = [carve("rt%d" % i, 512, F32, at=o_sg + 1024 + 512 * i)[0] for i in range(2)]
        xnx, _ = carve("xnx", 4096, F32, (4, D), at=o_alr)
        xn2, _ = carve("xn2", 512, BF16, at=o_alr + 4096)
        jkm = [carve("jkm%d" % i, 512, BF16, at=o_alr + 4608 + 512 * i)[0] for i in range(2)]
        yo2, _ = carve("yo2", 1024, F32, at=o_alr + 5632)
        assert o_alr + 6656 <= AR_WORDS
        cl, _ = carve("cl", 1024, F32, at=0)
        scl, _ = carve("scl", 1024, F32, at=1024)
        mpart, _ = carve("mpart", 1024, F32, at=2048)
        badab, _ = carve("badab", 1024, F32, at=3072)
        scT, _ = carve("scT", 20, BF16, (8, 5), at=4096)
        gbuf, _ = carve("gbuf", 1028, F32, at=4200)
        gb2 = [carve("gb2%d" % i, 1028, F32, at=5300 + 1100 * i)[0] for i in range(2)]
        Tacc, _ = carve("Tacc", 1024, F32, (4, 256), at=7600)
        Dj, _ = carve("Dj", 4, F32, at=8700)
        cach, _ = carve("cach", 512, F32, at=o_bo)

        def fence():
            S.fence(arena_tts)

        PSB = [S.ps("psb%d" % i, [128, 2 * N], BF16) for i in range(2)]
        PS = [S.ps("ps%d" % i, [128, N]) for i in range(6)]
        psc = [0, 0]

        def nps():
            t = PS[psc[0] % 6]
            psc[0] += 1
            return t

        def npsb():
            t = PSB[psc[1] % 2]
            psc[1] += 1
            return t

        slotc = [0]

        scr = {}

        def slab(wname, w, c0, C, K=8, r0=0, cache=True):
            t = ring[slotc[0] % NSLOT]
            slotc[0] += 1
            flat = t.h[:, 0:K * C]
            v = flat.rearrange("p (k c) -> p k c", c=C)
            key = (wname, c0, C, K, r0)
            if not cache:
                S.dma_load("pool", t, v, wsl(w, c0, C, K, r0))
            elif key not in scr:
                idx = len(scr)
                dtt = TT("scr%d" % idx, None)
                scr[key] = (idx, dtt)
                S.dma_load("pool", t, v, wsl(w, c0, C, K, r0))
                S.dma_store("sp", t, wscr[idx, :, 0:K * C], flat, dram_tt=dtt)
            else:
                idx, dtt = scr[key]
                S.dma_load("pool", t, flat, wscr[idx, :, 0:K * C], reads=[dtt])
            return t, v

        def wsl(w, c0, C, K=8, r0=0):
            return w[r0:r0 + K * 128, c0:c0 + C].rearrange("(k p) n -> p k n", p=128)

        S.dma_load("sp", c128, c128[:], cst128)
        S.dma_load("sp", c64, c64[:], cst64)
        S.dma_load("sp", selt, selt[:], sel)
        S.dma_load("sp", invc, invc[:], invcnt)
        S.dma_load("sp", flg, flg[:], flags)
        S.dma_load("sp", walpha, walpha[0:16, :], w_alpha)
        S.dma_load("sp", walpha, walpha[16:17, :], b_alpha)
        S.dma_load("sp", balpha, balpha[:], b_alpha)
        S.dma_load("sp", FG, FG[:], final_g.partition_broadcast(128))
        S.dma_load("sp", vecs, vecs[:, 0:8], norm1_g.rearrange("o (k p) -> p (o k)", p=128))
        S.dma_load("sp", vecs, vecs[:, 8:16], norm2_g.rearrange("o (k p) -> p (o k)", p=128))
        S.dma_load("sp", vecs, vecs[:, 16:20], pool_scale.rearrange("o (k p) -> p (o k)", p=128))
        S.dma_load("sp", vecs, vecs[:, 20:22], gla_norm_g.rearrange("o (k p) -> p (o k)", p=128))
        S.dma_load("pool", wpool, wpool[:], w_pool.rearrange("g c d -> c g d"))
        S.dma_load("sp", cl, cl[0:5, :], cvec)
        S.dma_load("sp", xht, xht[:], xh)
        identf = c128.h[:, 0:128]
        CUM = c128.h[:, 128:256]
        REM = c128.h[:, 256:384]
        CUMs = c64.h[:, 0:64]
        REMs = c64.h[:, 64:128]
        SEGM = c64.h[:, 128:132]
        S.op("dve", lambda e: e.tensor_copy(identb[:], identf), [c128], [identb])
        for h in range(4):
            S.op("dve", lambda e, h=h: e.tensor_copy(CUM4[:, h, :], CUM), [c128], [CUM4])
            S.op("dve", lambda e, h=h: e.tensor_copy(CUM4s[:, h, :], CUMs), [c64], [CUM4s])
        S.op("dve", lambda e: e.memset(ones_f[:], 1.0), [], [ones_f])
        S.op("dve", lambda e: e.memset(ones_b[:], 1.0), [], [ones_b])
        S.op("dve", lambda e: e.memset(St[:], 0.0), [], [St])
        S.op("dve", lambda e: e.memset(Sb[0][:], 0.0), [], [Sb[0]])
        S.op("dve", lambda e: e.memset(Lsum[:], 0.0), [], [Lsum])

        S.op("act", lambda e: e.activation(scl[0:5, :], cl[0:5, :], AF.Silu), [cl], [scl])
        p = nps()
        for k in range(8):
            S.op("pe", lambda e, k=k, p=p: e.transpose(p[:, k * 5:(k + 1) * 5], scl[0:5, k * 128:(k + 1) * 128],
                                                        identf[0:5, 0:5]), [scl, c128], [p], signal=(k == 7))
        S.op("dve", lambda e, p=p: e.tensor_copy(scT[:].rearrange("p a b -> p (a b)"), p[:, 0:40]), [p], [scT])
        for j in range(6):
            S.dma_load("sp", badab, badab[0:5, :], b_ada[:, j * D:(j + 1) * D].partition_broadcast(5))
            dst = gtm1 if j == 2 else (gtm2 if j == 5 else mpart)
            for hf in range(2):
                st, sv = slab("w_ada", w_ada, j * D + hf * 512, 512, cache=False)
                p = nps()
                for k in range(8):
                    S.op("pe", lambda e, k=k, p=p, sv=sv: e.matmul(p[0:5, :], scT[:, k, :], sv[:, k, :],
                                                                    start=(k == 0), stop=(k == 7)),
                         [scT, st], [p], signal=(k == 7))
                S.op("dve", lambda e, p=p, hf=hf, dst=dst: e.tensor_tensor(
                    dst[0:5, hf * 512:(hf + 1) * 512], p[0:5, :], badab[0:5, hf * 512:(hf + 1) * 512], ALU.add),
                    [p, badab], [dst])
            p = nps()
            for k in range(8):
                S.op("pe", lambda e, k=k, p=p, dst=dst: e.transpose(p[:, k * 5:(k + 1) * 5], dst[0:5, k * 128:(k + 1) * 128],
                                                                      identf[0:5, 0:5]), [dst, c128], [p], signal=(k == 7))
            S.op("dve", lambda e, p=p, j=j: e.tensor_copy(modT[:, j, :, :].rearrange("p a b -> p (a b)"), p[:, 0:40]),
                 [p], [modT])
        for i, (jsh, jsc, vo) in enumerate(((0, 1, 0), (3, 4, 8))):
            S.op("dve", lambda e, i=i, jsc=jsc: e.tensor_scalar(AB[:, 2 * i, :, :], modT[:, jsc, :, :], 1.0, None, ALU.add),
                 [modT], [AB])
            for k in range(8):
                S.op("dve", lambda e, i=i, k=k, vo=vo: e.tensor_scalar(AB[:, 2 * i, k, :], AB[:, 2 * i, k, :],
                                                                        vecs[:, vo + k:vo + k + 1], None, ALU.mult),
                     [AB, vecs], [AB])
            S.op("dve", lambda e, i=i, jsh=jsh: e.tensor_copy(AB[:, 2 * i + 1, :, :], modT[:, jsh, :, :]), [modT], [AB])

        def build_gates(selv, G):
            for gt, Gd in ((gtm1, G1), (gtm2, G2)):
                for hf in range(2):
                    p = nps()
                    S.op("pe", lambda e, p=p, gt=gt, hf=hf: e.matmul(p[0:G, :], selv, gt[0:5, hf * 512:(hf + 1) * 512],
                                                                      start=True, stop=True), [selt, gt], [p])
                    S.op("dve", lambda e, p=p, Gd=Gd, hf=hf: e.tensor_copy(Gd[0:G, hf * 512:(hf + 1) * 512], p[0:G, :]),
                         [p], [Gd])

        build_gates(selt.h[0:5, 0:128], 128)
        fence()

        def norm_to_hT(xsrc, rows, gcols, which, segs, hdst=None, xn=xn, stc=0):
            xsrc_t, xa = xsrc
            hd = hT if hdst is None else hdst
            ncol = rows
            S.op("act", lambda e: e.activation(xn[0:rows, :], xa, AF.Square, accum_out=stat[0:rows, 0:1]),
                 [xsrc_t], [xn, stat])
            S.op("act", lambda e: e.activation(stat[0:rows, 1:2], stat[0:rows, 0:1], AF.Ln, bias=EPS, scale=1.0 / D),
                 [stat], [stat])
            S.op("act", lambda e: e.activation(stat[0:rows, 2:3], stat[0:rows, 1:2], AF.Exp, scale=-0.5), [stat], [stat])
            S.op("dve", lambda e: e.tensor_scalar(xn[0:rows, :], xa, stat[0:rows, 2:3], None, ALU.mult),
                 [xsrc_t, stat], [xn])
            p = npsb()
            for k in range(8):
                S.op("pe", lambda e, k=k, p=p: e.transpose(p[:, k * 128:k * 128 + rows], xn[0:rows, k * 128:(k + 1) * 128],
                                                            identb[0:rows, 0:rows]), [xn, identb], [p], signal=(k == 7))
            ia, ib = (0, 1) if which == 1 else (2, 3)
            for k in range(8):
                for (c0, c1, r) in segs:
                    S.op("dve", lambda e, k=k, p=p, c0=c0, c1=c1, r=r: e.tensor_scalar(
                        hd[:, k, gcols + c0:gcols + c1], p[:, k * 128 + c0:k * 128 + c1],
                        AB[:, ia, k, r:r + 1], AB[:, ib, k, r:r + 1], ALU.mult, ALU.add), [p, AB], [(hd, (k, gcols + c0))])

        def mm_fm(p, M, st, sv, c0, ncols, rhs_t=None, K=8):
            rt_ = hT if rhs_t is None else rhs_t
            for k in range(K):
                S.op("pe", lambda e, k=k: e.matmul(p[0:M, 0:ncols], sv[:, k, c0:c0 + M], rt_[:, k, 0:ncols],
                                                   start=(k == 0), stop=(k == K - 1)),
                     [st, rt_], [p], signal=(k == K - 1))

        def mm_tm(p, G, gcols, st, sv, c0, C, lhs_t=None, K=8, k0=0, first=True, last=True):
            lt = hT if lhs_t is None else lhs_t
            for k in range(K):
                S.op("pe", lambda e, k=k: e.matmul(p[0:G, 0:C], lt[:, k0 + k, gcols:gcols + G], sv[:, k, c0:c0 + C],
                                                   start=(first and k == 0), stop=(last and k == K - 1)),
                     [st, lt], [p], signal=(k == K - 1))

        def gla_decay(G, gcols, cumv, remv, need_full):
            p = nps()
            S.op("pe", lambda e: e.matmul(p[0:G, :], alrT[0:17, gcols:gcols + G], walpha[0:17, :], start=True, stop=True),
                 [alrT, walpha], [p])
            S.op("act", lambda e: e.activation(apt[0:G, :], p[0:G, :], AF.Exp, scale=-1.0), [p], [apt])
            S.op("act", lambda e: e.activation(apt[0:G, :], apt[0:G, :], AF.Ln, bias=1.0), [apt], [apt])
            p2 = nps()
            S.op("pe", lambda e: e.matmul(p2[0:G, :], remv, apt[0:G, :], start=True, stop=True), [c128, c64, apt], [p2])
            S.op("act", lambda e: e.activation(E3[0:G, :], p2[0:G, :], AF.Exp, scale=-1.0 / 16), [p2], [E3])
            S.op("dve", lambda e: e.tensor_tensor(khat[0:G, :], ktm[0:G, gcols // 128 if G == 128 else 0, :], E3[0:G, :],
                                                  ALU.mult), [ktm, E3], [khat])
            if not need_full:
                return
            p3 = nps()
            for h in range(4):
                S.op("pe", lambda e, h=h: e.matmul(p3[:, h * G:(h + 1) * G], apt[0:G, h * 128:(h + 1) * 128], cumv,
                                                   start=True, stop=True), [apt, c128, c64], [p3], signal=(h == 3))
            S.op("act", lambda e: e.activation(E1[:, :, 0:G], p3[:, 0:4 * G].rearrange("p (a b) -> p a b", b=G), AF.Exp,
                                               scale=-1.0 / 16), [p3], [E1])
            S.op("act", lambda e: e.activation(E2[:, :, 0:G], p3[:, 0:4 * G].rearrange("p (a b) -> p a b", b=G), AF.Exp,
                                               scale=1.0 / 16), [p3], [E2])
            S.op("dve", lambda e: e.scalar_tensor_tensor(qtil[:, :, 0:G], qT[:, :, gcols:gcols + G], 128.0 ** -0.5,
                                                         E1[:, :, 0:G], ALU.mult, ALU.mult), [qT, E1], [qtil])
            S.op("dve", lambda e: e.tensor_tensor(ktil[:, :, 0:G], kT[:, :, gcols:gcols + G], E2[:, :, 0:G], ALU.mult),
                 [kT, E2], [ktil])

        def pool_branch(nseq, L, NN, first_tile):
            W = 16 + L
            U = uT.h[:, 0:4 * nseq * W].rearrange("p (g s w) -> p g s w", s=nseq, w=W)
            A = tA.h[:, 0:nseq * W].rearrange("p (s w) -> p s w", w=W)
            B = tB.h[:, 0:nseq * W].rearrange("p (s w) -> p s w", w=W)
            dv = dT.h[:, :, 0:NN].rearrange("p g (s l) -> p g s l", l=L)
            for gi in range(4):
                src = U[:, gi]
                cur_t, cur = uT, src
                lo = 0
                for lev in range(gi + 1):
                    sh = 1 << lev
                    lo2 = lo + sh
                    dstt, dst = (tA, A) if lev % 2 == 0 else (tB, B)
                    S.op("dve", lambda e, dst=dst, cur=cur, lo2=lo2, sh=sh: e.tensor_tensor(
                        dst[:, :, lo2:W], cur[:, :, lo2:W], cur[:, :, lo2 - sh:W - sh], ALU.add), [cur_t], [dstt])
                    cur_t, cur, lo = dstt, dst, lo2
                wdt = float(1 << (gi + 1))
                S.op("dve", lambda e, cur=cur, gi=gi, wdt=wdt: e.scalar_tensor_tensor(
                    dv[:, gi], cur[:, :, 16:W], 1.0 / wdt, U[:, gi, :, 16:W], ALU.mult, ALU.subtract), [cur_t, uT], [dT])
                if first_tile:
                    S.op("dve", lambda e, cur=cur, gi=gi: e.tensor_tensor(
                        cur[:, 0, 16:32], cur[:, 0, 16:32], invc[:, gi * 16:(gi + 1) * 16], ALU.mult), [cur_t, invc], [cur_t])
                    S.op("dve", lambda e, cur=cur, gi=gi: e.tensor_tensor(
                        dT[:, gi, 0:16], cur[:, 0, 16:32], U[:, gi, 0, 16:32], ALU.subtract), [cur_t, uT], [dT])
            for gi in range(4):
                p = nps()
                S.op("pe", lambda e, gi=gi, p=p: e.matmul(p[:, 0:NN], wpool[:, gi, :], dT[:, gi, 0:NN], start=True, stop=True),
                     [wpool, dT], [p])
                S.op("dve", lambda e, gi=gi, p=p: e.tensor_scalar(aoT[:, gi, 0:NN], p[:, 0:NN], vecs[:, 16 + gi:17 + gi], None,
                                                                  ALU.mult), [p, vecs], [(aoT, gi)])

        def gla_out(G, gcols, vrow, mask4, segstates, sbsel):
            p = nps()
            for h in range(4):
                S.op("pe", lambda e, h=h: e.matmul(p[0:G, h * G:(h + 1) * G], ktil[:, h, 0:G], qtil[:, h, 0:G],
                                                   start=True, stop=True), [ktil, qtil], [p], signal=(h == 3))
            S.op("dve", lambda e: e.tensor_tensor(scm[0:G, :, 0:G], p[0:G, 0:4 * G].rearrange("p (a b) -> p a b", b=G),
                                                  mask4, ALU.mult), [p, CUM4, CUM4s], [scm])
            po = [nps(), nps()]
            for h in range(4):
                for vc in range(2):
                    pp = po[h // 2]
                    o0 = ((h % 2) * 2 + vc) * G
                    S.op("pe", lambda e, h=h, vc=vc, pp=pp, o0=o0: e.matmul(
                        pp[:, o0:o0 + G], vtm[0:G, vrow, h * 256 + vc * 128:h * 256 + (vc + 1) * 128], scm[0:G, h, 0:G],
                        start=True, stop=False), [vtm, scm], [pp], signal=False)
                    ns = len(segstates)
                    for si, (c0, c1, sbt) in enumerate(segstates):
                        S.op("pe", lambda e, h=h, vc=vc, pp=pp, o0=o0, c0=c0, c1=c1, sbt=sbt, si=si: e.matmul(
                            pp[:, o0 + c0:o0 + c1], sbt[:, h, vc * 128:(vc + 1) * 128], qtil[:, h, c0:c1],
                            start=False, stop=(si == ns - 1)), [sbt, qtil], [pp],
                            signal=(si == ns - 1 and vc == 1 and h % 2 == 1))
            for half in range(2):
                S.op("act", lambda e, half=half: e.activation(
                    sq[:, half * 4:(half + 1) * 4, 0:G], po[half][:, 0:4 * G].rearrange("p (a b) -> p a b", b=G), AF.Square),
                    [po[half]], [(sq, half)])
            pss = nps()
            for h in range(4):
                for vc in range(2):
                    S.op("pe", lambda e, h=h, vc=vc: e.matmul(pss[:, h * G:(h + 1) * G], ones_b[:], sq[:, h * 2 + vc, 0:G],
                                                              start=(vc == 0), stop=(vc == 1)), [ones_b, sq], [pss],
                         signal=(h == 3 and vc == 1))
            S.op("act", lambda e: e.activation(rstT[:, :, 0:G], pss[:, 0:4 * G].rearrange("p (a b) -> p a b", b=G), AF.Ln,
                                               bias=EPS, scale=1.0 / 256), [pss], [rstT])
            S.op("act", lambda e: e.activation(rstT[:, :, 0:G], rstT[:, :, 0:G], AF.Exp, scale=-0.5), [rstT], [rstT])
            for h in range(4):
                for vc in range(2):
                    pp = po[h // 2]
                    o0 = ((h % 2) * 2 + vc) * G
                    S.op("dve", lambda e, h=h, vc=vc, pp=pp, o0=o0: e.scalar_tensor_tensor(
                        otmp[:, vc, 0:G], pp[:, o0:o0 + G], vecs[:, 20 + vc:21 + vc], rstT[:, h, 0:G],
                        ALU.mult, ALU.mult), [pp, vecs, rstT], [(otmp, vc)])
                S.op("dve", lambda e, h=h: e.tensor_tensor(boT[:, 2 * h:2 * h + 2, gcols:gcols + G], otmp[:, :, 0:G],
                                                           sgT[:, 2 * h:2 * h + 2, gcols:gcols + G], ALU.mult), [otmp, sgT], [boT])

        def gla_prompt_tile(sbi):
            G = 128
            sets = [dict(apt=apt, E3=E3, khat=khat, qtil=qtil, ktil=ktil, ebl=eblA),
                    dict(apt=aptB, E3=E3, khat=khatB, qtil=qtilB, ktil=ktilB, ebl=eblB)]

            PF = [TTv(PSB[0]), TTv(PSB[1])]
            bank = dict(z=PF[0], pss=PF[0], rem=PF[1], bT=PS[0], sc=PS[1], po=(PS[2], PS[3]), pp=(PS[4], PS[5]))

            def D1(g):
                b = sets[g % 2]
                p = bank["z"]
                S.op("pe", lambda e: e.matmul(p[:, :], alrT[0:17, g * 128:(g + 1) * 128], walpha[0:17, :], start=True, stop=True),
                     [alrT, walpha], [p])
                S.op("act", lambda e: e.activation(b["apt"][:], p[:], AF.Exp, scale=-1.0), [p], [b["apt"]])
                S.op("act", lambda e: e.activation(b["apt"][:], b["apt"][:], AF.Ln, bias=1.0), [b["apt"]], [b["apt"]])

            def D2(g):
                b = sets[g % 2]
                a_ = b["apt"]
                p2 = bank["rem"]
                S.op("pe", lambda e: e.matmul(p2[:, :], REM, a_[:], start=True, stop=True), [c128, a_], [p2])
                p3 = bank["bT"]
                for h in range(4):
                    S.op("pe", lambda e, h=h: e.matmul(p3[:, h * G:(h + 1) * G], a_[:, h * 128:(h + 1) * 128], CUM,
                                                       start=True, stop=True), [a_, c128], [p3], signal=(h == 3))
                p3v = p3[:, 0:4 * G].rearrange("p (a b) -> p a b", b=G)
                S.op("act", lambda e: e.activation(b["E3"][:], p2[:], AF.Exp, scale=-1.0 / 16), [p2], [b["E3"]])
                S.op("act", lambda e: e.activation(E1[:], p3v, AF.Exp, scale=-1.0 / 16), [p3], [E1])
                S.op("act", lambda e: e.activation(E2[:], p3v, AF.Exp, scale=1.0 / 16), [p3], [E2])
                S.op("act", lambda e: e.activation(b["ebl"][:, 0:4], p3v[:, :, G - 1], AF.Exp, scale=-1.0 / 16), [p3], [b["ebl"]])
                S.op("dve", lambda e: e.tensor_tensor(b["khat"][:], ktm[:, g, :], b["E3"][:], ALU.mult), [ktm, b["E3"]], [b["khat"]])
                S.op("dve", lambda e: e.scalar_tensor_tensor(b["qtil"][:], qT[:, :, g * 128:(g + 1) * 128], 128.0 ** -0.5, E1[:],
                                                             ALU.mult, ALU.mult), [qT, E1], [b["qtil"]])
                S.op("dve", lambda e: e.tensor_tensor(b["ktil"][:], kT[:, :, g * 128:(g + 1) * 128], E2[:], ALU.mult),
                     [kT, E2], [b["ktil"]])

            st_ = {}

            def O1(g):
                b = sets[g % 2]
                p = bank["sc"]
                for h in range(4):
                    S.op("pe", lambda e, h=h: e.matmul(p[:, h * G:(h + 1) * G], b["ktil"][:, h, :], b["qtil"][:, h, :],
                                                       start=True, stop=True), [b["ktil"], b["qtil"]], [p], signal=(h == 3))
                S.op("dve", lambda e: e.tensor_tensor(scm[:], p[:, 0:4 * G].rearrange("p (a b) -> p a b", b=G), CUM4[:], ALU.mult),
                     [p, CUM4], [scm])

            def O2(g):
                b = sets[g % 2]
                cur = Sb[sbi[0] % 2]
                po = list(bank["po"])
                st_["po"] = po
                for h in range(4):
                    for vc in range(2):
                        pp = po[h // 2]
                        o0 = ((h % 2) * 2 + vc) * G
                        S.op("pe", lambda e, h=h, vc=vc, pp=pp, o0=o0: e.matmul(
                            pp[:, o0:o0 + G], vtm[:, g, h * 256 + vc * 128:h * 256 + (vc + 1) * 128], scm[:, h, :],
                            start=True, stop=False), [vtm, scm], [pp], signal=False)
                        S.op("pe", lambda e, h=h, vc=vc, pp=pp, o0=o0: e.matmul(
                            pp[:, o0:o0 + G], cur[:, h, vc * 128:(vc + 1) * 128], b["qtil"][:, h, :],
                            start=False, stop=True), [cur, b["qtil"]], [pp], signal=(vc == 1 and h % 2 == 1))
                for half in range(2):
                    S.op("act", lambda e, half=half: e.activation(
                        sq[:, half * 4:(half + 1) * 4, :], po[half][:, 0:4 * G].rearrange("p (a b) -> p a b", b=G), AF.Square),
                        [po[half]], [(sq, half)])

            def O3(g):
                po = st_["po"]
                pss = bank["pss"]
                for h in range(4):
                    for vc in range(2):
                        S.op("pe", lambda e, h=h, vc=vc: e.matmul(pss[:, h * G:(h + 1) * G], ones_b[:], sq[:, h * 2 + vc, :],
                                                                  start=(vc == 0), stop=(vc == 1)), [ones_b, sq], [pss],
                             signal=(h == 3 and vc == 1))
                S.op("act", lambda e: e.activation(rstT[:], pss[:, 0:4 * G].rearrange("p (a b) -> p a b", b=G), AF.Ln,
                                                   bias=EPS, scale=1.0 / 256), [pss], [rstT])
                S.op("act", lambda e: e.activation(rstT[:], rstT[:], AF.Exp, scale=-0.5), [rstT], [rstT])
                for h in range(4):
                    for vc in range(2):
                        pp = po[h // 2]
                        o0 = ((h % 2) * 2 + vc) * G
                        S.op("dve", lambda e, h=h, vc=vc, pp=pp, o0=o0: e.scalar_tensor_tensor(
                            otmp[:, vc, :], pp[:, o0:o0 + G], vecs[:, 20 + vc:21 + vc], rstT[:, h, :],
                            ALU.mult, ALU.mult), [pp, vecs, rstT], [(otmp, vc)])
                    S.op("dve", lambda e, h=h: e.tensor_tensor(boT[:, 2 * h:2 * h + 2, g * 128:(g + 1) * 128], otmp[:],
                                                               sgT[:, 2 * h:2 * h + 2, g * 128:(g + 1) * 128], ALU.mult),
                         [otmp, sgT], [(boT, (h, g))])

            def STU(g):
                b = sets[g % 2]
                nxt_ = Sb[(sbi[0] + 1) % 2]
                sbi[0] += 1
                state_update(128, g, b["khat"], b["khat"], St, lambda h: (b["ebl"], b["ebl"][:, h:h + 1]), nxt_, banks=bank["pp"])

            D1(0)
            D2(0)
            for g in range(4):
                if g + 1 < 4:
                    D1(g + 1)
                O1(g)
                if g + 1 < 4:
                    D2(g + 1)
                O2(g)
                O3(g)
                STU(g)

        def state_update(G, vrow, khv_t, khv, Sfp, ecol, Sbf_dst, vtm=vtm, banks=None):
            pp = list(banks) if banks is not None else [nps(), nps()]
            for h in range(4):
                S.op("pe", lambda e, h=h: e.matmul(pp[h // 2][:, (h % 2) * 256:(h % 2 + 1) * 256], khv[0:G, h * 128:(h + 1) * 128],
                                                   vtm[0:G, vrow, h * 256:(h + 1) * 256], start=True, stop=True),
                     [khv_t, vtm], [pp[h // 2]], signal=(h % 2 == 1))
            for h in range(4):
                et, ea = ecol(h)
                S.op("dve", lambda e, h=h, ea=ea: e.scalar_tensor_tensor(
                    Sfp[:, h, :], Sfp[:, h, :], ea, pp[h // 2][:, (h % 2) * 256:(h % 2 + 1) * 256], ALU.mult, ALU.add),
                    [Sfp, et, pp[h // 2]], [Sfp])
            if Sbf_dst is not None:
                S.op("act", lambda e: e.activation(Sbf_dst[:], Sfp[:], AF.Copy), [Sfp], [Sbf_dst])

        if CARRY == "allgather":
            for t in range(NT):
                S.dma_load("sp", xt, xt[:], xp[t * N:(t + 1) * N, :].rearrange("(g p) d -> p g d", p=128))
                for g in range(4):
                    norm_to_hT((xt, xt[:, g, :]), 128, g * 128, 1, [(0, 128, 0)])
                st, sv = slab("w_in", w_in, O_K, 512)
                for g in range(4):
                    p = nps()
                    mm_tm(p, 128, g * 128, st, sv, 0, 512)
                    S.op("act", lambda e, p=p, g=g: e.activation(ktm[:, g, :], p[:], AF.Copy), [p], [ktm])
                for hf in range(2):
                    st, sv = slab("w_in", w_in, O_V + hf * 512, 512)
                    for g in range(4):
                        p = nps()
                        mm_tm(p, 128, g * 128, st, sv, 0, 512)
                        S.op("act", lambda e, p=p, g=g, hf=hf: e.activation(vtm[:, g, hf * 512:(hf + 1) * 512], p[:], AF.Copy),
                             [p], [vtm])
                st, sv = slab("w_in", w_in, O_ALR, 16)
                p = nps()
                mm_fm(p, 16, st, sv, 0, N)
                S.op("act", lambda e, p=p: e.activation(alrT[0:16, :], p[0:16, :], AF.Copy), [p], [alrT])
                for g in range(4):
                    gla_decay(128, g * 128, CUM, REM, False)
                    pl = nps()
                    for h in range(4):
                        S.op("pe", lambda e, h=h, pl=pl: e.matmul(pl[:, h:h + 1], apt[:, h * 128:(h + 1) * 128], ones_f[:, 0:1],
                                                                  start=True, stop=True), [apt, ones_f], [pl], signal=(h == 3))
                    S.op("act", lambda e, pl=pl: e.activation(ebl[:, 0:4], pl[:, 0:4], AF.Exp, scale=-1.0 / 16), [pl], [ebl])
                    S.op("dve", lambda e, pl=pl: e.tensor_tensor(Lsum[:], Lsum[:], pl[:, 0:4], ALU.add), [Lsum, pl], [Lsum])
                    state_update(128, g, khat, khat, St, lambda h: (ebl, ebl[:, h:h + 1]), None)
                fence()
            S.op("act", lambda e: e.activation(gbuf[:, 0:1024], St[:].rearrange("p a b -> p (a b)"), AF.Copy), [St], [gbuf])
            S.op("act", lambda e: e.activation(gbuf[:, 1024:1028], Lsum[:], AF.Copy), [Lsum], [gbuf])
            S.dma_store("sp", gbuf, gin, gbuf[:])
            ssem = S.sems[gbuf.ssem]
            sval = 16 * gbuf.scnt
            ccs = S.sems[S._sem("CC")]

            def cc(eng):
                eng.wait_ge(ssem, sval)
                eng.collective_compute("AllGather", ALU.bypass, replica_groups=[list(range(NCORES))],
                                       ins=[gin], outs=[gout]).then_inc(ccs, 1)
                eng.wait_ge(ccs, 1)
            S.raw("pool", cc)
            S.op("dve", lambda e: e.memset(St[:], 0.0), [], [St])
            S.op("dve", lambda e: e.memset(Tacc[:], 0.0), [], [Tacc])
            for j in range(NCORES):
                gj = gb2[j % 2]
                gj.w["CC"] = 1
                S.dma_load("sp", gj, gj[:], gout[j * 128:(j + 1) * 128, :])
                S.op("dve", lambda e, j=j: e.scalar_tensor_tensor(St[:], Tacc[:], flg[:, 1 + j:2 + j], St[:], ALU.mult, ALU.add),
                     [Tacc, flg, St], [St])
                S.op("act", lambda e, gj=gj: e.activation(Dj[:, 0:4], gj[:, 1024:1028], AF.Exp, scale=-1.0 / 16), [gj], [Dj])
                for h in range(4):
                    S.op("dve", lambda e, h=h, gj=gj: e.scalar_tensor_tensor(
                        Tacc[:, h, :], Tacc[:, h, :], Dj[:, h:h + 1], gj[:, h * 256:(h + 1) * 256], ALU.mult, ALU.add),
                        [Tacc, Dj, gj], [Tacc])
            S.op("act", lambda e: e.activation(Sb[0][:], St[:], AF.Copy), [St], [Sb[0]])
            fence()

        if CARRY == "prefix":
            PW = [0]

            def pcarve(name, words, dt, shape=None):
                t, o = carve(name, words, dt, shape, at=PW[0])
                PW[0] += words
                return t
            xg = [pcarve("xg%d" % i, 1024, F32) for i in range(4)]
            xnp = [xn, pcarve("xnp1", 512, BF16)]
            jks = [pcarve("jk%d" % i, 512, BF16) for i in range(2)]
            hTp = [hT, pcarve("hTp1", 2048, BF16, (8, N))]
            ktmp = [pcarve("ktmp%d" % i, 2048, F32, (4, 512)) for i in range(2)]
            vtmp = [pcarve("vtmp%d" % i, 2048, BF16, (4, 1024)) for i in range(2)]
            alrp = [pcarve("alrp%d" % i, 512, F32) for i in range(2)]
            aptp = [pcarve("aptp%d" % i, 512, F32) for i in range(4)]
            E3p = [pcarve("E3p%d" % i, 512, F32) for i in range(4)]
            khp = [pcarve("khp%d" % i, 256, BF16) for i in range(4)]
            eblp = [pcarve("eblp%d" % i, 4, F32) for i in range(4)]
            statp = [pcarve("statp%d" % i, 12, F32) for i in range(2)]
            assert PW[0] <= AR_WORDS, PW[0]
            fence()
            for a_ in alrp:
                S.op("dve", lambda e, a_=a_: e.memset(a_[0:32, :], 1.0), [], [a_])

            def A1(i):
                sp_ = statp[i % 2]
                for g in range(4):
                    S.dma_load("sp", xg[g], xg[g][:], xpre[i * N + g * 128:i * N + (g + 1) * 128, :])
                for g in range(4):
                    S.op("act", lambda e, g=g: e.activation(jks[g % 2][:], xg[g][:], AF.Square, accum_out=sp_[:, g:g + 1]),
                         [xg[g]], [jks[g % 2], (sp_, g)])
                S.op("act", lambda e: e.activation(sp_[:, 4:8], sp_[:, 0:4], AF.Ln, bias=EPS, scale=1.0 / D), [sp_], [sp_])
                S.op("act", lambda e: e.activation(sp_[:, 8:12], sp_[:, 4:8], AF.Exp, scale=-0.5), [sp_], [sp_])

            def A2a(i, g):
                sp_ = statp[i % 2]
                xn_ = xnp[g % 2]
                S.op("act", lambda e: e.activation(xn_[:], xg[g][:], AF.Copy, scale=sp_[:, 8 + g:9 + g]), [xg[g], sp_], [xn_])

            def A2b(i, g):
                xn_ = xnp[g % 2]
                hd = hTp[i % 2]
                p = npsb()
                for k in range(8):
                    S.op("pe", lambda e, k=k: e.transpose(p[:, k * 128:(k + 1) * 128], xn_[:, k * 128:(k + 1) * 128], identb[:]),
                         [xn_, identb], [p], signal=(k == 7))
                for k in range(8):
                    S.op("dve", lambda e, k=k: e.tensor_scalar(hd[:, k, g * 128:(g + 1) * 128], p[:, k * 128:(k + 1) * 128],
                                                               AB[:, 0, k, 0:1], AB[:, 1, k, 0:1], ALU.mult, ALU.add), [p, AB], [(hd, (k, g))])

            pref_slabs = {}

            def pslab(c0, C):
                if (c0, C) not in pref_slabs:
                    pref_slabs[(c0, C)] = slab("w_in", w_in, c0, C)
                return pref_slabs[(c0, C)]

            def B_pieces(i):
                hT_, ktm_, vtm_, alr_ = hTp[i % 2], ktmp[i % 2], vtmp[i % 2], alrp[i % 2]

                def Bk():
                    st, sv = pslab(O_K, 512)
                    for g in range(4):
                        p = nps()
                        mm_tm(p, 128, g * 128, st, sv, 0, 512, lhs_t=hT_)
                        S.op("act", lambda e, p=p, g=g: e.activation(ktm_[:, g, :], p[:], AF.Copy), [p], [(ktm_, g)])

                def Bv(hf):
                    def f():
                        st, sv = pslab(O_V + hf * 512, 512)
                        for g in range(4):
                            p = nps()
                            mm_tm(p, 128, g * 128, st, sv, 0, 512, lhs_t=hT_)
                            if g % 2 == 0:
                                S.op("act", lambda e, p=p, g=g: e.activation(vtm_[:, g, hf * 512:(hf + 1) * 512], p[:], AF.Copy,
                                                                             scale=flg[:, 1 + i // NT:2 + i // NT]),
                                     [p, flg], [(vtm_, (g, hf))])
                            else:
                                S.op("dve", lambda e, p=p, g=g: e.tensor_scalar(vtm_[:, g, hf * 512:(hf + 1) * 512], p[:],
                                                                                flg[:, 1 + i // NT:2 + i // NT], None, ALU.mult),
                                     [p, flg], [(vtm_, (g, hf))])
                    return f

                def Balr():
                    st, sv = pslab(O_ALR, 16)
                    p = nps()
                    mm_fm(p, 16, st, sv, 0, N, rhs_t=hT_)
                    S.op("act", lambda e, p=p: e.activation(alr_[0:16, :], p[0:16, :], AF.Copy), [p], [alr_])
                return [Bk, Bv(0), Bv(1), Balr]

            def C_stages(i):
                ktm_, vtm_, alr_ = ktmp[i % 2], vtmp[i % 2], alrp[i % 2]
                j = i // NT
                pz = [None] * 4
                pr = [None] * 4
                pl = [None] * 4

                def Z():
                    for g in range(4):
                        pz[g] = nps()
                        S.op("pe", lambda e, g=g: e.matmul(pz[g][:, :], alr_[0:17, g * 128:(g + 1) * 128], walpha[0:17, :],
                                                           start=True, stop=True), [alr_, walpha], [pz[g]])

                def ACT1():
                    for g in range(4):
                        S.op("act", lambda e, g=g: e.activation(aptp[g][:], pz[g][:], AF.Exp, scale=-1.0), [pz[g]], [aptp[g]])
                        S.op("act", lambda e, g=g: e.activation(aptp[g][:], aptp[g][:], AF.Ln, bias=1.0), [aptp[g]], [aptp[g]])

                def R():
                    for g in range(4):
                        pr[g] = nps()
                        S.op("pe", lambda e, g=g: e.matmul(pr[g][:, :], REM, aptp[g][:], start=True, stop=True), [c128, aptp[g]], [pr[g]])
                    pl[0] = nps()
                    for g in range(4):
                        for h in range(4):
                            S.op("pe", lambda e, g=g, h=h: e.matmul(pl[0][:, g * 4 + h:g * 4 + h + 1], aptp[g][:, h * 128:(h + 1) * 128],
                                                                    ones_f[:, 0:1], start=True, stop=True), [aptp[g], ones_f], [pl[0]],
                                 signal=(h == 3))

                def ACT2():
                    for g in range(4):
                        S.op("act", lambda e, g=g: e.activation(E3p[g][:], pr[g][:], AF.Exp, scale=-1.0 / 16), [pr[g]], [E3p[g]])
                        S.op("act", lambda e, g=g: e.activation(eblp[g][:, 0:4], pl[0][:, g * 4:g * 4 + 4], AF.Exp, scale=-1.0 / 16),
                             [pl[0]], [eblp[g]])

                def KH():
                    for g in range(4):
                        S.op("pool", lambda e, g=g: e.tensor_tensor(khp[g][:], ktm_[:, g, :], E3p[g][:], ALU.mult),
                             [ktm_, E3p[g]], [khp[g]])

                def ST(gs):
                    def f():
                        for g in gs:
                            state_update(128, g, khp[g], khp[g], St, lambda h, g=g: (eblp[g], eblp[g][:, h:h + 1]), None, vtm=vtm_)
                    return f
                return [Z, ACT1, R, ACT2, KH, ST((0, 1)), ST((2, 3))]

            A1(0)
            for g in range(4):
                A2a(0, g)
                A2b(0, g)
            A1(1)
            for f in B_pieces(0):
                f()
            for i in range(NPRE):
                cs = C_stages(i)
                nxt = i + 1 < NPRE
                bp = B_pieces(i + 1) if nxt else [lambda: None] * 4
                cs[0]()
                if nxt:
                    A2a(i + 1, 0)
                cs[1]()
                if nxt:
                    A2b(i + 1, 0)
                    A2a(i + 1, 1)
                cs[2]()
                if nxt:
                    A2b(i + 1, 1)
                    A2a(i + 1, 2)
                cs[3]()
                if nxt:
                    A2b(i + 1, 2)
                    A2a(i + 1, 3)
                cs[4]()
                if nxt:
                    A2b(i + 1, 3)
                if i + 2 < NPRE:
                    A1(i + 2)
                bp[0]()
                cs[5]()
                bp[1]()
                cs[6]()
                bp[2]()
                bp[3]()
            S.op("act", lambda e: e.activation(Sb[0][:], St[:], AF.Copy), [St], [Sb[0]])
            fence()
            S.op("dve", lambda e: e.memset(alrT[0:32, :], 1.0), [], [alrT])
        else:
            S.op("dve", lambda e: e.memset(alrT[0:32, :], 1.0), [], [alrT])

        sbi = [0]

        def make_pre(kind, t):
            isp = kind == "p"
            G_ = 128 if isp else 64
            NG_ = 4 if isp else 1
            segs_ = [(0, 128, 0)] if isp else [(16 * s_, 16 * s_ + 16, 1 + s_) for s_ in range(SPC)]

            def load():
                if isp:
                    S.dma_load("sp", xnx, xnx[:], xp[t * N:(t + 1) * N, :].rearrange("(g p) d -> p g d", p=128))
                else:
                    S.dma_load("sp", xnx, xnx[0:64, 0, :], xs)

            def stats():
                for g in range(NG_):
                    S.op("act", lambda e, g=g: e.activation(jkm[g % 2][0:G_, :], xnx[0:G_, g, :], AF.Square,
                                                            accum_out=stat2[0:G_, g:g + 1]), [xnx], [jkm[g % 2], (stat2, g)])
                S.op("act", lambda e: e.activation(stat2[0:G_, 4:4 + NG_], stat2[0:G_, 0:NG_], AF.Ln, bias=EPS, scale=1.0 / D),
                     [stat2], [stat2])
                S.op("act", lambda e: e.activation(stat2[0:G_, 8:8 + NG_], stat2[0:G_, 4:4 + NG_], AF.Exp, scale=-0.5),
                     [stat2], [stat2])

            def a2a(g):
                xn_ = (xn, xn2)[g % 2]
                S.op("act", lambda e: e.activation(xn_[0:G_, :], xnx[0:G_, g, :], AF.Copy, scale=stat2[0:G_, 8 + g:9 + g]),
                     [xnx, stat2], [xn_])

            def a2b(g):
                xn_ = (xn, xn2)[g % 2]
                p = npsb()
                for k in range(8):
                    S.op("pe", lambda e, k=k: e.transpose(p[:, k * 128:k * 128 + G_], xn_[0:G_, k * 128:(k + 1) * 128],
                                                          identb[0:G_, 0:G_]), [xn_, identb], [p], signal=(k == 7))
                for k in range(8):
                    for (c0, c1, r) in segs_:
                        S.op("dve", lambda e, k=k, c0=c0, c1=c1, r=r: e.tensor_scalar(
                            hT[:, k, g * 128 + c0:g * 128 + c1], p[:, k * 128 + c0:k * 128 + c1],
                            AB[:, 0, k, r:r + 1], AB[:, 1, k, r:r + 1], ALU.mult, ALU.add), [p, AB], [(hT, (k, g * 128 + c0))])

            def copy():
                S.dma_load("sp", xt, xt[0:G_, 0:NG_, :], xnx[0:G_, 0:NG_, :], reads=[xnx])
            return dict(load=load, stats=stats, a2a=a2a, a2b=a2b, copy=copy, ng=NG_)

        def layer_tile(kind, t, pre_done=False, next_tile=None):
            is_p = kind == "p"
            NN = N if is_p else SPC * LS
            G = 128 if is_p else 64
            NGR = 4 if is_p else 1
            first_tile = is_p and t == 0
            last_tile = is_p and t == NT - 1
            segs = [(0, 128, 0)] if is_p else [(16 * s, 16 * s + 16, 1 + s) for s in range(SPC)]
            W = 16 + (N if is_p else LS)
            nseq = 1 if is_p else SPC
            U = uT.h[:, 0:4 * nseq * W].rearrange("p (g s w) -> p g s w", s=nseq, w=W)
            if not pre_done:
                if is_p:
                    S.dma_load("sp", xt, xt[:], xp[t * N:(t + 1) * N, :].rearrange("(g p) d -> p g d", p=128))
                else:
                    S.dma_load("sp", xt, xt[0:64, 0, :], xs)
                for g in range(NGR):
                    norm_to_hT((xt, xt[0:G, g, :]), G, g * 128, 1, segs)
            if first_tile:
                norm_to_hT((xht, xht[:]), 16, 0, 1, [(0, 16, 0)], hdst=hTh)
            st, sv = slab("w_in", w_in, O_U, 512)
            for m in range(4):
                p = nps()
                mm_fm(p, 128, st, sv, m * 128, NN)
                S.op("act", lambda e, p=p, m=m: e.activation(U[:, m, :, 16:W], p[:, 0:NN].rearrange("p (s l) -> p s l", s=nseq),
                                                             AF.Copy), [p], [(uT, m)])
            if first_tile:
                p = nps()
                for m in range(4):
                    for k in range(8):
                        S.op("pe", lambda e, m=m, k=k, p=p, sv=sv: e.matmul(p[:, m * 16:(m + 1) * 16], sv[:, k, m * 128:(m + 1) * 128],
                                                                      hTh[:, k, :], start=(k == 0), stop=(k == 7)),
                             [st, hTh], [p], signal=(k == 7))
                S.op("dve", lambda e, p=p: e.tensor_scalar(U[:, :, 0, 0:16], p[:, 0:64].rearrange("p (g w) -> p g w", w=16),
                                                           flg[:, 0:1], None, ALU.mult), [p, flg], [uT])
            if not is_p:
                S.dma_load("sp", cach, cach[0:60, :], cache)
                p = nps()
                for gi in range(4):
                    S.op("pe", lambda e, gi=gi, p=p: e.transpose(p[:, gi * 60:(gi + 1) * 60], cach[0:60, gi * 128:(gi + 1) * 128],
                                                                  identf[0:60, 0:60]), [cach, c128], [p], signal=(gi == 3))
                S.op("dve", lambda e, p=p: e.tensor_copy(U[:, :, :, 1:16], p[:, 0:240].rearrange("p (g s w) -> p g s w", s=4, w=15)),
                     [p], [uT])
            if (not is_p) or last_tile:
                p = nps()
                gl = 0 if not is_p else 3 * 128
                mm_tm(p, G, gl, st, sv, 0, 512)
                S.op("act", lambda e, p=p: e.activation(utm[0:G, :], p[0:G, :], AF.Copy), [p], [utm])
                if is_p:
                    S.dma_store("sp", utm, cp_o, utm[113:128, :])
                else:
                    for s in range(SPC):
                        S.dma_store("sp", utm, cs_o[s * 15:(s + 1) * 15, :], utm[16 * s + 1:16 * s + 16, :])
            st, sv = slab("w_in", w_in, O_Q, 512)
            for m in range(4):
                p = nps()
                mm_fm(p, 128, st, sv, m * 128, NN)
                S.op("act", lambda e, p=p, m=m: e.activation(qT[:, m, 0:NN], p[:, 0:NN], AF.Copy), [p], [(qT, m)])
            st, sv = slab("w_in", w_in, O_K, 512)
            for m in range(4):
                p = nps()
                mm_fm(p, 128, st, sv, m * 128, NN)
                S.op("act", lambda e, p=p, m=m: e.activation(kT[:, m, 0:NN], p[:, 0:NN], AF.Copy), [p], [(kT, m)])
            for g in range(NGR):
                p = nps()
                mm_tm(p, G, g * 128, st, sv, 0, 512)
                S.op("act", lambda e, p=p, g=g: e.activation(ktm[0:G, g, :], p[0:G, :], AF.Copy), [p], [(ktm, g)])
            for hf in range(2):
                st, sv = slab("w_in", w_in, O_V + hf * 512, 512)
                for g in range(NGR):
                    p = nps()
                    mm_tm(p, G, g * 128, st, sv, 0, 512)
                    S.op("act", lambda e, p=p, g=g, hf=hf: e.activation(vtm[0:G, g, hf * 512:(hf + 1) * 512], p[0:G, :], AF.Copy),
                         [p], [(vtm, (g, hf))])
            for hf in range(2):
                st, sv = slab("w_in", w_in, O_G + hf * 512, 512)
                for m in range(4):
                    p = nps()
                    mm_fm(p, 128, st, sv, m * 128, NN)
                    S.op("act", lambda e, p=p, m=m, hf=hf: e.activation(sgT[:, hf * 4 + m, 0:NN], p[:, 0:NN], AF.Silu), [p], [(sgT, hf * 4 + m)])
            st, sv = slab("w_in", w_in, O_ALR, 16)
            p = nps()
            mm_fm(p, 16, st, sv, 0, NN)
            S.op("dve", lambda e: e.memset(alrT[0:32, :], 1.0), [], [alrT])
            S.op("act", lambda e, p=p: e.activation(alrT[0:16, 0:NN], p[0:16, 0:NN], AF.Copy), [p], [alrT])

            pool_branch(nseq, N if is_p else LS, NN, first_tile)
            if is_p:
                S.op("dve", lambda e: e.tensor_copy(U[:, :, 0, 0:16], U[:, :, 0, N:N + 16]), [uT], [uT])
            if is_p:
                gla_prompt_tile(sbi)
            for g in range(NGR):
                if is_p:
                    pass
                else:
                    gla_decay(64, 0, CUMs, REMs, True)
                    for s in range(SPC):
                        S.dma_load("sp", St, St[:], s0[s].rearrange("h c v -> c h v"))
                        S.op("act", lambda e, s=s: e.activation(Sb[s][:], St[:], AF.Copy), [St], [Sb[s]])
                    gla_out(64, 0, 0, CUM4s[:], [(16 * s, 16 * s + 16, Sb[s]) for s in range(SPC)], None)
                    for s in range(SPC):
                        S.dma_load("sp", St, St[:], s0[s].rearrange("h c v -> c h v"))
                        S.op("dve", lambda e, s=s: e.tensor_scalar(khs[0:64, :], khat[0:64, :], SEGM[:, s:s + 1], None, ALU.mult),
                             [khat, c64], [khs])
                        state_update(64, 0, khs, khs, St, lambda h, s=s: (E1, E1[:, h, 16 * s + 15:16 * s + 16]), None)
                        S.dma_store("sp", St, ss_o[s].rearrange("h c v -> c h v"), St[:])
            if last_tile:
                S.dma_store("sp", St, sp_o.rearrange("h c v -> c h v"), St[:])
                out_tiles.append(St)
            fence()

            st, sv = slab("w_pa", w_pa, 0, D, K=4)
            sga = []
            for hf in range(2):
                stg, svg = slab("w_in", w_in, O_GA + hf * 512, 512)
                for m in range(4):
                    mm = hf * 4 + m
                    pg = nps()
                    mm_fm(pg, 128, stg, svg, m * 128, NN)
                    r = rl[mm % 2]
                    S.op("act", lambda e, pg=pg, r=r: e.activation(r[:, 0:NN], pg[:, 0:NN], AF.Sigmoid), [pg], [r])
                    pa = nps()
                    mm_fm(pa, 128, st, sv, mm * 128, NN, rhs_t=aoT, K=4)
                    S.op("dve", lambda e, pa=pa, r=r, mm=mm: e.tensor_tensor(mgA[:, mm, 0:NN], pa[:, 0:NN], r[:, 0:NN], ALU.mult),
                         [pa, r], [(mgA, mm)])
            wpb = [slab("w_pb", w_pb, hf * 512, 512) for hf in range(2)]
            for hf in range(2):
                stg, svg = slab("w_in", w_in, O_GB + hf * 512, 512)
                st, sv = wpb[hf]
                for m in range(4):
                    mm = hf * 4 + m
                    pg = nps()
                    mm_fm(pg, 128, stg, svg, m * 128, NN)
                    r = rl[mm % 2]
                    S.op("act", lambda e, pg=pg, r=r: e.activation(r[:, 0:NN], pg[:, 0:NN], AF.Sigmoid), [pg], [r])
                    pb = nps()
                    mm_fm(pb, 128, st, sv, m * 128, NN, rhs_t=boT)
                    r2 = rt[mm % 2]
                    S.op("dve", lambda e, pb=pb, r=r, r2=r2: e.tensor_tensor(r2[:, 0:NN], pb[:, 0:NN], r[:, 0:NN], ALU.mult),
                         [pb, r], [r2])
                    if MODE == "pool_only":
                        S.op("dve", lambda e, mm=mm: e.tensor_copy(mrg[:, mm, 0:NN], mgA[:, mm, 0:NN]), [mgA], [mrg])
                    elif MODE == "gla_only":
                        S.op("dve", lambda e, r2=r2, mm=mm: e.tensor_copy(mrg[:, mm, 0:NN], r2[:, 0:NN]), [r2], [mrg])
                    else:
                        S.op("dve", lambda e, r2=r2, mm=mm: e.tensor_tensor(mrg[:, mm, 0:NN], r2[:, 0:NN], mgA[:, mm, 0:NN], ALU.add),
                             [r2, mgA], [mrg])
            for hf in range(2):
                st, sv = slab("w_out", w_out, hf * 512, 512)
                for g in range(NGR):
                    p = nps()
                    mm_tm(p, G, g * 128, st, sv, 0, 512, lhs_t=mrg)
                    r2 = rt[(hf * NGR + g) % 2]
                    S.op("dve", lambda e, p=p, r2=r2, hf=hf: e.tensor_tensor(r2[0:G, :], p[0:G, :], G1[0:G, hf * 512:(hf + 1) * 512],
                                                                             ALU.mult), [p, G1], [r2])
                    if "nomix" not in MODE:
                        S.op("dve", lambda e, r2=r2, g=g, hf=hf: e.tensor_tensor(xt[0:G, g, hf * 512:(hf + 1) * 512],
                                                                                 xt[0:G, g, hf * 512:(hf + 1) * 512], r2[0:G, :], ALU.add),
                             [xt, r2], [xt])
            fence()

            pre = make_pre(*next_tile) if next_tile is not None else None
            if pre:
                pre["load"]()
            for g in range(NGR):
                norm_to_hT((xt, xt[0:G, g, :]), G, g * 128, 2, segs)
            if DEBUG and is_p and t == 0:
                dump(S, "h2T", hT, hT[:, :, 0:128], 128, 1024, inner=128)
            for sl in range(8):
                st, sv = slab("w_ff1", w_ff1, sl * 512, 512)
                for m in range(4):
                    p = nps()
                    mm_fm(p, 128, st, sv, m * 128, NN)
                    r = rl[m % 2]
                    S.op("act", lambda e, p=p, r=r: e.activation(r[:, 0:NN], p[:, 0:NN], AF.Relu), [p], [r])
                    S.op("dve", lambda e, r=r, sl=sl, m=m: e.tensor_tensor(ffT[:, sl * 4 + m, 0:NN], r[:, 0:NN], r[:, 0:NN], ALU.mult),
                         [r], [(ffT, sl * 4 + m)])
            if DEBUG and is_p and t == 0:
                dump(S, "ffT", ffT, ffT[:, 0:8, 0:128], 128, 1024, inner=128)
            if pre:
                pre["stats"]()
                pre["a2a"](0)
            pstep = [0]

            def pre_step():
                if not pre or pstep[0] >= pre["ng"]:
                    return
                g_ = pstep[0]
                pstep[0] += 1
                pre["a2b"](g_)
                if g_ + 1 < pre["ng"]:
                    pre["a2a"](g_ + 1)
            for hf in range(2):
                acc = [nps() for _ in range(NGR)]
                for kp in range(4):
                    st, sv = slab("w_ff2", w_ff2, hf * 512, 512, 8, kp * 1024)
                    for g in range(NGR):
                        mm_tm(acc[g], G, g * 128, st, sv, 0, 512, lhs_t=ffT, K=8, k0=kp * 8, first=(kp == 0), last=(kp == 3))
                    if kp % 2 == 1:
                        pre_step()
                for g in range(NGR):
                    r2 = rt[g % 2]
                    S.op("dve", lambda e, g=g, r2=r2, hf=hf, acc=acc: e.tensor_tensor(r2[0:G, :], acc[g][0:G, :],
                                                                             G2[0:G, hf * 512:(hf + 1) * 512], ALU.mult),
                         [acc[g], G2], [r2])
                    if "noffn" not in MODE:
                        S.op("dve", lambda e, r2=r2, g=g, hf=hf: e.tensor_tensor(xt[0:G, g, hf * 512:(hf + 1) * 512],
                                                                                 xt[0:G, g, hf * 512:(hf + 1) * 512], r2[0:G, :], ALU.add),
                             [xt, r2], [xt])
            for g in range(NGR):
                S.op("act", lambda e, g=g: e.activation(jkm[g % 2][0:G, :], xt[0:G, g, :], AF.Square, accum_out=stat[0:G, 4 + g:5 + g]),
                     [xt], [jkm[g % 2], (stat, g)])
            S.op("act", lambda e: e.activation(stat[0:G, 8:8 + NGR], stat[0:G, 4:4 + NGR], AF.Ln, bias=EPS, scale=1.0 / D), [stat], [stat])
            S.op("act", lambda e: e.activation(stat[0:G, 12:12 + NGR], stat[0:G, 8:8 + NGR], AF.Exp, scale=-0.5), [stat], [stat])
            for g in range(NGR):
                yo_ = (yo, yo2)[g % 2]
                S.op("dve", lambda e, g=g, yo_=yo_: e.scalar_tensor_tensor(yo_[0:G, :], xt[0:G, g, :], stat[0:G, 12 + g:13 + g], FG[0:G, :],
                                                                           ALU.mult, ALU.mult), [xt, stat, FG], [yo_])
                if is_p:
                    S.dma_store("sp", yo_, yp[t * N + g * 128:t * N + (g + 1) * 128, :], yo_[:])
                else:
                    S.dma_store("sp", yo_, ys, yo_[0:64, :])
            if pre:
                while pstep[0] < pre["ng"]:
                    pre_step()
                pre["copy"]()
            fence()

        tiles = [("p", t) for t in range(NT)] + [("s", 0)]
        for ti, (kind, t) in enumerate(tiles):
            if kind == "s":
                build_gates(selt.h[0:5, 128:192], 64)
            layer_tile(kind, t, pre_done=(ti > 0), next_tile=(tiles[ti + 1] if ti + 1 < len(tiles) else None))

        if DEBUG:
            dump(S, "gtm1", gtm1, gtm1[:], 5, D)
            dump(S, "selt", selt, selt[:], 5, 192)
            dump(S, "G1", G1, G1[:], 128, D)
            dump(S, "G2", G2, G2[:], 128, D)
            dump(S, "AB", AB, AB[:].rearrange("p a b c -> p (a b c)"), 128, 160)
            dump(S, "vecs", vecs, vecs[:], 128, 24)
            out_tiles.append(dbg_stage[0])
        out_tiles += [yo, yo2, utm, St]
        S.finish(out_tiles)
        S.emit_all()
        print("ops", S.nops, "waits", S.nwaits, "sems", len(S.sems))
    return nc


def _consts():
    ident = np.eye(128, dtype=np.float32)
    cum = np.triu(np.ones((128, 128), np.float32))
    rem = np.tril(np.ones((128, 128), np.float32), -1)
    cst128 = np.concatenate([ident, cum, rem], axis=1)
    seg = np.arange(64) // 16
    same = (seg[:, None] == seg[None, :]).astype(np.float32)
    cums = np.triu(np.ones((64, 64), np.float32)) * same
    rems = np.tril(np.ones((64, 64), np.float32), -1) * same
    segm = (seg[:, None] == np.arange(4)[None, :]).astype(np.float32)
    cst64 = np.concatenate([cums, rems, segm], axis=1)
    sel = np.zeros((5, 192), np.float32)
    sel[0, 0:128] = 1.0
    for s in range(4):
        sel[1 + s, 128 + 16 * s:128 + 16 * (s + 1)] = 1.0
    return cst128, cst64, sel


_NC_CACHE = {}


def kernel(x_prompt, x_sample, c_prompt, c_sample, state_gla, cache_pool, w_ada, b_ada,
           norm1_g, w_in, w_alpha, b_alpha, w_pool, pool_scale, gla_norm_g, w_pa, w_pb,
           w_out, norm2_g, w_ff1, w_ff2, final_g):
    f = lambda a: np.ascontiguousarray(np.asarray(a, dtype=np.float32))
    x_prompt, x_sample, c_prompt, c_sample = f(x_prompt), f(x_sample), f(c_prompt), f(c_sample)
    state_gla, cache_pool = f(state_gla), f(cache_pool)
    cst128, cst64, sel = _consts()
    shared = dict(
        w_ada=f(w_ada)[0], b_ada=f(b_ada), norm1_g=f(norm1_g), w_in=f(w_in)[0], w_alpha=f(w_alpha)[0],
        b_alpha=f(b_alpha), w_pool=f(w_pool)[0], pool_scale=f(pool_scale), gla_norm_g=f(gla_norm_g),
        w_pa=f(w_pa)[0], w_pb=f(w_pb)[0], w_out=f(w_out)[0], norm2_g=f(norm2_g), w_ff1=f(w_ff1)[0],
        w_ff2=f(w_ff2)[0], final_g=f(final_g).reshape(1, D), cst128=cst128, cst64=cst64, sel=sel)
    in_maps = []
    for c in range(NCORES):
        t0 = c * TPC
        xh = np.zeros((16, D), np.float32)
        if c > 0:
            xh[:] = x_prompt[0, t0 - 16:t0]
        invc = np.zeros((128, 64), np.float32)
        for gi in range(4):
            w = 2 << gi
            pos = t0 + np.arange(16)
            invc[:, gi * 16:(gi + 1) * 16] = (1.0 / np.minimum(pos + 1, w))[None, :]
        flags = np.zeros((128, 16), np.float32)
        flags[:, 0] = 0.0 if c == 0 else 1.0
        xpre = None
        if CARRY == "prefix":
            xpre = np.zeros((7 * TPC, D), np.float32)
            for j in range(7):
                b = c - 7 + j
                if b >= 0:
                    xpre[j * TPC:(j + 1) * TPC] = x_prompt[0, b * TPC:(b + 1) * TPC]
                    flags[:, 1 + j] = 1.0
        else:
            flags[:, 1 + c] = 1.0
        m = dict(shared)
        m.update(
            xp=x_prompt[0, t0:t0 + TPC], xh=xh, xs=x_sample[c * SPC:(c + 1) * SPC].reshape(SPC * LS, D),
            cvec=np.concatenate([c_prompt, c_sample[c * SPC:(c + 1) * SPC]], axis=0),
            s0=state_gla[0, c * SPC:(c + 1) * SPC], cache=cache_pool[0, c * SPC:(c + 1) * SPC].reshape(SPC * 15, 512),
            invcnt=invc, flags=flags)
        if xpre is not None:
            m["xpre"] = xpre
        in_maps.append({k: np.ascontiguousarray(v) for k, v in m.items()})
    if "nc" not in _NC_CACHE:
        _NC_CACHE["nc"] = build()
    nc = _NC_CACHE["nc"]
    res = run_bass_kernel_spmd(nc, in_maps, core_ids=list(range(NCORES)))
    R = res.results
    _NC_CACHE['last'] = R
    y_prompt = np.concatenate([R[c]["yp"] for c in range(NCORES)], axis=0)[None]
    y_sample = np.concatenate([R[c]["ys"].reshape(SPC, LS, D) for c in range(NCORES)], axis=0)
    st_p = R[NCORES - 1]["sp_o"][None, None]
    ch_p = R[NCORES - 1]["cp_o"][None, None]
    st_s = np.concatenate([R[c]["ss_o"] for c in range(NCORES)], axis=0)[None]
    ch_s = np.concatenate([R[c]["cs_o"].reshape(SPC, 15, 512) for c in range(NCORES)], axis=0)[None]
    return (y_prompt.astype(np.float32), y_sample.astype(np.float32), st_p.astype(np.float32),
            ch_p.astype(np.float32), st_s.astype(np.float32), ch_s.astype(np.float32))
```
